# Optimizing a Trainium2 kernel written in Bass

```python
import math
import jax, jax.numpy as jnp
from jax import lax
import numpy as np

D_MODEL = 1024
BATCH = 8
SEQ = 2048
DEPTH = 1

MEM_LEN = 256
CONV_CH = D_MODEL
CONV_K = 31
DIFF_HEADS = D_MODEL // 128
DIFF_HEAD_DIM = 64
DIFF_V_DIM = 2 * DIFF_HEAD_DIM
DIFF_WIDTH = DIFF_HEADS * DIFF_V_DIM
X_HEADS = 4
X_HEAD_DIM = D_MODEL // X_HEADS
X_WIDTH = X_HEADS * X_HEAD_DIM
N_BRANCH = 3
Q_BLOCK = 128

IN_SIZES = (
    2 * CONV_CH,
    CONV_CH,
    DIFF_WIDTH,
    DIFF_WIDTH,
    DIFF_WIDTH,
    DIFF_WIDTH,
    X_WIDTH,
    X_WIDTH,
    N_BRANCH * D_MODEL,
)
IN_COLS = sum(IN_SIZES)
IN_SPLITS = tuple(int(c) for c in np.cumsum(IN_SIZES)[:-1])

kernel_name = "hybrid_conformer_diffattn_memxattn_gated"


def rms_norm(x, g, eps=1e-6):
    xf = x.astype(jnp.float32)
    y = xf * lax.rsqrt(jnp.mean(xf * xf, axis=-1, keepdims=True) + eps)
    return (y * g.astype(jnp.float32)).astype(x.dtype)


def layer_norm(x, g, b, eps=1e-5):
    xf = x.astype(jnp.float32)
    mu = jnp.mean(xf, axis=-1, keepdims=True)
    xc = xf - mu
    y = xc * lax.rsqrt(jnp.mean(xc * xc, axis=-1, keepdims=True) + eps)
    return (y * g.astype(jnp.float32) + b.astype(jnp.float32)).astype(x.dtype)


def alibi_slopes(n_heads):
    return 2.0 ** (-8.0 * jnp.arange(1, n_heads + 1, dtype=jnp.float32) / n_heads)


def lambda_init_for(layer_idx):
    return 0.8 - 0.6 * math.exp(-0.3 * layer_idx)


def conformer_conv_branch(a_glu, gate, dw, dw_b, ln_g, ln_b, w_proj):
    a, b = jnp.split(a_glu, 2, axis=-1)
    u = a * jax.nn.sigmoid(b)
    u = lax.conv_general_dilated(
        u, dw[:, None, :], window_strides=(1,), padding=[(CONV_K - 1, 0)],
        dimension_numbers=("NWC", "WIO", "NWC"), feature_group_count=CONV_CH) + dw_b
    u = layer_norm(u, ln_g, ln_b)
    u = jax.nn.silu(u) * jax.nn.silu(gate)
    return u @ w_proj


def differential_attention_branch(q, k, v, gate, qn_g, kn_g, lq1, lk1, lq2, lk2,
                                  subln_g, w_proj, lambda_init):
    B, S = q.shape[0], q.shape[1]
    H, d = DIFF_HEADS, DIFF_HEAD_DIM
    nb = S // Q_BLOCK
    q = rms_norm(q.reshape(B, S, H, 2, d), qn_g)
    k = rms_norm(k.reshape(B, S, H, 2, d), kn_g)
    kt = k.transpose(0, 2, 3, 1, 4)
    vt = v.reshape(B, S, H, DIFF_V_DIM).transpose(0, 2, 1, 3)
    qb = q.reshape(B, nb, Q_BLOCK, H, 2, d).transpose(1, 0, 3, 4, 2, 5)
    lam = (jnp.exp(jnp.sum(lq1.astype(jnp.float32) * lk1.astype(jnp.float32)))
           - jnp.exp(jnp.sum(lq2.astype(jnp.float32) * lk2.astype(jnp.float32)))
           + lambda_init)
    slopes = alibi_slopes(H)
    scale = DIFF_HEAD_DIM ** -0.5
    kpos = jnp.arange(S)

    def one_block(args):
        qblk, t0 = args
        tpos = t0 + jnp.arange(Q_BLOCK)
        dist = (tpos[:, None] - kpos[None, :]).astype(jnp.float32)
        bias = -slopes[:, None, None] * dist
        s = jnp.einsum("bhmqd,bhmkd->bhmqk", qblk, kt).astype(jnp.float32) * scale
        s = s + bias[None, :, None]
        s = jnp.where(dist >= 0, s, -jnp.inf)
        p = jax.nn.softmax(s, axis=-1)
        a = (p[:, :, 0] - lam * p[:, :, 1]).astype(vt.dtype)
        return jnp.einsum("bhqk,bhkd->bhqd", a, vt)

    o = lax.map(one_block, (qb, jnp.arange(nb) * Q_BLOCK))
    o = o.transpose(1, 0, 3, 2, 4).reshape(B, S, H, DIFF_V_DIM)
    o = rms_norm(o, subln_g) * (1.0 - lambda_init)
    o = o.reshape(B, S, DIFF_WIDTH) * jax.nn.silu(gate)
    return o @ w_proj


def memory_cross_attention_branch(q, gate, mem_h, w_mem_kv, qn_g, kn_g, w_proj):
    B, S = q.shape[0], q.shape[1]
    M = mem_h.shape[1]
    k, v = jnp.split(mem_h @ w_mem_kv, 2, axis=-1)
    q = rms_norm(q.reshape(B, S, X_HEADS, X_HEAD_DIM), qn_g)
    k = rms_norm(k.reshape(B, M, X_HEADS, X_HEAD_DIM), kn_g)
    v = v.reshape(B, M, X_HEADS, X_HEAD_DIM)
    s = jnp.einsum("bshd,bmhd->bhsm", q, k).astype(jnp.float32) * (X_HEAD_DIM ** -0.5)
    p = jax.nn.softmax(s, axis=-1).astype(v.dtype)
    o = jnp.einsum("bhsm,bmhd->bshd", p, v).reshape(B, S, X_WIDTH) * jax.nn.silu(gate)
    return o @ w_proj


def setup_inputs(seed: int = 0) -> dict:
    key = jax.random.key(seed)
    ks = jax.random.split(key, 24)
    f32 = jnp.float32
    L, D = DEPTH, D_MODEL

    def nrm(k, shape, scale):
        return jax.random.normal(k, shape, f32) * scale

    def gain(k, shape):
        return 1.0 + 0.05 * jax.random.normal(k, shape, f32)

    return {
        "x": jax.random.normal(ks[0], (BATCH, SEQ, D), f32),
        "mem": jax.random.normal(ks[1], (BATCH, MEM_LEN, D), f32),
        "norm_g": gain(ks[2], (L, D)),
        "mem_norm_g": gain(ks[3], (L, D)),
        "w_in": nrm(ks[4], (L, D, IN_COLS), D ** -0.5),
        "conv_dw": nrm(ks[5], (L, CONV_K, CONV_CH), CONV_K ** -0.5),
        "conv_dw_b": nrm(ks[6], (L, CONV_CH), 0.02),
        "conv_ln_g": gain(ks[7], (L, CONV_CH)),
        "conv_ln_b": nrm(ks[8], (L, CONV_CH), 0.02),
        "w_conv_proj": nrm(ks[9], (L, CONV_CH, D), CONV_CH ** -0.5),
        "diff_qn_g": gain(ks[10], (L, DIFF_HEAD_DIM)),
        "diff_kn_g": gain(ks[11], (L, DIFF_HEAD_DIM)),
        "lambda_q1": nrm(ks[12], (L, DIFF_HEAD_DIM), 0.1),
        "lambda_k1": nrm(ks[13], (L, DIFF_HEAD_DIM), 0.1),
        "lambda_q2": nrm(ks[14], (L, DIFF_HEAD_DIM), 0.1),
        "lambda_k2": nrm(ks[15], (L, DIFF_HEAD_DIM), 0.1),
        "diff_subln_g": gain(ks[16], (L, DIFF_V_DIM)),
        "w_diff_proj": nrm(ks[17], (L, DIFF_WIDTH, D), DIFF_WIDTH ** -0.5),
        "w_mem_kv": nrm(ks[18], (L, D, 2 * X_WIDTH), D ** -0.5),
        "x_qn_g": gain(ks[19], (L, X_HEAD_DIM)),
        "x_kn_g": gain(ks[20], (L, X_HEAD_DIM)),
        "w_x_proj": nrm(ks[21], (L, X_WIDTH, D), X_WIDTH ** -0.5),
        "w_out": nrm(ks[22], (L, D, D), D ** -0.5),
    }


def reference(x, mem, norm_g, mem_norm_g, w_in, conv_dw, conv_dw_b, conv_ln_g, conv_ln_b,
              w_conv_proj, diff_qn_g, diff_kn_g, lambda_q1, lambda_k1, lambda_q2, lambda_k2,
              diff_subln_g, w_diff_proj, w_mem_kv, x_qn_g, x_kn_g, w_x_proj, w_out):
    B, S, D = x.shape
    for l in range(DEPTH):
        h = rms_norm(x, norm_g[l])
        mem_h = rms_norm(mem, mem_norm_g[l])
        (c_glu, c_gate, d_q, d_k, d_v, d_gate, x_q, x_gate, merge) = jnp.split(
            h @ w_in[l], IN_SPLITS, axis=-1)
        y_conv = conformer_conv_branch(c_glu, c_gate, conv_dw[l], conv_dw_b[l],
                                       conv_ln_g[l], conv_ln_b[l], w_conv_proj[l])
        y_diff = differential_attention_branch(
            d_q, d_k, d_v, d_gate, diff_qn_g[l], diff_kn_g[l], lambda_q1[l], lambda_k1[l],
            lambda_q2[l], lambda_k2[l], diff_subln_g[l], w_diff_proj[l], lambda_init_for(l))
        y_mem = memory_cross_attention_branch(x_q, x_gate, mem_h, w_mem_kv[l],
                                              x_qn_g[l], x_kn_g[l], w_x_proj[l])
        g = jax.nn.sigmoid(merge.reshape(B, S, N_BRANCH, D))
        y = g[:, :, 0] * y_conv + g[:, :, 1] * y_diff + g[:, :, 2] * y_mem
        x = x + y @ w_out[l]
    return x
```

```python
import contextlib
import numpy as np
import concourse.bass as bass
import concourse.mybir as mybir
from concourse.bass_utils import run_bass_kernel_spmd

F32 = mybir.dt.float32
BF16 = mybir.dt.bfloat16
AF = mybir.ActivationFunctionType
ALU = mybir.AluOpType
AX = mybir.AxisListType

D = 1024
S = 2048
MEM = 256
NCH = 8
NTB = 4
NT = 16
IN_COLS = 12288
C_GLU, C_GATE, D_Q, D_K, D_V, D_GATE, X_Q, X_GATE, MERGE = 0, 2048, 3072, 4096, 5120, 6144, 7168, 8192, 9216
NSLOT = 3
ENGINES = ("pe", "act", "dve", "pool", "sp")

R_DWB, R_LNG, R_LNB, R_DW, R_QN, R_KN, R_SUB, R_XQ, R_XK, NROWS = 0, 8, 16, 24, 272, 273, 274, 275, 277, 279

DEBUG = None


class _Stop(Exception):
    pass


class Prog:
    def __init__(self, nc):
        self.nc = nc
        self.ins = []
        self.last_w = {}
        self.readers = {}
        self.chan_count = {}
        self.last_on = {}

    def _add(self, eng, fn, reads, writes, dma_chan=None, extra_deps=()):
        idx = len(self.ins)
        deps = set(extra_deps)
        for r in reads:
            w = self.last_w.get(r)
            if w is not None:
                deps.add(w)
            if fn is not None:
                self.readers.setdefault(r, []).append(idx)
        for w_ in writes:
            w = self.last_w.get(w_)
            if w is not None:
                deps.add(w)
            for rd in self.readers.get(w_, ()):
                if rd != idx:
                    deps.add(rd)
            self.last_w[w_] = idx
            self.readers[w_] = []
        rec = dict(eng=eng, fn=fn, dma_chan=dma_chan, mark=False)
        waits = []
        for d in deps:
            dr = self.ins[d]
            if dr["dma_chan"] is not None:
                waits.append(("dma", dr["dma_chan"], self.chan_count[dr["dma_chan"]]))
            else:
                if dr["eng"] == "pe" and eng == "pe":
                    continue
                dr["mark"] = True
                waits.append(("eng", dr["eng"], d))
        rec["waits"] = waits
        if dma_chan is not None:
            self.chan_count[dma_chan] = self.chan_count.get(dma_chan, 0) + 16
        elif fn is not None:
            self.last_on[eng] = idx
        self.ins.append(rec)
        return idx

    def op(self, eng, fn, reads=(), writes=()):
        return self._add(eng, fn, tuple(reads), tuple(writes))

    def dma(self, eng, out, in_, reads=(), writes=(), chan=None):
        return self._add(eng, lambda e: e.dma_start(out=out, in_=in_), tuple(reads), tuple(writes),
                         dma_chan=chan)

    def barrier(self):
        lasts = [v for v in self.last_on.values()]
        for e in ENGINES:
            idx = self._add(e, None, (), (), extra_deps=lasts)
            rec = self.ins[idx]
            for c, v in self.chan_count.items():
                rec["waits"].append(("dma", c, v))

    def emit(self, stack):
        nc = self.nc
        sems = {e: stack.enter_context(nc.semaphore("s_" + e)) for e in ENGINES}
        csems = {c: stack.enter_context(nc.semaphore("c_" + str(c))) for c in self.chan_count}
        cnt = {e: 0 for e in ENGINES}
        for r in self.ins:
            if r["dma_chan"] is None and r["mark"]:
                cnt[r["eng"]] += 1
                r["ord"] = cnt[r["eng"]]
        per = {e: [] for e in ENGINES}
        for r in self.ins:
            per[r["eng"]].append(r)
        block = stack.enter_context(nc.Block())
        ins = self.ins

        def run(engname, eng):
            waited = {}
            for r in per[engname]:
                need = {}
                for w in r["waits"]:
                    if w[0] == "dma":
                        key = ("c", w[1]); val = w[2]
                    else:
                        key = ("e", w[1]); val = ins[w[2]]["ord"]
                    if val > need.get(key, 0):
                        need[key] = val
                for key, val in need.items():
                    if waited.get(key, 0) >= val:
                        continue
                    waited[key] = val
                    eng.wait_ge(csems[key[1]] if key[0] == "c" else sems[key[1]], val)
                if r["fn"] is None:
                    continue
                bi = r["fn"](eng)
                if r["dma_chan"] is not None:
                    bi.then_inc(csems[r["dma_chan"]], 16)
                elif r["mark"]:
                    bi.then_inc(sems[engname], 1)

        block.tensor(lambda e: run("pe", e))
        block.scalar(lambda e: run("act", e))
        block.vector(lambda e: run("dve", e))
        block.gpsimd(lambda e: run("pool", e))
        block.sync(lambda e: run("sp", e))


class Banks:
    def __init__(self, ids):
        self.ids = list(ids)
        self.busy = set()
        self.ptr = 0

    def alloc(self):
        n = len(self.ids)
        for k in range(n):
            b = self.ids[(self.ptr + k) % n]
            if b not in self.busy:
                self.busy.add(b)
                self.ptr = (self.ptr + k + 1) % n
                return b
        raise RuntimeError("out of PSUM banks")

    def free(self, b):
        self.busy.discard(b)


class Ring:
    def __init__(self, items):
        self.items = items
        self.i = 0

    def next(self):
        r = self.items[self.i % len(self.items)]
        k = self.i % len(self.items)
        self.i += 1
        return k, r


def build_program():
    nc = bass.Bass("TRN2", target_bir_lowering=False)
    dt_in = lambda name, shape: nc.dram_tensor(name, list(shape), F32, kind="ExternalInput").ap()
    x_d = dt_in("x", [S, D])
    mem_d = dt_in("mem", [MEM, D])
    w_in_d = dt_in("w_in", [D, IN_COLS])
    wcp_d = dt_in("w_conv_proj", [D, D])
    wdp_d = dt_in("w_diff_proj", [D, D])
    wxp_d = dt_in("w_x_proj", [D, D])
    wout_d = dt_in("w_out", [D, D])
    wkv_d = dt_in("w_mem_kv", [D, 2 * D])
    ng_d = dt_in("norm_g", [1, D])
    mg_d = dt_in("mem_norm_g", [1, D])
    vecs_d = dt_in("vecs", [NROWS, 128])
    lam_d = dt_in("lamv", [1, 256])
    cb_d = dt_in("cbits", [128, 4 * 128])
    idf_d = dt_in("identf", [128, 128])
    qaug_d = dt_in("qaug", [8, 4, S])
    kaug_d = dt_in("kaug", [8, 4, S])
    out_d = nc.dram_tensor("out", [S, D], F32, kind="ExternalOutput").ap()
    dbg_d = None
    if DEBUG is not None:
        dbg_d = nc.dram_tensor("dbg", [128, 8 * S], BF16, kind="ExternalOutput").ap()

    wviews = {
        "in": w_in_d.rearrange("(kc p) c -> p kc c", p=128),
        "cp": wcp_d.rearrange("(kc p) c -> p kc c", p=128),
        "dp": wdp_d.rearrange("(kc p) c -> p kc c", p=128),
        "xp": wxp_d.rearrange("(kc p) c -> p kc c", p=128),
        "out": wout_d.rearrange("(kc p) c -> p kc c", p=128),
        "kv": wkv_d.rearrange("(kc p) c -> p kc c", p=128),
    }

    with contextlib.ExitStack() as st:
        P = Prog(nc)

        tcount = [0]

        def T(stack, name, shape, dt):
            tcount[0] += 1
            return stack.enter_context(nc.sbuf_tensor(f"sb{tcount[0]}_" + name, list(shape), dt))

        hT = T(st, "hT", [128, NCH, S], BF16)
        yT = T(st, "yT", [128, NCH, S], BF16)
        actT = T(st, "actT", [128, NCH, S], BF16)
        wsl = [T(st, f"wsl{i}", [128, NCH, 512], BF16) for i in range(NSLOT)]
        cb = T(st, "cb", [128, 4, 128], BF16)
        identf = T(st, "identf", [128, 128], F32)
        vecsT = T(st, "vecsT", [128, NROWS], F32)
        sc = T(st, "sc", [128, 64], F32)
        ps = [st.enter_context(nc.psum_tensor(f"ps{i}", [128, 512], F32)) for i in range(8)]
        ident = cb[:, 0, :]
        ones = cb[:, 1, :]
        bd64 = cb[:, 2, :]
        negmask = cb[:, 3, :]
        SC_QN, SC_KN, SC_SUB, SC_XQ, SC_XK, SC_NLAM, SC_LNGH, SC_LNBH, SC_EPS6, SC_EPS5 = 0, 1, 2, 3, 5, 7, 8, 16, 30, 31
        banks = Banks(range(8))

        def PSK(b):
            return ("ps", b)

        groups = []

        def G(*parts):
            groups.append(list(parts))
            return len(groups) - 1

        g_kv = [G((0, "kv", i * 512, 512)) for i in range(4)]
        g_xh = [G((0, "in", X_Q + h * 256, 256), (256, "in", X_GATE + h * 256, 256)) for h in range(4)]
        g_xp = [G((0, "xp", i * 512, 512)) for i in range(2)]
        g_m2 = [G((0, "in", MERGE + 2 * D + i * 512, 512)) for i in range(2)]
        g_cab = [G((0, "in", C_GLU + jj * 256, 256), (256, "in", C_GLU + D + jj * 256, 256)) for jj in range(4)]
        g_cg = [G((0, "in", C_GATE + i * 512, 512)) for i in range(2)]
        g_cp = [G((0, "cp", i * 512, 512)) for i in range(2)]
        g_m0 = [G((0, "in", MERGE + i * 512, 512)) for i in range(2)]
        g_dh = [G((0, "in", D_Q + h * 128, 128), (128, "in", D_K + h * 128, 128),
                  (256, "in", D_V + h * 128, 128), (384, "in", D_GATE + h * 128, 128)) for h in range(8)]
        g_dp = [G((0, "dp", i * 512, 512)) for i in range(2)]
        g_m1 = [G((0, "in", MERGE + D + i * 512, 512)) for i in range(2)]
        g_out = [G((0, "out", i * 512, 512)) for i in range(2)]
        order = (g_kv + g_xh + [g_xp[0], g_m2[0], g_xp[1], g_m2[1]] + g_cab + g_cg
                 + [g_cp[0], g_m0[0], g_cp[1], g_m0[1]] + g_dh + [g_dp[0], g_m1[0], g_dp[1], g_m1[1]] + g_out)
        assert sorted(order) == list(range(len(groups)))
        pos_of = {g: i for i, g in enumerate(order)}
        wstate = {"issued": 0}

        def wkeys(slot, c0, n):
            return [("w", slot, q) for q in range(c0 // 128, (c0 + n) // 128)]

        def w_issue_next():
            i = wstate["issued"]
            if i >= len(order):
                return
            g = order[i]
            slot = i % NSLOT
            for (dc, wn, sc0, n) in groups[g]:
                P.dma("pool", wsl[slot][:, :, dc:dc + n], wviews[wn][:, :, sc0:sc0 + n],
                      writes=wkeys(slot, dc, n), chan=f"w{slot}")
            wstate["issued"] = i + 1

        def w_slot(g):
            i = pos_of[g]
            assert i < wstate["issued"], "weight group not issued yet"
            assert i >= wstate["issued"] - NSLOT
            return i % NSLOT

        def w_done(g):
            w_issue_next()

        def mm(out, lhsT, rhs, start, stop, reads, writes, skip=False):
            if skip:
                P.op("pe", lambda e: e.matmul(out, lhsT=lhsT, rhs=rhs, start=start, stop=stop,
                                              skip_group_check=True), reads, writes)
            else:
                P.op("pe", lambda e: e.matmul(out, lhsT=lhsT, rhs=rhs, start=start, stop=stop), reads, writes)

        def act(out, in_, func, reads, writes, scale=1.0, bias=0.0, accum=None):
            if accum is not None:
                P.op("act", lambda e: e.activation(out=out, in_=in_, func=func, scale=scale, bias=bias,
                                                   accum_out=accum), reads, writes)
            else:
                P.op("act", lambda e: e.activation(out=out, in_=in_, func=func, scale=scale, bias=bias),
                     reads, writes)

        def ts(eng, out, in0, s1, s2, op0, op1, reads, writes):
            if s2 is None:
                P.op(eng, lambda e: e.tensor_scalar(out=out, in0=in0, scalar1=s1, scalar2=None, op0=op0),
                     reads, writes)
            else:
                P.op(eng, lambda e: e.tensor_scalar(out=out, in0=in0, scalar1=s1, scalar2=s2, op0=op0, op1=op1),
                     reads, writes)

        def stt(out, in0, scalar, in1, op0, op1, reads, writes):
            P.op("dve", lambda e: e.scalar_tensor_tensor(out=out, in0=in0, scalar=scalar, in1=in1, op0=op0,
                                                         op1=op1), reads, writes)

        def tt(eng, out, in0, in1, op, reads, writes):
            P.op(eng, lambda e: e.tensor_tensor(out=out, in0=in0, in1=in1, op=op), reads, writes)

        def cp(eng, out, in_, reads, writes):
            if eng == "act":
                P.op("act", lambda e: e.copy(out=out, in_=in_), reads, writes)
            else:
                P.op(eng, lambda e: e.tensor_copy(out=out, in_=in_), reads, writes)

        def proj(bank, slot, col0, rhs_fn, rkeys, ncols=128, n=512, accum_extra=None):
            for kc in range(NCH):
                mm(ps[bank][:, 0:n], wsl[slot][:, kc, col0:col0 + 128], rhs_fn(kc), kc == 0, kc == NCH - 1,
                   reads=wkeys(slot, col0, 128) + rkeys, writes=[PSK(bank)])

        def hT_rhs(tb):
            return (lambda kc: hT[:, kc, tb * 512:(tb + 1) * 512]), [("hT", tb)]

        def rsqrt_from_psum(bank, n, scale, eps, vbuf, vkey):
            ecol = sc[:, SC_EPS6:SC_EPS6 + 1] if eps == 1e-6 else sc[:, SC_EPS5:SC_EPS5 + 1]
            act(vbuf[:, 0:n], ps[bank][:, 0:n], AF.Ln, ["sc"], [PSK(bank), vkey], scale=scale, bias=ecol)
            act(vbuf[:, 0:n], vbuf[:, 0:n], AF.Exp, [vkey], [vkey], scale=-0.5)

        state = {"done": False, "inph": False}

        def ph_enter():
            state["inph"] = True

        def ph_exit():
            state["inph"] = False
            if state["done"]:
                raise _Stop()

        def rsqrt_dve_evac(bank, n, scale, eps, vbuf, vkey):
            ts("dve", vbuf[:, 0:n], ps[bank][:, 0:n], scale, eps, ALU.mult, ALU.add, reads=[], writes=[PSK(bank), vkey])
            act(vbuf[:, 0:n], vbuf[:, 0:n], AF.Ln, [vkey], [vkey])
            act(vbuf[:, 0:n], vbuf[:, 0:n], AF.Exp, [vkey], [vkey], scale=-0.5)

        def dump(tag, tensor):
            if DEBUG == tag:
                P.barrier()
                P.dma("sp", dbg_d, tensor[:].rearrange("p a b -> p (a b)"), writes=["dbg"], chan="dbg")
                P.op("sp", None, reads=["dbg"])
                state["done"] = True
                if not state["inph"]:
                    raise _Stop()

        for i in range(NSLOT):
            w_issue_next()
        P.dma("pool", cb[:].rearrange("p a b -> p (a b)"), cb_d, writes=["cb"], chan="c0")
        P.dma("sp", identf[:], idf_d, writes=["identf"], chan="c1")
        P.op("pool", lambda e: e.memset(sc[:, SC_EPS6:SC_EPS6 + 1], 1e-6), writes=["sc"])
        P.op("pool", lambda e: e.memset(sc[:, SC_EPS5:SC_EPS5 + 1], 1e-5), writes=["sc"])
        with contextlib.ExitStack() as ph:
            vst = [T(ph, f"vst{i}", [128, 128], F32) for i in range(3)]
            lamb = T(ph, "lamb", [128, 256], F32)
            lamp = T(ph, "lamp", [128, 128], F32)
            lams = T(ph, "lams", [128, 2], F32)
            rows = [(0, 128), (128, 128), (256, NROWS - 256)]
            for i, (r0, n) in enumerate(rows):
                P.dma("sp", vst[i][0:n, :], vecs_d[r0:r0 + n, :], writes=[("vst", i)], chan="c1")
            P.dma("sp", lamb[:], lam_d.partition_broadcast(128), writes=["lamb"], chan="c1")
            b = banks.alloc()
            for i, (r0, n) in enumerate(rows):
                mm(ps[b][:, r0:r0 + n], vst[i][0:n, :], identf[0:n, 0:n], True, True,
                   reads=[("vst", i), "identf"], writes=[PSK(b)], skip=True)
            cp("dve", vecsT[:], ps[b][:, 0:NROWS], reads=[], writes=[PSK(b), "vecsT"])
            banks.free(b)
            ts("dve", sc[:, SC_QN:SC_QN + 1], vecsT[:, R_QN:R_QN + 1], 0.125, None, ALU.mult, None, ["vecsT"], ["sc"])
            cp("dve", sc[:, SC_KN:SC_KN + 1], vecsT[:, R_KN:R_KN + 1], ["vecsT"], ["sc"])
            ts("dve", sc[:, SC_SUB:SC_SUB + 1], vecsT[:, R_SUB:R_SUB + 1], 0.4, None, ALU.mult, None, ["vecsT"], ["sc"])
            cp("dve", sc[:, SC_XQ:SC_XQ + 2], vecsT[:, R_XQ:R_XQ + 2], ["vecsT"], ["sc"])
            ts("dve", sc[:, SC_XK:SC_XK + 2], vecsT[:, R_XK:R_XK + 2], 1.0 / 16.0, None, ALU.mult, None, ["vecsT"], ["sc"])
            ts("dve", sc[:, SC_LNGH:SC_LNGH + 8], vecsT[:, R_LNG:R_LNG + 8], 0.5, None, ALU.mult, None, ["vecsT"], ["sc"])
            ts("dve", sc[:, SC_LNBH:SC_LNBH + 8], vecsT[:, R_LNB:R_LNB + 8], 0.5, None, ALU.mult, None, ["vecsT"], ["sc"])
            tt("dve", lamp[:], lamb[:, 0:128], lamb[:, 128:256], ALU.mult, ["lamb"], ["lamp"])
            P.op("dve", lambda e: e.reduce_sum(out=lams[:], in_=lamp[:].rearrange("p (a b) -> p a b", a=2),
                                               axis=AX.X), ["lamp"], ["lams"])
            act(lams[:], lams[:], AF.Exp, ["lams"], ["lams"])
            tt("dve", lams[:, 0:1], lams[:, 1:2], lams[:, 0:1], ALU.subtract, ["lams"], ["lams"])
            ts("dve", sc[:, SC_NLAM:SC_NLAM + 1], lams[:, 0:1], -0.2, None, ALU.add, None, ["lams"], ["sc"])
            P.barrier()

        try:

            def branch_out(gp, gm, first, ph):
                thm = [T(ph, f"thm{i}", [128, 512], F32) for i in range(2)]
                tb_ = [T(ph, f"tbo{i}", [128, 512], F32) for i in range(2)]
                r_th = Ring(thm)
                r_t = Ring(tb_)
                for half in range(2):
                    sp_ = w_slot(gp[half])
                    sm_ = w_slot(gm[half])
                    for jj in range(4):
                        j = half * 4 + jj
                        for tb in range(NTB):
                            bp = banks.alloc()
                            for c in range(NCH):
                                mm(ps[bp][:, :], wsl[sp_][:, c, jj * 128:(jj + 1) * 128], actT[:, c, tb * 512:(tb + 1) * 512],
                                   c == 0, c == NCH - 1, reads=wkeys(sp_, jj * 128, 128) + [("act", c, tb)], writes=[PSK(bp)])
                            bm = banks.alloc()
                            rf, rk = hT_rhs(tb)
                            proj(bm, sm_, jj * 128, rf, rk)
                            k1, th = r_th.next()
                            act(th[:], ps[bm][:, :], AF.Tanh, [], [PSK(bm), ("thm", k1)], scale=0.5)
                            banks.free(bm)
                            k2, tbuf = r_t.next()
                            stt(tbuf[:], th[:], 1.0, ps[bp][:, :], ALU.add, ALU.mult, [("thm", k1)], [PSK(bp), ("tbo", k2)])
                            banks.free(bp)
                            ysl = yT[:, j, tb * 512:(tb + 1) * 512]
                            if first:
                                ts("dve", ysl, tbuf[:], 0.5, None, ALU.mult, None, [("tbo", k2)], [("y", j, tb)])
                            else:
                                stt(ysl, tbuf[:], 0.5, ysl, ALU.mult, ALU.add, [("tbo", k2)], [("y", j, tb)])
                    w_done(gp[half])
                    w_done(gm[half])

            with contextlib.ExitStack() as ph:
                ph_enter()
                xt = [T(ph, f"xt{i}", [128, D], F32) for i in range(3)]
                xn = [T(ph, f"xn{i}", [128, D], BF16) for i in range(3)]
                junk = [T(ph, f"junk{i}", [128, D], BF16) for i in range(1)]
                gbc = T(ph, "gbc", [128, D], F32)
                gmbc = T(ph, "gmbc", [128, D], F32)
                ssq = T(ph, "ssq", [128, 32], F32)
                memT = T(ph, "memT", [128, NCH, MEM], BF16)
                P.dma("sp", gbc[:], ng_d.partition_broadcast(128), writes=["gbc"], chan="c1")
                P.dma("sp", gmbc[:], mg_d.partition_broadcast(128), writes=["gmbc"], chan="c1")
                tiles = [("m", i) for i in range(2)] + [("x", i) for i in range(NT)]
                for n_, (kind, t) in enumerate(tiles):
                    i = n_ % 3
                    jk = 0
                    src = (mem_d if kind == "m" else x_d)[t * 128:(t + 1) * 128, :]
                    P.dma("sp", xt[i][:], src, writes=[("xt", i)], chan=f"x{i}")
                    col = ssq[:, n_:n_ + 1]
                    act(junk[jk][:], xt[i][:], AF.Square, [("xt", i)], [("junk", jk)])
                    P.op("dve", (lambda e, col=col, jk=jk: e.reduce_sum(out=col, in_=junk[jk][:], axis=AX.X)), [("junk", jk)], [("ssq", n_)])
                    act(col, col, AF.Ln, [("ssq", n_), "sc"], [("ssq", n_)], scale=1.0 / D, bias=sc[:, SC_EPS6:SC_EPS6 + 1])
                    act(col, col, AF.Exp, [("ssq", n_)], [("ssq", n_)], scale=-0.5)
                    gsrc = gmbc if kind == "m" else gbc
                    stt(xn[i][:], xt[i][:], col, gsrc[:], ALU.mult, ALU.mult,
                        [("xt", i), ("ssq", n_), "gbc", "gmbc"], [("xn", i)])
                    b = banks.alloc()
                    pbf = ps[b][:].bitcast(BF16)
                    for kc in range(NCH):
                        P.op("pe", (lambda e, kc=kc, i=i, pbf=pbf: e.transpose(out=pbf[:, kc * 128:(kc + 1) * 128],
                                                                              in_=xn[i][:, kc * 128:(kc + 1) * 128],
                                                                              identity=ident)),
                             reads=[("xn", i), "cb"], writes=[PSK(b)])
                    src3 = pbf.rearrange("p (a b) -> p a b", a=NCH)
                    if kind == "m":
                        cp("dve", memT[:, :, t * 128:(t + 1) * 128], src3, [], [PSK(b), "memT"])
                    else:
                        cp("dve", hT[:, :, t * 128:(t + 1) * 128], src3, [], [PSK(b), ("hT", t // 4)])
                    banks.free(b)
                with contextlib.ExitStack() as ph:
                    ph_enter()
                    kT = T(ph, "kT", [128, NCH, MEM], BF16)
                    Vx = T(ph, "Vx", [128, 2, D], BF16)
                    qT = T(ph, "qT", [128, 2, S], BF16)
                    sqb = [T(ph, f"xsq{i}", [128, 512], BF16) for i in range(4)]
                    vb = [T(ph, f"xvb{i}", [128, 512], F32) for i in range(2)]
                    PT = [T(ph, f"xPT{i}", [128, 512], BF16) for i in range(4)]
                    thg = [T(ph, f"xthg{i}", [128, 512], F32) for i in range(2)]
                    gsb = [T(ph, f"xgs{i}", [128, 512], F32) for i in range(2)]
                    rlb = [T(ph, f"xrl{i}", [128, 512], F32) for i in range(2)]
                    tob = [T(ph, f"xto{i}", [128, 512], F32) for i in range(2)]
                    r_sq, r_vb, r_PT, r_thg, r_gs, r_rl, r_to = (Ring(sqb), Ring(vb), Ring(PT), Ring(thg), Ring(gsb),
                                                                 Ring(rlb), Ring(tob))
                    for hx in range(4):
                        bks = []
                        sqs = []
                        for dc in range(2):
                            c = hx * 2 + dc
                            g = g_kv[c // 4]
                            sl = w_slot(g)
                            b = banks.alloc()
                            proj(b, sl, (c % 4) * 128, lambda kc: memT[:, kc, :], ["memT"], n=MEM)
                            k_, sq = r_sq.next()
                            act(sq[:, 0:MEM], ps[b][:, 0:MEM], AF.Square, [], [PSK(b), ("xsq", k_)])
                            bks.append(b)
                            sqs.append((k_, sq))
                            if c == 3:
                                w_done(g_kv[0])
                            if c == 7:
                                w_done(g_kv[1])
                        bs = banks.alloc()
                        for dc in range(2):
                            mm(ps[bs][:, 0:MEM], ones, sqs[dc][1][:, 0:MEM], dc == 0, dc == 1, ["cb", ("xsq", sqs[dc][0])], [PSK(bs)])
                        kv_, v = r_vb.next()
                        rsqrt_from_psum(bs, MEM, 1.0 / 256.0, 1e-6, v, ("xvb", kv_))
                        banks.free(bs)
                        for dc in range(2):
                            c = hx * 2 + dc
                            stt(kT[:, c, :], ps[bks[dc]][:, 0:MEM], sc[:, SC_XK + dc:SC_XK + dc + 1], v[:, 0:MEM], ALU.mult, ALU.mult,
                                ["sc", ("xvb", kv_)], [PSK(bks[dc]), ("kT", c)])
                            banks.free(bks[dc])
                    for vg in range(2):
                        sl = w_slot(g_kv[2 + vg])
                        for mt in range(2):
                            b = banks.alloc()
                            for kc in range(NCH):
                                mm(ps[b][:, :], memT[:, kc, mt * 128:(mt + 1) * 128], wsl[sl][:, kc, :], kc == 0, kc == NCH - 1,
                                   ["memT"] + wkeys(sl, 0, 512), [PSK(b)])
                            cp("dve", Vx[:, mt, vg * 512:(vg + 1) * 512], ps[b][:, :], [], [PSK(b), ("Vx", mt, vg)])
                            banks.free(b)
                        w_done(g_kv[2 + vg])
                    for hx in range(4):
                        sl = w_slot(g_xh[hx])
                        pend = None

                        def q_finish(pend):
                            tb, bq, sqs = pend
                            bs = banks.alloc()
                            for dc in range(2):
                                mm(ps[bs][:, :], ones, sqs[dc][1][:], dc == 0, dc == 1, ["cb", ("xsq", sqs[dc][0])], [PSK(bs)])
                            kv_, v = r_vb.next()
                            rsqrt_from_psum(bs, 512, 1.0 / 256.0, 1e-6, v, ("xvb", kv_))
                            banks.free(bs)
                            for dc in range(2):
                                stt(qT[:, dc, tb * 512:(tb + 1) * 512], ps[bq[dc]][:, :], sc[:, SC_XQ + dc:SC_XQ + dc + 1], v[:],
                                    ALU.mult, ALU.mult, ["sc", ("xvb", kv_)], [PSK(bq[dc]), ("qT", dc, tb)])
                                banks.free(bq[dc])

                        for tb in range(NTB):
                            bq = []
                            sqs = []
                            rf, rk = hT_rhs(tb)
                            for dc in range(2):
                                b = banks.alloc()
                                proj(b, sl, dc * 128, rf, rk)
                                k_, sq = r_sq.next()
                                act(sq[:], ps[b][:, :], AF.Square, [], [PSK(b), ("xsq", k_)])
                                bq.append(b)
                                sqs.append((k_, sq))
                            if pend is not None:
                                q_finish(pend)
                            pend = (tb, bq, sqs)
                        q_finish(pend)
                        for tb in range(NTB):
                            pts = []
                            for mt in range(2):
                                b = banks.alloc()
                                for dc in range(2):
                                    mm(ps[b][:, :], kT[:, hx * 2 + dc, mt * 128:(mt + 1) * 128], qT[:, dc, tb * 512:(tb + 1) * 512],
                                       dc == 0, dc == 1, [("kT", hx * 2 + dc), ("qT", dc, tb)], [PSK(b)])
                                kp, pt = r_PT.next()
                                act(pt[:], ps[b][:, :], AF.Exp, [], [PSK(b), ("xPT", kp)])
                                banks.free(b)
                                pts.append((kp, pt))
                            bo = []
                            for vc in range(2):
                                b = banks.alloc()
                                for mt in range(2):
                                    c0 = hx * 256 + vc * 128
                                    mm(ps[b][:, :], Vx[:, mt, c0:c0 + 128], pts[mt][1][:], mt == 0, mt == 1,
                                       [("Vx", mt, c0 // 512), ("xPT", pts[mt][0])], [PSK(b)])
                                bo.append(b)
                            bl = banks.alloc()
                            for mt in range(2):
                                mm(ps[bl][:, :], ones, pts[mt][1][:], mt == 0, mt == 1, ["cb", ("xPT", pts[mt][0])], [PSK(bl)])
                            bg = []
                            rf, rk = hT_rhs(tb)
                            for vc in range(2):
                                b = banks.alloc()
                                proj(b, sl, 256 + vc * 128, rf, rk)
                                bg.append(b)
                            kr, rl = r_rl.next()
                            act(rl[:], ps[bl][:, :], AF.Ln, [], [PSK(bl), ("xrl", kr)])
                            banks.free(bl)
                            act(rl[:], rl[:], AF.Exp, [("xrl", kr)], [("xrl", kr)], scale=-1.0)
                            for vc in range(2):
                                kt_, th = r_thg.next()
                                act(th[:], ps[bg[vc]][:, :], AF.Tanh, [], [PSK(bg[vc]), ("xthg", kt_)], scale=0.5)
                                kg, gs = r_gs.next()
                                stt(gs[:], th[:], 1.0, ps[bg[vc]][:, :], ALU.add, ALU.mult, [("xthg", kt_)], [PSK(bg[vc]), ("xgs", kg)])
                                banks.free(bg[vc])
                                ko, to = r_to.next()
                                tt("dve", to[:], ps[bo[vc]][:, :], rl[:], ALU.mult, [("xrl", kr)], [PSK(bo[vc]), ("xto", ko)])
                                banks.free(bo[vc])
                                stt(actT[:, hx * 2 + vc, tb * 512:(tb + 1) * 512], to[:], 0.5, gs[:], ALU.mult, ALU.mult,
                                    [("xto", ko), ("xgs", kg)], [("act", hx * 2 + vc, tb)])
                        w_done(g_xh[hx])
                    dump("actx", actT)
                    if not state["done"]:
                        branch_out(g_xp, g_m2, True, ph)
                    P.barrier()
            ph_exit()
            dump("yx", yT)

            with contextlib.ExitStack() as ph:
                ph_enter()
                PADW = 30
                ub = [T(ph, f"ub{i}", [128, PADW + S], BF16) for i in range(2)]
                dg = [T(ph, f"dg{i}", [128, 31, 128], BF16) for i in range(2)]
                thb = [T(ph, f"cth{i}", [128, 512], F32) for i in range(2)]
                Ms = [T(ph, f"cM{i}", [128, 512], F32) for i in range(NTB)]
                Rs = [T(ph, f"cR{i}", [128, 512], F32) for i in range(NTB)]
                zt = [T(ph, f"cz{i}", [128, 512], F32) for i in range(3)]
                sqc = [T(ph, f"csq{i}", [128, 512], BF16) for i in range(2)]
                gsc = [T(ph, f"cgs{i}", [128, 512], F32) for i in range(2)]
                r_th, r_z, r_sq, r_gs = Ring(thb), Ring(zt), Ring(sqc), Ring(gsc)
                for i in range(2):
                    P.op("pool", (lambda e, i=i: e.memset(ub[i][:, 0:PADW], 0.0)), writes=[("ub", i, -1)])
                for j in range(NCH):
                    g = g_cab[j // 2]
                    sl = w_slot(g)
                    jl = j % 2
                    ui = j % 2
                    for k in range(31):
                        col = vecsT[:, R_DW + k * 8 + j:R_DW + k * 8 + j + 1]
                        P.op("pool", (lambda e, k=k, ui=ui, col=col: e.tensor_scalar(out=dg[ui][:, k, :], in0=ident, scalar1=col,
                                                                                     scalar2=0.5, op0=ALU.mult, op1=ALU.mult)),
                             reads=["cb", "vecsT"], writes=[("dg", ui)])
                    pend = None

                    def conv_block(tb, ui=ui, j=j):
                        b = banks.alloc()
                        rk = [("ub", ui, tb), ("ub", ui, tb - 1), ("dg", ui)]
                        for k in range(31):
                            mm(ps[b][:, :], dg[ui][:, k, :], ub[ui][:, tb * 512 + k:tb * 512 + k + 512], k == 0, k == 30, rk, [PSK(b)])
                        ts("dve", actT[:, j, tb * 512:(tb + 1) * 512], ps[b][:, :], vecsT[:, R_DWB + j:R_DWB + j + 1], None, ALU.add, None,
                           ["vecsT"], [PSK(b), ("act", j, tb)])
                        banks.free(b)

                    for tb in range(NTB):
                        rf, rk = hT_rhs(tb)
                        ba = banks.alloc()
                        proj(ba, sl, jl * 128, rf, rk)
                        bb = banks.alloc()
                        proj(bb, sl, 256 + jl * 128, rf, rk)
                        kt_, th = r_th.next()
                        act(th[:], ps[bb][:, :], AF.Tanh, [], [PSK(bb), ("cth", kt_)], scale=0.5)
                        banks.free(bb)
                        stt(ub[ui][:, PADW + tb * 512:PADW + (tb + 1) * 512], th[:], 1.0, ps[ba][:, :], ALU.add, ALU.mult,
                            [("cth", kt_)], [PSK(ba), ("ub", ui, tb)])
                        banks.free(ba)
                        if pend is not None:
                            conv_block(pend)
                        pend = tb
                    conv_block(pend)
                    if jl == 1:
                        w_done(g)
                dump("conv", actT)
                for tb in range(NTB):
                    bs = banks.alloc()
                    bq = banks.alloc()
                    for j in range(NCH):
                        a = actT[:, j, tb * 512:(tb + 1) * 512]
                        mm(ps[bs][:, :], ones, a, j == 0, j == NCH - 1, ["cb", ("act", j, tb)], [PSK(bs)])
                        ks, sq = r_sq.next()
                        act(sq[:], a, AF.Square, [("act", j, tb)], [("csq", ks)])
                        mm(ps[bq][:, :], ones, sq[:], j == 0, j == NCH - 1, ["cb", ("csq", ks)], [PSK(bq)])
                    M, R = Ms[tb], Rs[tb]
                    ts("dve", M[:], ps[bs][:, :], 1.0 / D, None, ALU.mult, None, [], [PSK(bs), ("cM", tb)])
                    banks.free(bs)
                    tt("dve", R[:], M[:], M[:], ALU.mult, [("cM", tb)], [("cR", tb)])
                    stt(R[:], ps[bq][:, :], 1.0 / D, R[:], ALU.mult, ALU.subtract, [], [PSK(bq), ("cR", tb)])
                    banks.free(bq)
                    act(R[:], R[:], AF.Ln, [("cR", tb), "sc"], [("cR", tb)], scale=1.0, bias=sc[:, SC_EPS5:SC_EPS5 + 1])
                    act(R[:], R[:], AF.Exp, [("cR", tb)], [("cR", tb)], scale=-0.5)
                    stt(M[:], M[:], -1.0, R[:], ALU.mult, ALU.mult, [("cR", tb)], [("cM", tb)])
                for half in range(2):
                    slg = w_slot(g_cg[half])
                    for jj in range(4):
                        j = half * 4 + jj
                        for tb in range(NTB):
                            a = actT[:, j, tb * 512:(tb + 1) * 512]
                            kz, z = r_z.next()
                            tt("pool", z[:], a, Rs[tb][:], ALU.mult, [("act", j, tb), ("cR", tb)], [("cz", kz)])
                            tt("pool", z[:], z[:], Ms[tb][:], ALU.add, [("cM", tb)], [("cz", kz)])
                            kt_, th = r_th.next()
                            act(th[:], z[:], AF.Tanh, [("cz", kz), "sc"], [("cth", kt_)],
                                scale=sc[:, SC_LNGH + j:SC_LNGH + j + 1], bias=sc[:, SC_LNBH + j:SC_LNBH + j + 1])
                            ts("pool", z[:], z[:], vecsT[:, R_LNG + j:R_LNG + j + 1], vecsT[:, R_LNB + j:R_LNB + j + 1], ALU.mult, ALU.add,
                               ["vecsT"], [("cz", kz)])
                            stt(z[:], th[:], 1.0, z[:], ALU.add, ALU.mult, [("cth", kt_)], [("cz", kz)])
                            bg = banks.alloc()
                            rf, rk = hT_rhs(tb)
                            proj(bg, slg, jj * 128, rf, rk)
                            kt2, th2 = r_th.next()
                            act(th2[:], ps[bg][:, :], AF.Tanh, [], [PSK(bg), ("cth", kt2)], scale=0.5)
                            kg, gs = r_gs.next()
                            stt(gs[:], th2[:], 1.0, ps[bg][:, :], ALU.add, ALU.mult, [("cth", kt2)], [PSK(bg), ("cgs", kg)])
                            banks.free(bg)
                            stt(a, z[:], 0.25, gs[:], ALU.mult, ALU.mult, [("cz", kz), ("cgs", kg)], [("act", j, tb)])
                    w_done(g_cg[half])
                dump("actc", actT)
                if not state["done"]:
                    branch_out(g_cp, g_m0, False, ph)
                P.barrier()
            ph_exit()
            dump("yc", yT)

            with contextlib.ExitStack() as ph:
                ph_enter()
                QL = [T(ph, f"QL{i}", [128, S], BF16) for i in range(2)]
                QU = [T(ph, f"QU{i}", [128, S], BF16) for i in range(2)]
                KL = [T(ph, f"KL{i}", [128, S], BF16) for i in range(2)]
                KU = [T(ph, f"KU{i}", [128, S], BF16) for i in range(2)]
                Vd = [T(ph, f"Vd{i}", [128, NT, 128], BF16) for i in range(2)]
                gsd = [T(ph, f"gsd{i}", [128, S], BF16) for i in range(2)]
                PTd = [[T(ph, f"PT{m}_{i}", [128, 512], BF16) for i in range(3)] for m in range(2)]
                sqd = [T(ph, f"dsq{i}", [128, 512], BF16) for i in range(2)]
                thd = [T(ph, f"dth{i}", [128, 512], F32) for i in range(1)]
                vbd = [T(ph, f"dvb{i}", [128, 512], F32) for i in range(2)]
                E1 = [T(ph, f"E1_{i}", [128, 512], F32) for i in range(2)]
                E2 = [T(ph, f"E2_{i}", [128, 512], F32) for i in range(2)]
                sqe = [T(ph, f"sqe{i}", [128, 512], BF16) for i in range(2)]
                RL = [T(ph, f"RL{i}", [128, 512], F32) for i in range(2)]
                r_PT = [Ring(PTd[0]), Ring(PTd[1])]
                rawd = [T(ph, f"draw{i}", [128, 512], F32) for i in range(3)]
                r_raw = Ring(rawd)
                r_sq, r_th, r_vb = Ring(sqd), Ring(thd), Ring(vbd)
                for i in range(2):
                    P.op("pool", (lambda e, i=i: e.memset(QU[i][0:64, :], 0.0)), writes=[("QUa", i)])
                    P.op("pool", (lambda e, i=i: e.memset(KU[i][0:64, :], 0.0)), writes=[("KUa", i)])
                O_B = [4, 6]
                L_B = [5, 7]
                for b in (4, 5, 6, 7):
                    banks.busy.add(b)
                dbanks = Banks([0, 1, 2, 3])

                def qk_unit(h, which, tb):
                    s_ = h % 2
                    st_ = {}
                    c0 = 0 if which == "q" else 128
                    gcol = sc[:, SC_QN:SC_QN + 1] if which == "q" else sc[:, SC_KN:SC_KN + 1]

                    def st0():
                        sl = w_slot(g_dh[h])
                        rf, rk = hT_rhs(tb)
                        b = dbanks.alloc()
                        proj(b, sl, c0, rf, rk)
                        st_["kraw"], st_["raw"] = r_raw.next()
                        cp("dve", st_["raw"][:], ps[b][:, :], [], [PSK(b), ("draw", st_["kraw"])])
                        dbanks.free(b)
                        st_["ks"], sq = r_sq.next()
                        tt("pool", sq[:], st_["raw"][:], st_["raw"][:], ALU.mult, [("draw", st_["kraw"])], [("dsq", st_["ks"])])

                    def st1():
                        bs = dbanks.alloc()
                        mm(ps[bs][:, :], bd64, sqd[st_["ks"]][:], True, True, ["cb", ("dsq", st_["ks"])], [PSK(bs)])
                        st_["kv"], st_["v"] = r_vb.next()
                        ts("dve", st_["v"][:], ps[bs][:, :], 1.0 / 64.0, 1e-6, ALU.mult, ALU.add, [], [PSK(bs), ("dvb", st_["kv"])])
                        dbanks.free(bs)

                    def st2():
                        v, vk = st_["v"], ("dvb", st_["kv"])
                        act(v[:], v[:], AF.Ln, [vk], [vk])
                        act(v[:], v[:], AF.Exp, [vk], [vk], scale=-0.5)

                    def st3():
                        v, vk = st_["v"], ("dvb", st_["kv"])
                        raw, rk_ = st_["raw"], ("draw", st_["kraw"])
                        lo = (QL if which == "q" else KL)[s_]
                        up = (QU if which == "q" else KU)[s_]
                        sl_ = slice(tb * 512, (tb + 1) * 512)
                        stt(lo[0:64, sl_], raw[0:64, :], gcol[0:64, :], v[0:64, :], ALU.mult, ALU.mult,
                            ["sc", vk, rk_], [(which + "L", s_, tb)])
                        stt(up[64:128, sl_], raw[64:128, :], gcol[64:128, :], v[64:128, :], ALU.mult, ALU.mult,
                            ["sc", vk, rk_], [(which + "U", s_, tb)])

                    return [st0, st1, st2, st3]

                def v_unit(h, g4):
                    s_ = h % 2

                    def st0():
                        sl = w_slot(g_dh[h])
                        b = dbanks.alloc()
                        for tl in range(4):
                            t = g4 * 4 + tl
                            for kc in range(NCH):
                                mm(ps[b][:, tl * 128:(tl + 1) * 128], hT[:, kc, t * 128:(t + 1) * 128], wsl[sl][:, kc, 256:384],
                                   kc == 0, kc == NCH - 1, [("hT", g4)] + wkeys(sl, 256, 128), [PSK(b)], skip=True)
                        cp("dve", Vd[s_][:, g4 * 4:(g4 + 1) * 4, :], ps[b][:, :].rearrange("p (a b) -> p a b", a=4), [],
                           [PSK(b), ("Vd", s_, g4)])
                        dbanks.free(b)

                    return [st0]

                def g_unit(h, tb, last):
                    s_ = h % 2
                    st_ = {}

                    def st0():
                        sl = w_slot(g_dh[h])
                        rf, rk = hT_rhs(tb)
                        b = dbanks.alloc()
                        proj(b, sl, 384, rf, rk)
                        st_["kraw"], st_["raw"] = r_raw.next()
                        cp("dve", st_["raw"][:], ps[b][:, :], [], [PSK(b), ("draw", st_["kraw"])])
                        dbanks.free(b)
                        if last:
                            w_done(g_dh[h])

                    def st1():
                        st_["kt"], st_["th"] = r_th.next()
                        act(st_["th"][:], st_["raw"][:], AF.Tanh, [("draw", st_["kraw"])], [("dth", st_["kt"])], scale=0.5)

                    def st2():
                        stt(gsd[s_][:, tb * 512:(tb + 1) * 512], st_["th"][:], 1.0, st_["raw"][:], ALU.add, ALU.mult,
                            [("dth", st_["kt"]), ("draw", st_["kraw"])], [("gsd", s_, tb)])

                    return [st0, st1, st2]

                class Prologue:
                    def __init__(self, h, period=2):
                        self.h = h
                        s_ = h % 2
                        self.units = ([qk_unit(h, "q", tb) for tb in range(NTB)] + [qk_unit(h, "k", tb) for tb in range(NTB)]
                                      + [v_unit(h, g4) for g4 in range(4)] + [g_unit(h, tb, tb == NTB - 1) for tb in range(NTB)])
                        self.active = []
                        self.t = 0
                        self.period = period
                        self.started = False

                    def tick(self):
                        h = self.h
                        s_ = h % 2
                        if not self.started:
                            self.started = True
                            P.dma("pool", QL[s_][64:68, :], qaug_d[h], writes=[("QLa", s_)], chan=f"aug{s_}")
                            P.dma("pool", QU[s_][0:4, :], qaug_d[h], writes=[("QUa", s_)], chan=f"aug{s_}")
                            P.dma("pool", KL[s_][64:68, :], kaug_d[h], writes=[("KLa", s_)], chan=f"aug{s_}")
                            P.dma("pool", KU[s_][0:4, :], kaug_d[h], writes=[("KUa", s_)], chan=f"aug{s_}")
                        if self.t % self.period == 0 and self.units:
                            self.active.append(self.units.pop(0))
                        self.t += 1
                        for u in list(self.active):
                            u.pop(0)()
                            if not u:
                                self.active.remove(u)

                    def done(self):
                        return not self.units and not self.active

                    def flush(self):
                        while not self.done():
                            self.tick()

                def scores(h, qb, kt):
                    s_ = h % 2
                    r = kt - 4 * qb
                    c0 = 128 * r if r > 0 else 0
                    n = 512 - c0
                    outp = []
                    for m in range(2):
                        b = dbanks.alloc()
                        if m == 0:
                            lhsT = KL[s_][0:68, kt * 128:(kt + 1) * 128]
                            rhs = QL[s_][0:68, qb * 512 + c0:(qb + 1) * 512]
                            rk = [("kL", s_, kt // 4), ("KLa", s_), ("qL", s_, qb), ("QLa", s_)]
                        else:
                            lhsT = KU[s_][:, kt * 128:(kt + 1) * 128]
                            rhs = QU[s_][:, qb * 512 + c0:(qb + 1) * 512]
                            rk = [("kU", s_, kt // 4), ("KUa", s_), ("qU", s_, qb), ("QUa", s_)]
                        mm(ps[b][:, c0:512], lhsT, rhs, True, r < 0, rk, [PSK(b)], skip=True)
                        if r >= 0:
                            mm(ps[b][:, c0:c0 + 128], ident, negmask, False, True, ["cb"], [PSK(b)], skip=True)
                        kp, pt = r_PT[m].next()
                        act(pt[:, c0:512], ps[b][:, c0:512], AF.Exp, [], [PSK(b), ("PT", m, kp)])
                        dbanks.free(b)
                        outp.append((kp, pt))
                    return (kt, c0, outp)

                def av(h, qb, sc_, first, last):
                    s_ = h % 2
                    kt, c0, outp = sc_
                    for m in range(2):
                        kp, pt = outp[m]
                        mm(ps[O_B[m]][:, c0:512], Vd[s_][:, kt, :], pt[:, c0:512], first, last,
                           [("Vd", s_, kt // 4), ("PT", m, kp)], [PSK(O_B[m])], skip=True)
                        mm(ps[L_B[m]][:, c0:512], ones, pt[:, c0:512], first, last,
                           ["cb", ("PT", m, kp)], [PSK(L_B[m])], skip=True)

                def epiA(h, qb, e):
                    e1, e2 = E1[e], E2[e]
                    cp("dve", e1[:], ps[O_B[0]][:, :], [], [PSK(O_B[0]), ("E1", e)])
                    act(RL[0][:], ps[L_B[0]][:, :], AF.Ln, [], [PSK(L_B[0]), ("RL", 0)])
                    cp("dve", e2[:], ps[O_B[1]][:, :], [], [PSK(O_B[1]), ("E2", e)])
                    act(RL[1][:], ps[L_B[1]][:, :], AF.Ln, [], [PSK(L_B[1]), ("RL", 1)])

                def epiB(h, qb, e):
                    e1, e2 = E1[e], E2[e]
                    act(RL[0][:], RL[0][:], AF.Exp, [("RL", 0)], [("RL", 0)], scale=-1.0)
                    act(RL[1][:], RL[1][:], AF.Exp, [("RL", 1)], [("RL", 1)], scale=-1.0)
                    tt("pool", e1[:], e1[:], RL[0][:], ALU.mult, [("RL", 0)], [("E1", e)])
                    tt("pool", e2[:], e2[:], RL[1][:], ALU.mult, [("RL", 1)], [("E2", e)])
                    stt(e1[:], e2[:], sc[:, SC_NLAM:SC_NLAM + 1], e1[:], ALU.mult, ALU.add, ["sc", ("E2", e)], [("E1", e)])

                def epiC(h, qb, e):
                    tt("pool", sqe[e][:], E1[e][:], E1[e][:], ALU.mult, [("E1", e)], [("sqe", e)])

                def epiD(h, qb, e):
                    s_ = h % 2
                    e1, e2 = E1[e], E2[e]
                    bs = dbanks.alloc()
                    mm(ps[bs][:, :], ones, sqe[e][:], True, True, ["cb", ("sqe", e)], [PSK(bs)])
                    rsqrt_dve_evac(bs, 512, 1.0 / 128.0, 1e-6, e2, ("E2", e))
                    dbanks.free(bs)
                    tt("pool", e1[:], e1[:], e2[:], ALU.mult, [("E2", e)], [("E1", e)])
                    stt(actT[:, h, qb * 512:(qb + 1) * 512], e1[:], sc[:, SC_SUB:SC_SUB + 1], gsd[s_][:, qb * 512:(qb + 1) * 512],
                        ALU.mult, ALU.mult, ["sc", ("E1", e), ("gsd", s_, qb)], [("act", h, qb)])

                ecount = 0
                pend = []
                pro = Prologue(0)
                pro.flush()
                for h in range(8):
                    pro = Prologue(h + 1) if h + 1 < 8 else None
                    for qb in range(NTB):
                        kts = list(range(4 * qb + 4))
                        cur = scores(h, qb, kts[0])
                        for i, kt in enumerate(kts):
                            nxt = scores(h, qb, kts[i + 1]) if i + 1 < len(kts) else None
                            av(h, qb, cur, i == 0, i == len(kts) - 1)
                            cur = nxt
                            if pend:
                                pend.pop(0)()
                            if pro is not None and not pro.done():
                                pro.tick()
                        assert not pend
                        e = ecount % 2
                        ecount += 1
                        epiA(h, qb, e)
                        pend = [(lambda h=h, qb=qb, e=e: epiB(h, qb, e)), (lambda h=h, qb=qb, e=e: epiC(h, qb, e)),
                                (lambda h=h, qb=qb, e=e: epiD(h, qb, e))]
                    if pro is not None:
                        pro.flush()
                while pend:
                    pend.pop(0)()
                for b in (4, 5, 6, 7):
                    banks.free(b)
                dump("actd", actT)
                P.barrier()
            ph_exit()
            with contextlib.ExitStack() as ph:
                ph_enter()
                branch_out(g_dp, g_m1, False, ph)
                P.barrier()

            ph_exit()
            with contextlib.ExitStack() as ph:
                ph_enter()
                xt = [T(ph, f"fx{i}", [128, D], F32) for i in range(2)]
                ot = [T(ph, f"fo{i}", [128, D], F32) for i in range(2)]
                s0 = w_slot(g_out[0])
                s1 = w_slot(g_out[1])
                for t in range(NT):
                    i = t % 2
                    P.dma("sp", xt[i][:], x_d[t * 128:(t + 1) * 128, :], writes=[("fx", i)], chan=f"x{i}")
                    for half in range(2):
                        sl = (s0, s1)[half]
                        b = banks.alloc()
                        for dc in range(NCH):
                            mm(ps[b][:, :], yT[:, dc, t * 128:(t + 1) * 128], wsl[sl][:, dc, :], dc == 0, dc == NCH - 1,
                               [("y", dc, t // 4)] + wkeys(sl, 0, 512), [PSK(b)])
                        tt("dve", ot[i][:, half * 512:(half + 1) * 512], ps[b][:, :], xt[i][:, half * 512:(half + 1) * 512], ALU.add,
                           [("fx", i)], [PSK(b), ("fo", i, half)])
                        banks.free(b)
                    P.dma("sp", out_d[t * 128:(t + 1) * 128, :], ot[i][:], reads=[("fo", i, 0), ("fo", i, 1)],
                          writes=[("fo_st", i)], chan=f"o{i}")
                P.op("sp", None, reads=[("fo_st", 0), ("fo_st", 1)])
            ph_exit()
        except _Stop:
            pass
        except Exception:
            import traceback
            traceback.print_exc()
            raise
        P.emit(st)
    return nc


def _host_constants():
    ident = np.eye(128, dtype=np.float32)
    ones = np.ones((128, 128), np.float32)
    p = np.arange(128)
    bd64 = (p[:, None] // 64 == p[None, :] // 64).astype(np.float32)
    negmask = np.where(p[None, :] < p[:, None], -30000.0, 0.0).astype(np.float32)
    cb = np.concatenate([ident, ones, bd64, negmask], axis=1)
    tok = np.arange(S)
    il = (tok % 128).astype(np.float32)
    ib = (tok // 128).astype(np.float32)
    qaug = np.zeros((8, 4, S), np.float32)
    kaug = np.zeros((8, 4, S), np.float32)
    for h in range(8):
        slope = 2.0 ** (-(h + 1))
        qaug[h, 0] = 1.0
        qaug[h, 1] = -slope * il
        qaug[h, 2] = 1.0
        qaug[h, 3] = -slope * 128.0 * ib
        kaug[h, 0] = slope * il
        kaug[h, 1] = 1.0
        kaug[h, 2] = slope * 128.0 * ib
        kaug[h, 3] = 1.0
    return cb, ident, qaug, kaug


_NC_CACHE = {}


def kernel(x, mem, norm_g, mem_norm_g, w_in, conv_dw, conv_dw_b, conv_ln_g, conv_ln_b, w_conv_proj,
           diff_qn_g, diff_kn_g, lambda_q1, lambda_k1, lambda_q2, lambda_k2, diff_subln_g, w_diff_proj,
           w_mem_kv, x_qn_g, x_kn_g, w_x_proj, w_out):
    f = lambda a: np.ascontiguousarray(np.asarray(a, dtype=np.float32))
    x = f(x); mem = f(mem)
    B = x.shape[0]
    vecs = np.concatenate([
        f(conv_dw_b)[0].reshape(8, 128), f(conv_ln_g)[0].reshape(8, 128), f(conv_ln_b)[0].reshape(8, 128),
        f(conv_dw)[0].reshape(31 * 8, 128),
        np.concatenate([f(diff_qn_g)[0], f(diff_qn_g)[0]])[None, :],
        np.concatenate([f(diff_kn_g)[0], f(diff_kn_g)[0]])[None, :],
        f(diff_subln_g)[0][None, :],
        f(x_qn_g)[0].reshape(2, 128), f(x_kn_g)[0].reshape(2, 128)], axis=0)
    assert vecs.shape == (NROWS, 128)
    lamv = np.concatenate([f(lambda_q1)[0], f(lambda_q2)[0], f(lambda_k1)[0], f(lambda_k2)[0]])[None, :]
    cb, identf, qaug, kaug = _host_constants()
    shared = {
        "w_in": f(w_in)[0], "w_conv_proj": f(w_conv_proj)[0], "w_diff_proj": f(w_diff_proj)[0],
        "w_x_proj": f(w_x_proj)[0], "w_out": f(w_out)[0], "w_mem_kv": f(w_mem_kv)[0],
        "norm_g": f(norm_g), "mem_norm_g": f(mem_norm_g), "vecs": np.ascontiguousarray(vecs),
        "lamv": np.ascontiguousarray(lamv), "cbits": cb, "identf": identf, "qaug": qaug, "kaug": kaug,
    }
    if "nc" not in _NC_CACHE:
        _NC_CACHE["nc"] = build_program()
    nc = _NC_CACHE["nc"]
    in_maps = [dict(shared, x=x[b], mem=mem[b]) for b in range(B)]
    res = run_bass_kernel_spmd(nc, in_maps, core_ids=list(range(B)))
    out = np.stack([np.asarray(r["out"]) for r in res.results], axis=0).astype(np.float32)
    if DEBUG is not None:
        kernel.dbg = [np.asarray(r["dbg"]) for r in res.results]
    return out
```

```python
import contextlib
import numpy as np
import concourse.bass as bass
import concourse.mybir as mybir
from concourse.bass_utils import run_bass_kernel_spmd

F32 = mybir.dt.float32
BF16 = mybir.dt.bfloat16
AF = mybir.ActivationFunctionType
ALU = mybir.AluOpType
AX = mybir.AxisListType

D = 1024
S = 2048
MEM = 256
NCH = 8
NTB = 4
NT = 16
IN_COLS = 12288
C_GLU, C_GATE, D_Q, D_K, D_V, D_GATE, X_Q, X_GATE, MERGE = 0, 2048, 3072, 4096, 5120, 6144, 7168, 8192, 9216
NSLOT = 3
ENGINES = ("pe", "act", "dve", "pool", "sp")

R_DWB, R_LNG, R_LNB, R_DW, R_QN, R_KN, R_SUB, R_XQ, R_XK, NROWS = 0, 8, 16, 24, 272, 273, 274, 275, 277, 279

DEBUG = None


class _Stop(Exception):
    pass


class Prog:
    def __init__(self, nc):
        self.nc = nc
        self.ins = []
        self.last_w = {}
        self.readers = {}
        self.chan_count = {}
        self.last_on = {}

    def _add(self, eng, fn, reads, writes, dma_chan=None, extra_deps=()):
        idx = len(self.ins)
        deps = set(extra_deps)
        for r in reads:
            w = self.last_w.get(r)
            if w is not None:
                deps.add(w)
            if fn is not None:
                self.readers.setdefault(r, []).append(idx)
        for w_ in writes:
            w = self.last_w.get(w_)
            if w is not None:
                deps.add(w)
            for rd in self.readers.get(w_, ()):
                if rd != idx:
                    deps.add(rd)
            self.last_w[w_] = idx
            self.readers[w_] = []
        rec = dict(eng=eng, fn=fn, dma_chan=dma_chan, mark=False)
        waits = []
        for d in deps:
            dr = self.ins[d]
            if dr["dma_chan"] is not None:
                waits.append(("dma", dr["dma_chan"], self.chan_count[dr["dma_chan"]]))
            else:
                if dr["eng"] == "pe" and eng == "pe":
                    continue
                dr["mark"] = True
                waits.append(("eng", dr["eng"], d))
        rec["waits"] = waits
        if dma_chan is not None:
            self.chan_count[dma_chan] = self.chan_count.get(dma_chan, 0) + 16
        elif fn is not None:
            self.last_on[eng] = idx
        self.ins.append(rec)
        return idx

    def op(self, eng, fn, reads=(), writes=()):
        return self._add(eng, fn, tuple(reads), tuple(writes))

    def dma(self, eng, out, in_, reads=(), writes=(), chan=None):
        return self._add(eng, lambda e: e.dma_start(out=out, in_=in_), tuple(reads), tuple(writes),
                         dma_chan=chan)

    def barrier(self):
        lasts = [v for v in self.last_on.values()]
        for e in ENGINES:
            idx = self._add(e, None, (), (), extra_deps=lasts)
            rec = self.ins[idx]
            for c, v in self.chan_count.items():
                rec["waits"].append(("dma", c, v))

    def emit(self, stack):
        nc = self.nc
        sems = {e: stack.enter_context(nc.semaphore("s_" + e)) for e in ENGINES}
        csems = {c: stack.enter_context(nc.semaphore("c_" + str(c))) for c in self.chan_count}
        cnt = {e: 0 for e in ENGINES}
        for r in self.ins:
            if r["dma_chan"] is None and r["mark"]:
                cnt[r["eng"]] += 1
                r["ord"] = cnt[r["eng"]]
        per = {e: [] for e in ENGINES}
        for r in self.ins:
            per[r["eng"]].append(r)
        block = stack.enter_context(nc.Block())
        ins = self.ins

        def run(engname, eng):
            waited = {}
            for r in per[engname]:
                need = {}
                for w in r["waits"]:
                    if w[0] == "dma":
                        key = ("c", w[1]); val = w[2]
                    else:
                        key = ("e", w[1]); val = ins[w[2]]["ord"]
                    if val > need.get(key, 0):
                        need[key] = val
                for key, val in need.items():
                    if waited.get(key, 0) >= val:
                        continue
                    waited[key] = val
                    eng.wait_ge(csems[key[1]] if key[0] == "c" else sems[key[1]], val)
                if r["fn"] is None:
                    continue
                bi = r["fn"](eng)
                if r["dma_chan"] is not None:
                    bi.then_inc(csems[r["dma_chan"]], 16)
                elif r["mark"]:
                    bi.then_inc(sems[engname], 1)

        block.tensor(lambda e: run("pe", e))
        block.scalar(lambda e: run("act", e))
        block.vector(lambda e: run("dve", e))
        block.gpsimd(lambda e: run("pool", e))
        block.sync(lambda e: run("sp", e))


class Banks:
    def __init__(self, ids):
        self.ids = list(ids)
        self.busy = set()
        self.ptr = 0

    def alloc(self):
        n = len(self.ids)
        for k in range(n):
            b = self.ids[(self.ptr + k) % n]
            if b not in self.busy:
                self.busy.add(b)
                self.ptr = (self.ptr + k + 1) % n
                return b
        raise RuntimeError("out of PSUM banks")

    def free(self, b):
        self.busy.discard(b)


class Ring:
    def __init__(self, items):
        self.items = items
        self.i = 0

    def next(self):
        r = self.items[self.i % len(self.items)]
        k = self.i % len(self.items)
        self.i += 1
        return k, r


def build_program():
    nc = bass.Bass("TRN2", target_bir_lowering=False)
    dt_in = lambda name, shape: nc.dram_tensor(name, list(shape), F32, kind="ExternalInput").ap()
    x_d = dt_in("x", [S, D])
    mem_d = dt_in("mem", [MEM, D])
    w_in_d = dt_in("w_in", [D, IN_COLS])
    wcp_d = dt_in("w_conv_proj", [D, D])
    wdp_d = dt_in("w_diff_proj", [D, D])
    wxp_d = dt_in("w_x_proj", [D, D])
    wout_d = dt_in("w_out", [D, D])
    wkv_d = dt_in("w_mem_kv", [D, 2 * D])
    ng_d = dt_in("norm_g", [1, D])
    mg_d = dt_in("mem_norm_g", [1, D])
    vecs_d = dt_in("vecs", [NROWS, 128])
    lam_d = dt_in("lamv", [1, 256])
    cb_d = dt_in("cbits", [128, 4 * 128])
    idf_d = dt_in("identf", [128, 128])
    qaug_d = dt_in("qaug", [8, 4, S])
    kaug_d = dt_in("kaug", [8, 4, S])
    out_d = nc.dram_tensor("out", [S, D], F32, kind="ExternalOutput").ap()
    dbg_d = None
    if DEBUG is not None:
        dbg_d = nc.dram_tensor("dbg", [128, 8 * S], BF16, kind="ExternalOutput").ap()

    wviews = {
        "in": w_in_d.rearrange("(kc p) c -> p kc c", p=128),
        "cp": wcp_d.rearrange("(kc p) c -> p kc c", p=128),
        "dp": wdp_d.rearrange("(kc p) c -> p kc c", p=128),
        "xp": wxp_d.rearrange("(kc p) c -> p kc c", p=128),
        "out": wout_d.rearrange("(kc p) c -> p kc c", p=128),
        "kv": wkv_d.rearrange("(kc p) c -> p kc c", p=128),
    }

    with contextlib.ExitStack() as st:
        P = Prog(nc)

        tcount = [0]

        def T(stack, name, shape, dt):
            tcount[0] += 1
            return stack.enter_context(nc.sbuf_tensor(f"sb{tcount[0]}_" + name, list(shape), dt))

        hT = T(st, "hT", [128, NCH, S], BF16)
        yT = T(st, "yT", [128, NCH, S], BF16)
        actT = T(st, "actT", [128, NCH, S], BF16)
        wsl = [T(st, f"wsl{i}", [128, NCH, 512], BF16) for i in range(NSLOT)]
        cb = T(st, "cb", [128, 4, 128], BF16)
        identf = T(st, "identf", [128, 128], F32)
        vecsT = T(st, "vecsT", [128, NROWS], F32)
        sc = T(st, "sc", [128, 64], F32)
        ps = [st.enter_context(nc.psum_tensor(f"ps{i}", [128, 512], F32)) for i in range(8)]
        ident = cb[:, 0, :]
        ones = cb[:, 1, :]
        bd64 = cb[:, 2, :]
        negmask = cb[:, 3, :]
        SC_QN, SC_KN, SC_SUB, SC_XQ, SC_XK, SC_NLAM, SC_LNGH, SC_LNBH, SC_EPS6, SC_EPS5 = 0, 1, 2, 3, 5, 7, 8, 16, 30, 31
        banks = Banks(range(8))

        def PSK(b):
            return ("ps", b)

        groups = []

        def G(*parts):
            groups.append(list(parts))
            return len(groups) - 1

        g_kv = [G((0, "kv", i * 512, 512)) for i in range(4)]
        g_xh = [G((0, "in", X_Q + h * 256, 256), (256, "in", X_GATE + h * 256, 256)) for h in range(4)]
        g_xp = [G((0, "xp", i * 512, 512)) for i in range(2)]
        g_m2 = [G((0, "in", MERGE + 2 * D + i * 512, 512)) for i in range(2)]
        g_cab = [G((0, "in", C_GLU + jj * 256, 256), (256, "in", C_GLU + D + jj * 256, 256)) for jj in range(4)]
        g_cg = [G((0, "in", C_GATE + i * 512, 512)) for i in range(2)]
        g_cp = [G((0, "cp", i * 512, 512)) for i in range(2)]
        g_m0 = [G((0, "in", MERGE + i * 512, 512)) for i in range(2)]
        g_dh = [G((0, "in", D_Q + h * 128, 128), (128, "in", D_K + h * 128, 128),
                  (256, "in", D_V + h * 128, 128), (384, "in", D_GATE + h * 128, 128)) for h in range(8)]
        g_dp = [G((0, "dp", i * 512, 512)) for i in range(2)]
        g_m1 = [G((0, "in", MERGE + D + i * 512, 512)) for i in range(2)]
        g_out = [G((0, "out", i * 512, 512)) for i in range(2)]
        order = (g_kv + g_xh + [g_xp[0], g_m2[0], g_xp[1], g_m2[1]] + g_cab + g_cg
                 + [g_cp[0], g_m0[0], g_cp[1], g_m0[1]] + g_dh + [g_dp[0], g_m1[0], g_dp[1], g_m1[1]] + g_out)
        assert sorted(order) == list(range(len(groups)))
        pos_of = {g: i for i, g in enumerate(order)}
        wstate = {"issued": 0}

        def wkeys(slot, c0, n):
            return [("w", slot, q) for q in range(c0 // 128, (c0 + n) // 128)]

        def w_issue_next():
            i = wstate["issued"]
            if i >= len(order):
                return
            g = order[i]
            slot = i % NSLOT
            for (dc, wn, sc0, n) in groups[g]:
                P.dma("pool", wsl[slot][:, :, dc:dc + n], wviews[wn][:, :, sc0:sc0 + n],
                      writes=wkeys(slot, dc, n), chan=f"w{slot}")
            wstate["issued"] = i + 1

        def w_slot(g):
            i = pos_of[g]
            assert i < wstate["issued"], "weight group not issued yet"
            assert i >= wstate["issued"] - NSLOT
            return i % NSLOT

        def w_done(g):
            w_issue_next()

        def mm(out, lhsT, rhs, start, stop, reads, writes, skip=False):
            if skip:
                P.op("pe", lambda e: e.matmul(out, lhsT=lhsT, rhs=rhs, start=start, stop=stop,
                                              skip_group_check=True), reads, writes)
            else:
                P.op("pe", lambda e: e.matmul(out, lhsT=lhsT, rhs=rhs, start=start, stop=stop), reads, writes)

        def act(out, in_, func, reads, writes, scale=1.0, bias=0.0, accum=None):
            if accum is not None:
                P.op("act", lambda e: e.activation(out=out, in_=in_, func=func, scale=scale, bias=bias,
                                                   accum_out=accum), reads, writes)
            else:
                P.op("act", lambda e: e.activation(out=out, in_=in_, func=func, scale=scale, bias=bias),
                     reads, writes)

        def ts(eng, out, in0, s1, s2, op0, op1, reads, writes):
            if s2 is None:
                P.op(eng, lambda e: e.tensor_scalar(out=out, in0=in0, scalar1=s1, scalar2=None, op0=op0),
                     reads, writes)
            else:
                P.op(eng, lambda e: e.tensor_scalar(out=out, in0=in0, scalar1=s1, scalar2=s2, op0=op0, op1=op1),
                     reads, writes)

        def stt(out, in0, scalar, in1, op0, op1, reads, writes):
            P.op("dve", lambda e: e.scalar_tensor_tensor(out=out, in0=in0, scalar=scalar, in1=in1, op0=op0,
                                                         op1=op1), reads, writes)

        def tt(eng, out, in0, in1, op, reads, writes):
            P.op(eng, lambda e: e.tensor_tensor(out=out, in0=in0, in1=in1, op=op), reads, writes)

        def cp(eng, out, in_, reads, writes):
            if eng == "act":
                P.op("act", lambda e: e.copy(out=out, in_=in_), reads, writes)
            else:
                P.op(eng, lambda e: e.tensor_copy(out=out, in_=in_), reads, writes)

        def proj(bank, slot, col0, rhs_fn, rkeys, ncols=128, n=512, accum_extra=None):
            for kc in range(NCH):
                mm(ps[bank][:, 0:n], wsl[slot][:, kc, col0:col0 + 128], rhs_fn(kc), kc == 0, kc == NCH - 1,
                   reads=wkeys(slot, col0, 128) + rkeys, writes=[PSK(bank)])

        def hT_rhs(tb):
            return (lambda kc: hT[:, kc, tb * 512:(tb + 1) * 512]), [("hT", tb)]

        def rsqrt_from_psum(bank, n, scale, eps, vbuf, vkey):
            ecol = sc[:, SC_EPS6:SC_EPS6 + 1] if eps == 1e-6 else sc[:, SC_EPS5:SC_EPS5 + 1]
            act(vbuf[:, 0:n], ps[bank][:, 0:n], AF.Ln, ["sc"], [PSK(bank), vkey], scale=scale, bias=ecol)
            act(vbuf[:, 0:n], vbuf[:, 0:n], AF.Exp, [vkey], [vkey], scale=-0.5)

        state = {"done": False, "inph": False}

        def ph_enter():
            state["inph"] = True

        def ph_exit():
            state["inph"] = False
            if state["done"]:
                raise _Stop()

        def rsqrt_dve_evac(bank, n, scale, eps, vbuf, vkey):
            ts("dve", vbuf[:, 0:n], ps[bank][:, 0:n], scale, eps, ALU.mult, ALU.add, reads=[], writes=[PSK(bank), vkey])
            act(vbuf[:, 0:n], vbuf[:, 0:n], AF.Ln, [vkey], [vkey])
            act(vbuf[:, 0:n], vbuf[:, 0:n], AF.Exp, [vkey], [vkey], scale=-0.5)

        def dump(tag, tensor):
            if DEBUG == tag:
                P.barrier()
                P.dma("sp", dbg_d, tensor[:].rearrange("p a b -> p (a b)"), writes=["dbg"], chan="dbg")
                P.op("sp", None, reads=["dbg"])
                state["done"] = True
                if not state["inph"]:
                    raise _Stop()

        for i in range(NSLOT):
            w_issue_next()
        P.dma("pool", cb[:].rearrange("p a b -> p (a b)"), cb_d, writes=["cb"], chan="c0")
        P.dma("sp", identf[:], idf_d, writes=["identf"], chan="c1")
        P.op("pool", lambda e: e.memset(sc[:, SC_EPS6:SC_EPS6 + 1], 1e-6), writes=["sc"])
        P.op("pool", lambda e: e.memset(sc[:, SC_EPS5:SC_EPS5 + 1], 1e-5), writes=["sc"])
        with contextlib.ExitStack() as ph:
            vst = [T(ph, f"vst{i}", [128, 128], F32) for i in range(3)]
            lamb = T(ph, "lamb", [128, 256], F32)
            lamp = T(ph, "lamp", [128, 128], F32)
            lams = T(ph, "lams", [128, 2], F32)
            rows = [(0, 128), (128, 128), (256, NROWS - 256)]
            for i, (r0, n) in enumerate(rows):
                P.dma("sp", vst[i][0:n, :], vecs_d[r0:r0 + n, :], writes=[("vst", i)], chan="c1")
            P.dma("sp", lamb[:], lam_d.partition_broadcast(128), writes=["lamb"], chan="c1")
            b = banks.alloc()
            for i, (r0, n) in enumerate(rows):
                mm(ps[b][:, r0:r0 + n], vst[i][0:n, :], identf[0:n, 0:n], True, True,
                   reads=[("vst", i), "identf"], writes=[PSK(b)], skip=True)
            cp("dve", vecsT[:], ps[b][:, 0:NROWS], reads=[], writes=[PSK(b), "vecsT"])
            banks.free(b)
            ts("dve", sc[:, SC_QN:SC_QN + 1], vecsT[:, R_QN:R_QN + 1], 0.125, None, ALU.mult, None, ["vecsT"], ["sc"])
            cp("dve", sc[:, SC_KN:SC_KN + 1], vecsT[:, R_KN:R_KN + 1], ["vecsT"], ["sc"])
            ts("dve", sc[:, SC_SUB:SC_SUB + 1], vecsT[:, R_SUB:R_SUB + 1], 0.4, None, ALU.mult, None, ["vecsT"], ["sc"])
            cp("dve", sc[:, SC_XQ:SC_XQ + 2], vecsT[:, R_XQ:R_XQ + 2], ["vecsT"], ["sc"])
            ts("dve", sc[:, SC_XK:SC_XK + 2], vecsT[:, R_XK:R_XK + 2], 1.0 / 16.0, None, ALU.mult, None, ["vecsT"], ["sc"])
            ts("dve", sc[:, SC_LNGH:SC_LNGH + 8], vecsT[:, R_LNG:R_LNG + 8], 0.5, None, ALU.mult, None, ["vecsT"], ["sc"])
            ts("dve", sc[:, SC_LNBH:SC_LNBH + 8], vecsT[:, R_LNB:R_LNB + 8], 0.5, None, ALU.mult, None, ["vecsT"], ["sc"])
            tt("dve", lamp[:], lamb[:, 0:128], lamb[:, 128:256], ALU.mult, ["lamb"], ["lamp"])
            P.op("dve", lambda e: e.reduce_sum(out=lams[:], in_=lamp[:].rearrange("p (a b) -> p a b", a=2),
                                               axis=AX.X), ["lamp"], ["lams"])
            act(lams[:], lams[:], AF.Exp, ["lams"], ["lams"])
            tt("dve", lams[:, 0:1], lams[:, 1:2], lams[:, 0:1], ALU.subtract, ["lams"], ["lams"])
            ts("dve", sc[:, SC_NLAM:SC_NLAM + 1], lams[:, 0:1], -0.2, None, ALU.add, None, ["lams"], ["sc"])
            P.barrier()

        try:

            def branch_out(gp, gm, first, ph):
                thm = [T(ph, f"thm{i}", [128, 512], F32) for i in range(2)]
                tb_ = [T(ph, f"tbo{i}", [128, 512], F32) for i in range(2)]
                r_th = Ring(thm)
                r_t = Ring(tb_)
                for half in range(2):
                    sp_ = w_slot(gp[half])
                    sm_ = w_slot(gm[half])
                    for jj in range(4):
                        j = half * 4 + jj
                        for tb in range(NTB):
                            bp = banks.alloc()
                            for c in range(NCH):
                                mm(ps[bp][:, :], wsl[sp_][:, c, jj * 128:(jj + 1) * 128], actT[:, c, tb * 512:(tb + 1) * 512],
                                   c == 0, c == NCH - 1, reads=wkeys(sp_, jj * 128, 128) + [("act", c, tb)], writes=[PSK(bp)])
                            bm = banks.alloc()
                            rf, rk = hT_rhs(tb)
                            proj(bm, sm_, jj * 128, rf, rk)
                            k1, th = r_th.next()
                            act(th[:], ps[bm][:, :], AF.Tanh, [], [PSK(bm), ("thm", k1)], scale=0.5)
                            banks.free(bm)
                            k2, tbuf = r_t.next()
                            stt(tbuf[:], th[:], 1.0, ps[bp][:, :], ALU.add, ALU.mult, [("thm", k1)], [PSK(bp), ("tbo", k2)])
                            banks.free(bp)
                            ysl = yT[:, j, tb * 512:(tb + 1) * 512]
                            if first:
                                ts("dve", ysl, tbuf[:], 0.5, None, ALU.mult, None, [("tbo", k2)], [("y", j, tb)])
                            else:
                                stt(ysl, tbuf[:], 0.5, ysl, ALU.mult, ALU.add, [("tbo", k2)], [("y", j, tb)])
                    w_done(gp[half])
                    w_done(gm[half])

            with contextlib.ExitStack() as ph:
                ph_enter()
                xt = [T(ph, f"xt{i}", [128, D], F32) for i in range(2)]
                xn = [T(ph, f"xn{i}", [128, D], BF16) for i in range(2)]
                junk = T(ph, "junk", [128, D], BF16)
                gbc = T(ph, "gbc", [128, D], F32)
                gmbc = T(ph, "gmbc", [128, D], F32)
                ssq = T(ph, "ssq", [128, 32], F32)
                memT = T(ph, "memT", [128, NCH, MEM], BF16)
                P.dma("sp", gbc[:], ng_d.partition_broadcast(128), writes=["gbc"], chan="c1")
                P.dma("sp", gmbc[:], mg_d.partition_broadcast(128), writes=["gmbc"], chan="c1")
                tiles = [("m", i) for i in range(2)] + [("x", i) for i in range(NT)]
                for n_, (kind, t) in enumerate(tiles):
                    i = n_ % 2
                    src = (mem_d if kind == "m" else x_d)[t * 128:(t + 1) * 128, :]
                    P.dma("sp", xt[i][:], src, writes=[("xt", i)], chan=f"x{i}")
                    col = ssq[:, n_:n_ + 1]
                    act(junk[:], xt[i][:], AF.Square, [("xt", i)], ["junk"])
                    P.op("dve", (lambda e, col=col: e.reduce_sum(out=col, in_=junk[:], axis=AX.X)), ["junk"], [("ssq", n_)])
                    act(col, col, AF.Ln, [("ssq", n_), "sc"], [("ssq", n_)], scale=1.0 / D, bias=sc[:, SC_EPS6:SC_EPS6 + 1])
                    act(col, col, AF.Exp, [("ssq", n_)], [("ssq", n_)], scale=-0.5)
                    gsrc = gmbc if kind == "m" else gbc
                    stt(xn[i][:], xt[i][:], col, gsrc[:], ALU.mult, ALU.mult,
                        [("xt", i), ("ssq", n_), "gbc", "gmbc"], [("xn", i)])
                    b = banks.alloc()
                    pbf = ps[b][:].bitcast(BF16)
                    for kc in range(NCH):
                        P.op("pe", (lambda e, kc=kc, i=i, pbf=pbf: e.transpose(out=pbf[:, kc * 128:(kc + 1) * 128],
                                                                              in_=xn[i][:, kc * 128:(kc + 1) * 128],
                                                                              identity=ident)),
                             reads=[("xn", i), "cb"], writes=[PSK(b)])
                    src3 = pbf.rearrange("p (a b) -> p a b", a=NCH)
                    if kind == "m":
                        cp("dve", memT[:, :, t * 128:(t + 1) * 128], src3, [], [PSK(b), "memT"])
                    else:
                        cp("dve", hT[:, :, t * 128:(t + 1) * 128], src3, [], [PSK(b), ("hT", t // 4)])
                    banks.free(b)
                with contextlib.ExitStack() as ph:
                    ph_enter()
                    kT = T(ph, "kT", [128, NCH, MEM], BF16)
                    Vx = T(ph, "Vx", [128, 2, D], BF16)
                    qT = T(ph, "qT", [128, 2, S], BF16)
                    sqb = [T(ph, f"xsq{i}", [128, 512], BF16) for i in range(4)]
                    vb = [T(ph, f"xvb{i}", [128, 512], F32) for i in range(2)]
                    PT = [T(ph, f"xPT{i}", [128, 512], BF16) for i in range(4)]
                    thg = [T(ph, f"xthg{i}", [128, 512], F32) for i in range(2)]
                    gsb = [T(ph, f"xgs{i}", [128, 512], F32) for i in range(2)]
                    rlb = [T(ph, f"xrl{i}", [128, 512], F32) for i in range(2)]
                    tob = [T(ph, f"xto{i}", [128, 512], F32) for i in range(2)]
                    r_sq, r_vb, r_PT, r_thg, r_gs, r_rl, r_to = (Ring(sqb), Ring(vb), Ring(PT), Ring(thg), Ring(gsb),
                                                                 Ring(rlb), Ring(tob))
                    for hx in range(4):
                        bks = []
                        sqs = []
                        for dc in range(2):
                            c = hx * 2 + dc
                            g = g_kv[c // 4]
                            sl = w_slot(g)
                            b = banks.alloc()
                            proj(b, sl, (c % 4) * 128, lambda kc: memT[:, kc, :], ["memT"], n=MEM)
                            k_, sq = r_sq.next()
                            act(sq[:, 0:MEM], ps[b][:, 0:MEM], AF.Square, [], [PSK(b), ("xsq", k_)])
                            bks.append(b)
                            sqs.append((k_, sq))
                            if c == 3:
                                w_done(g_kv[0])
                            if c == 7:
                                w_done(g_kv[1])
                        bs = banks.alloc()
                        for dc in range(2):
                            mm(ps[bs][:, 0:MEM], ones, sqs[dc][1][:, 0:MEM], dc == 0, dc == 1, ["cb", ("xsq", sqs[dc][0])], [PSK(bs)])
                        kv_, v = r_vb.next()
                        rsqrt_from_psum(bs, MEM, 1.0 / 256.0, 1e-6, v, ("xvb", kv_))
                        banks.free(bs)
                        for dc in range(2):
                            c = hx * 2 + dc
                            stt(kT[:, c, :], ps[bks[dc]][:, 0:MEM], sc[:, SC_XK + dc:SC_XK + dc + 1], v[:, 0:MEM], ALU.mult, ALU.mult,
                                ["sc", ("xvb", kv_)], [PSK(bks[dc]), ("kT", c)])
                            banks.free(bks[dc])
                    for vg in range(2):
                        sl = w_slot(g_kv[2 + vg])
                        for mt in range(2):
                            b = banks.alloc()
                            for kc in range(NCH):
                                mm(ps[b][:, :], memT[:, kc, mt * 128:(mt + 1) * 128], wsl[sl][:, kc, :], kc == 0, kc == NCH - 1,
                                   ["memT"] + wkeys(sl, 0, 512), [PSK(b)])
                            cp("dve", Vx[:, mt, vg * 512:(vg + 1) * 512], ps[b][:, :], [], [PSK(b), ("Vx", mt, vg)])
                            banks.free(b)
                        w_done(g_kv[2 + vg])
                    for hx in range(4):
                        sl = w_slot(g_xh[hx])
                        pend = None

                        def q_finish(pend):
                            tb, bq, sqs = pend
                            bs = banks.alloc()
                            for dc in range(2):
                                mm(ps[bs][:, :], ones, sqs[dc][1][:], dc == 0, dc == 1, ["cb", ("xsq", sqs[dc][0])], [PSK(bs)])
                            kv_, v = r_vb.next()
                            rsqrt_from_psum(bs, 512, 1.0 / 256.0, 1e-6, v, ("xvb", kv_))
                            banks.free(bs)
                            for dc in range(2):
                                stt(qT[:, dc, tb * 512:(tb + 1) * 512], ps[bq[dc]][:, :], sc[:, SC_XQ + dc:SC_XQ + dc + 1], v[:],
                                    ALU.mult, ALU.mult, ["sc", ("xvb", kv_)], [PSK(bq[dc]), ("qT", dc, tb)])
                                banks.free(bq[dc])

                        for tb in range(NTB):
                            bq = []
                            sqs = []
                            rf, rk = hT_rhs(tb)
                            for dc in range(2):
                                b = banks.alloc()
                                proj(b, sl, dc * 128, rf, rk)
                                k_, sq = r_sq.next()
                                act(sq[:], ps[b][:, :], AF.Square, [], [PSK(b), ("xsq", k_)])
                                bq.append(b)
                                sqs.append((k_, sq))
                            if pend is not None:
                                q_finish(pend)
                            pend = (tb, bq, sqs)
                        q_finish(pend)
                        for tb in range(NTB):
                            pts = []
                            for mt in range(2):
                                b = banks.alloc()
                                for dc in range(2):
                                    mm(ps[b][:, :], kT[:, hx * 2 + dc, mt * 128:(mt + 1) * 128], qT[:, dc, tb * 512:(tb + 1) * 512],
                                       dc == 0, dc == 1, [("kT", hx * 2 + dc), ("qT", dc, tb)], [PSK(b)])
                                kp, pt = r_PT.next()
                                act(pt[:], ps[b][:, :], AF.Exp, [], [PSK(b), ("xPT", kp)])
                                banks.free(b)
                                pts.append((kp, pt))
                            bo = []
                            for vc in range(2):
                                b = banks.alloc()
                                for mt in range(2):
                                    c0 = hx * 256 + vc * 128
                                    mm(ps[b][:, :], Vx[:, mt, c0:c0 + 128], pts[mt][1][:], mt == 0, mt == 1,
                                       [("Vx", mt, c0 // 512), ("xPT", pts[mt][0])], [PSK(b)])
                                bo.append(b)
                            bl = banks.alloc()
                            for mt in range(2):
                                mm(ps[bl][:, :], ones, pts[mt][1][:], mt == 0, mt == 1, ["cb", ("xPT", pts[mt][0])], [PSK(bl)])
                            bg = []
                            rf, rk = hT_rhs(tb)
                            for vc in range(2):
                                b = banks.alloc()
                                proj(b, sl, 256 + vc * 128, rf, rk)
                                bg.append(b)
                            kr, rl = r_rl.next()
                            act(rl[:], ps[bl][:, :], AF.Ln, [], [PSK(bl), ("xrl", kr)])
                            banks.free(bl)
                            act(rl[:], rl[:], AF.Exp, [("xrl", kr)], [("xrl", kr)], scale=-1.0)
                            for vc in range(2):
                                kt_, th = r_thg.next()
                                act(th[:], ps[bg[vc]][:, :], AF.Tanh, [], [PSK(bg[vc]), ("xthg", kt_)], scale=0.5)
                                kg, gs = r_gs.next()
                                stt(gs[:], th[:], 1.0, ps[bg[vc]][:, :], ALU.add, ALU.mult, [("xthg", kt_)], [PSK(bg[vc]), ("xgs", kg)])
                                banks.free(bg[vc])
                                ko, to = r_to.next()
                                tt("dve", to[:], ps[bo[vc]][:, :], rl[:], ALU.mult, [("xrl", kr)], [PSK(bo[vc]), ("xto", ko)])
                                banks.free(bo[vc])
                                stt(actT[:, hx * 2 + vc, tb * 512:(tb + 1) * 512], to[:], 0.5, gs[:], ALU.mult, ALU.mult,
                                    [("xto", ko), ("xgs", kg)], [("act", hx * 2 + vc, tb)])
                        w_done(g_xh[hx])
                    dump("actx", actT)
                    if not state["done"]:
                        branch_out(g_xp, g_m2, True, ph)
                    P.barrier()
            ph_exit()
            dump("yx", yT)

            with contextlib.ExitStack() as ph:
                ph_enter()
                PADW = 30
                ub = [T(ph, f"ub{i}", [128, PADW + S], BF16) for i in range(2)]
                dg = [T(ph, f"dg{i}", [128, 31, 128], BF16) for i in range(2)]
                thb = [T(ph, f"cth{i}", [128, 512], F32) for i in range(2)]
                Ms = [T(ph, f"cM{i}", [128, 512], F32) for i in range(NTB)]
                Rs = [T(ph, f"cR{i}", [128, 512], F32) for i in range(NTB)]
                zt = [T(ph, f"cz{i}", [128, 512], F32) for i in range(2)]
                sqc = [T(ph, f"csq{i}", [128, 512], BF16) for i in range(2)]
                gsc = [T(ph, f"cgs{i}", [128, 512], F32) for i in range(2)]
                r_th, r_z, r_sq, r_gs = Ring(thb), Ring(zt), Ring(sqc), Ring(gsc)
                for i in range(2):
                    P.op("pool", (lambda e, i=i: e.memset(ub[i][:, 0:PADW], 0.0)), writes=[("ub", i, -1)])
                for j in range(NCH):
                    g = g_cab[j // 2]
                    sl = w_slot(g)
                    jl = j % 2
                    ui = j % 2
                    for k in range(31):
                        col = vecsT[:, R_DW + k * 8 + j:R_DW + k * 8 + j + 1]
                        P.op("pool", (lambda e, k=k, ui=ui, col=col: e.tensor_scalar(out=dg[ui][:, k, :], in0=ident, scalar1=col,
                                                                                     scalar2=0.5, op0=ALU.mult, op1=ALU.mult)),
                             reads=["cb", "vecsT"], writes=[("dg", ui)])
                    pend = None

                    def conv_block(tb, ui=ui, j=j):
                        b = banks.alloc()
                        rk = [("ub", ui, tb), ("ub", ui, tb - 1), ("dg", ui)]
                        for k in range(31):
                            mm(ps[b][:, :], dg[ui][:, k, :], ub[ui][:, tb * 512 + k:tb * 512 + k + 512], k == 0, k == 30, rk, [PSK(b)])
                        ts("dve", actT[:, j, tb * 512:(tb + 1) * 512], ps[b][:, :], vecsT[:, R_DWB + j:R_DWB + j + 1], None, ALU.add, None,
                           ["vecsT"], [PSK(b), ("act", j, tb)])
                        banks.free(b)

                    for tb in range(NTB):
                        rf, rk = hT_rhs(tb)
                        ba = banks.alloc()
                        proj(ba, sl, jl * 128, rf, rk)
                        bb = banks.alloc()
                        proj(bb, sl, 256 + jl * 128, rf, rk)
                        kt_, th = r_th.next()
                        act(th[:], ps[bb][:, :], AF.Tanh, [], [PSK(bb), ("cth", kt_)], scale=0.5)
                        banks.free(bb)
                        stt(ub[ui][:, PADW + tb * 512:PADW + (tb + 1) * 512], th[:], 1.0, ps[ba][:, :], ALU.add, ALU.mult,
                            [("cth", kt_)], [PSK(ba), ("ub", ui, tb)])
                        banks.free(ba)
                        if pend is not None:
                            conv_block(pend)
                        pend = tb
                    conv_block(pend)
                    if jl == 1:
                        w_done(g)
                dump("conv", actT)
                for tb in range(NTB):
                    bs = banks.alloc()
                    bq = banks.alloc()
                    for j in range(NCH):
                        a = actT[:, j, tb * 512:(tb + 1) * 512]
                        mm(ps[bs][:, :], ones, a, j == 0, j == NCH - 1, ["cb", ("act", j, tb)], [PSK(bs)])
                        ks, sq = r_sq.next()
                        act(sq[:], a, AF.Square, [("act", j, tb)], [("csq", ks)])
                        mm(ps[bq][:, :], ones, sq[:], j == 0, j == NCH - 1, ["cb", ("csq", ks)], [PSK(bq)])
                    M, R = Ms[tb], Rs[tb]
                    ts("dve", M[:], ps[bs][:, :], 1.0 / D, None, ALU.mult, None, [], [PSK(bs), ("cM", tb)])
                    banks.free(bs)
                    tt("dve", R[:], M[:], M[:], ALU.mult, [("cM", tb)], [("cR", tb)])
                    stt(R[:], ps[bq][:, :], 1.0 / D, R[:], ALU.mult, ALU.subtract, [], [PSK(bq), ("cR", tb)])
                    banks.free(bq)
                    act(R[:], R[:], AF.Ln, [("cR", tb), "sc"], [("cR", tb)], scale=1.0, bias=sc[:, SC_EPS5:SC_EPS5 + 1])
                    act(R[:], R[:], AF.Exp, [("cR", tb)], [("cR", tb)], scale=-0.5)
                    stt(M[:], M[:], -1.0, R[:], ALU.mult, ALU.mult, [("cR", tb)], [("cM", tb)])
                for half in range(2):
                    slg = w_slot(g_cg[half])
                    for jj in range(4):
                        j = half * 4 + jj
                        for tb in range(NTB):
                            a = actT[:, j, tb * 512:(tb + 1) * 512]
                            kz, z = r_z.next()
                            tt("dve", z[:], a, Rs[tb][:], ALU.mult, [("act", j, tb), ("cR", tb)], [("cz", kz)])
                            tt("dve", z[:], z[:], Ms[tb][:], ALU.add, [("cM", tb)], [("cz", kz)])
                            kt_, th = r_th.next()
                            act(th[:], z[:], AF.Tanh, [("cz", kz), "sc"], [("cth", kt_)],
                                scale=sc[:, SC_LNGH + j:SC_LNGH + j + 1], bias=sc[:, SC_LNBH + j:SC_LNBH + j + 1])
                            ts("dve", z[:], z[:], vecsT[:, R_LNG + j:R_LNG + j + 1], vecsT[:, R_LNB + j:R_LNB + j + 1], ALU.mult, ALU.add,
                               ["vecsT"], [("cz", kz)])
                            stt(z[:], th[:], 1.0, z[:], ALU.add, ALU.mult, [("cth", kt_)], [("cz", kz)])
                            bg = banks.alloc()
                            rf, rk = hT_rhs(tb)
                            proj(bg, slg, jj * 128, rf, rk)
                            kt2, th2 = r_th.next()
                            act(th2[:], ps[bg][:, :], AF.Tanh, [], [PSK(bg), ("cth", kt2)], scale=0.5)
                            kg, gs = r_gs.next()
                            stt(gs[:], th2[:], 1.0, ps[bg][:, :], ALU.add, ALU.mult, [("cth", kt2)], [PSK(bg), ("cgs", kg)])
                            banks.free(bg)
                            stt(a, z[:], 0.25, gs[:], ALU.mult, ALU.mult, [("cz", kz), ("cgs", kg)], [("act", j, tb)])
                    w_done(g_cg[half])
                dump("actc", actT)
                if not state["done"]:
                    branch_out(g_cp, g_m0, False, ph)
                P.barrier()
            ph_exit()
            dump("yc", yT)

            with contextlib.ExitStack() as ph:
                ph_enter()
                QL = [T(ph, f"QL{i}", [128, S], BF16) for i in range(2)]
                QU = [T(ph, f"QU{i}", [128, S], BF16) for i in range(2)]
                KL = [T(ph, f"KL{i}", [128, S], BF16) for i in range(2)]
                KU = [T(ph, f"KU{i}", [128, S], BF16) for i in range(2)]
                Vd = [T(ph, f"Vd{i}", [128, NT, 128], BF16) for i in range(2)]
                gsd = [T(ph, f"gsd{i}", [128, S], BF16) for i in range(2)]
                PTd = [[T(ph, f"PT{m}_{i}", [128, 512], BF16) for i in range(3)] for m in range(2)]
                sqd = [T(ph, f"dsq{i}", [128, 512], BF16) for i in range(2)]
                thd = [T(ph, f"dth{i}", [128, 512], F32) for i in range(1)]
                vbd = [T(ph, f"dvb{i}", [128, 512], F32) for i in range(2)]
                E1 = [T(ph, f"E1_{i}", [128, 512], F32) for i in range(2)]
                E2 = [T(ph, f"E2_{i}", [128, 512], F32) for i in range(2)]
                sqe = [T(ph, f"sqe{i}", [128, 512], BF16) for i in range(2)]
                RL = [T(ph, f"RL{i}", [128, 512], F32) for i in range(2)]
                r_PT = [Ring(PTd[0]), Ring(PTd[1])]
                rawd = [T(ph, f"draw{i}", [128, 512], F32) for i in range(3)]
                r_raw = Ring(rawd)
                r_sq, r_th, r_vb = Ring(sqd), Ring(thd), Ring(vbd)
                for i in range(2):
                    P.op("pool", (lambda e, i=i: e.memset(QU[i][0:64, :], 0.0)), writes=[("QUa", i)])
                    P.op("pool", (lambda e, i=i: e.memset(KU[i][0:64, :], 0.0)), writes=[("KUa", i)])
                O_B = [4, 6]
                L_B = [5, 7]
                for b in (4, 5, 6, 7):
                    banks.busy.add(b)
                dbanks = Banks([0, 1, 2, 3])

                def qk_unit(h, which, tb):
                    s_ = h % 2
                    st_ = {}
                    c0 = 0 if which == "q" else 128
                    gcol = sc[:, SC_QN:SC_QN + 1] if which == "q" else sc[:, SC_KN:SC_KN + 1]

                    def st0():
                        sl = w_slot(g_dh[h])
                        rf, rk = hT_rhs(tb)
                        b = dbanks.alloc()
                        proj(b, sl, c0, rf, rk)
                        st_["kraw"], st_["raw"] = r_raw.next()
                        cp("dve", st_["raw"][:], ps[b][:, :], [], [PSK(b), ("draw", st_["kraw"])])
                        dbanks.free(b)
                        st_["ks"], sq = r_sq.next()
                        tt("dve", sq[:], st_["raw"][:], st_["raw"][:], ALU.mult, [("draw", st_["kraw"])], [("dsq", st_["ks"])])

                    def st1():
                        bs = dbanks.alloc()
                        mm(ps[bs][:, :], bd64, sqd[st_["ks"]][:], True, True, ["cb", ("dsq", st_["ks"])], [PSK(bs)])
                        st_["kv"], st_["v"] = r_vb.next()
                        ts("dve", st_["v"][:], ps[bs][:, :], 1.0 / 64.0, 1e-6, ALU.mult, ALU.add, [], [PSK(bs), ("dvb", st_["kv"])])
                        dbanks.free(bs)

                    def st2():
                        v, vk = st_["v"], ("dvb", st_["kv"])
                        act(v[:], v[:], AF.Ln, [vk], [vk])
                        act(v[:], v[:], AF.Exp, [vk], [vk], scale=-0.5)

                    def st3():
                        v, vk = st_["v"], ("dvb", st_["kv"])
                        raw, rk_ = st_["raw"], ("draw", st_["kraw"])
                        lo = (QL if which == "q" else KL)[s_]
                        up = (QU if which == "q" else KU)[s_]
                        sl_ = slice(tb * 512, (tb + 1) * 512)
                        stt(lo[0:64, sl_], raw[0:64, :], gcol[0:64, :], v[0:64, :], ALU.mult, ALU.mult,
                            ["sc", vk, rk_], [(which + "L", s_, tb)])
                        stt(up[64:128, sl_], raw[64:128, :], gcol[64:128, :], v[64:128, :], ALU.mult, ALU.mult,
                            ["sc", vk, rk_], [(which + "U", s_, tb)])

                    return [st0, None, st1, st2, st3]

                def v_unit(h, g4):
                    s_ = h % 2

                    def st0():
                        sl = w_slot(g_dh[h])
                        b = dbanks.alloc()
                        for tl in range(4):
                            t = g4 * 4 + tl
                            for kc in range(NCH):
                                mm(ps[b][:, tl * 128:(tl + 1) * 128], hT[:, kc, t * 128:(t + 1) * 128], wsl[sl][:, kc, 256:384],
                                   kc == 0, kc == NCH - 1, [("hT", g4)] + wkeys(sl, 256, 128), [PSK(b)], skip=True)
                        cp("dve", Vd[s_][:, g4 * 4:(g4 + 1) * 4, :], ps[b][:, :].rearrange("p (a b) -> p a b", a=4), [],
                           [PSK(b), ("Vd", s_, g4)])
                        dbanks.free(b)

                    return [st0]

                def g_unit(h, tb, last):
                    s_ = h % 2
                    st_ = {}

                    def st0():
                        sl = w_slot(g_dh[h])
                        rf, rk = hT_rhs(tb)
                        b = dbanks.alloc()
                        proj(b, sl, 384, rf, rk)
                        st_["kraw"], st_["raw"] = r_raw.next()
                        cp("dve", st_["raw"][:], ps[b][:, :], [], [PSK(b), ("draw", st_["kraw"])])
                        dbanks.free(b)
                        if last:
                            w_done(g_dh[h])

                    def st1():
                        st_["kt"], st_["th"] = r_th.next()
                        act(st_["th"][:], st_["raw"][:], AF.Tanh, [("draw", st_["kraw"])], [("dth", st_["kt"])], scale=0.5)

                    def st2():
                        stt(gsd[s_][:, tb * 512:(tb + 1) * 512], st_["th"][:], 1.0, st_["raw"][:], ALU.add, ALU.mult,
                            [("dth", st_["kt"]), ("draw", st_["kraw"])], [("gsd", s_, tb)])

                    return [st0, st1, st2]

                class Prologue:
                    def __init__(self, h, period=2):
                        self.h = h
                        s_ = h % 2
                        self.units = ([qk_unit(h, "q", tb) for tb in range(NTB)] + [qk_unit(h, "k", tb) for tb in range(NTB)]
                                      + [v_unit(h, g4) for g4 in range(4)] + [g_unit(h, tb, tb == NTB - 1) for tb in range(NTB)])
                        self.active = []
                        self.t = 0
                        self.period = period
                        self.started = False

                    def tick(self):
                        h = self.h
                        s_ = h % 2
                        if not self.started:
                            self.started = True
                            P.dma("pool", QL[s_][64:68, :], qaug_d[h], writes=[("QLa", s_)], chan=f"aug{s_}")
                            P.dma("pool", QU[s_][0:4, :], qaug_d[h], writes=[("QUa", s_)], chan=f"aug{s_}")
                            P.dma("pool", KL[s_][64:68, :], kaug_d[h], writes=[("KLa", s_)], chan=f"aug{s_}")
                            P.dma("pool", KU[s_][0:4, :], kaug_d[h], writes=[("KUa", s_)], chan=f"aug{s_}")
                        if self.t % self.period == 0 and self.units:
                            self.active.append(self.units.pop(0))
                        self.t += 1
                        for u in list(self.active):
                            f_ = u.pop(0)
                            if f_ is not None:
                                f_()
                            if not u:
                                self.active.remove(u)

                    def done(self):
                        return not self.units and not self.active

                    def flush(self):
                        while not self.done():
                            self.tick()

                def scores(h, qb, kt):
                    s_ = h % 2
                    r = kt - 4 * qb
                    c0 = 128 * r if r > 0 else 0
                    n = 512 - c0
                    outp = []
                    for m in range(2):
                        b = dbanks.alloc()
                        if m == 0:
                            lhsT = KL[s_][0:68, kt * 128:(kt + 1) * 128]
                            rhs = QL[s_][0:68, qb * 512 + c0:(qb + 1) * 512]
                            rk = [("kL", s_, kt // 4), ("KLa", s_), ("qL", s_, qb), ("QLa", s_)]
                        else:
                            lhsT = KU[s_][:, kt * 128:(kt + 1) * 128]
                            rhs = QU[s_][:, qb * 512 + c0:(qb + 1) * 512]
                            rk = [("kU", s_, kt // 4), ("KUa", s_), ("qU", s_, qb), ("QUa", s_)]
                        mm(ps[b][:, c0:512], lhsT, rhs, True, r < 0, rk, [PSK(b)], skip=True)
                        if r >= 0:
                            mm(ps[b][:, c0:c0 + 128], ident, negmask, False, True, ["cb"], [PSK(b)], skip=True)
                        kp, pt = r_PT[m].next()
                        act(pt[:, c0:512], ps[b][:, c0:512], AF.Exp, [], [PSK(b), ("PT", m, kp)])
                        dbanks.free(b)
                        outp.append((kp, pt))
                    return (kt, c0, outp)

                def av(h, qb, sc_, first, last):
                    s_ = h % 2
                    kt, c0, outp = sc_
                    for m in range(2):
                        kp, pt = outp[m]
                        mm(ps[O_B[m]][:, c0:512], Vd[s_][:, kt, :], pt[:, c0:512], first, last,
                           [("Vd", s_, kt // 4), ("PT", m, kp)], [PSK(O_B[m])], skip=True)
                        mm(ps[L_B[m]][:, c0:512], ones, pt[:, c0:512], first, last,
                           ["cb", ("PT", m, kp)], [PSK(L_B[m])], skip=True)

                def epiA(h, qb, e):
                    e1, e2 = E1[e], E2[e]
                    cp("dve", e1[:], ps[O_B[0]][:, :], [], [PSK(O_B[0]), ("E1", e)])
                    act(RL[0][:], ps[L_B[0]][:, :], AF.Ln, [], [PSK(L_B[0]), ("RL", 0)])
                    cp("dve", e2[:], ps[O_B[1]][:, :], [], [PSK(O_B[1]), ("E2", e)])
                    act(RL[1][:], ps[L_B[1]][:, :], AF.Ln, [], [PSK(L_B[1]), ("RL", 1)])

                def epiB(h, qb, e):
                    e1, e2 = E1[e], E2[e]
                    act(RL[0][:], RL[0][:], AF.Exp, [("RL", 0)], [("RL", 0)], scale=-1.0)
                    act(RL[1][:], RL[1][:], AF.Exp, [("RL", 1)], [("RL", 1)], scale=-1.0)
                    tt("dve", e1[:], e1[:], RL[0][:], ALU.mult, [("RL", 0)], [("E1", e)])
                    tt("dve", e2[:], e2[:], RL[1][:], ALU.mult, [("RL", 1)], [("E2", e)])
                    stt(e1[:], e2[:], sc[:, SC_NLAM:SC_NLAM + 1], e1[:], ALU.mult, ALU.add, ["sc", ("E2", e)], [("E1", e)])

                def epiC(h, qb, e):
                    tt("dve", sqe[e][:], E1[e][:], E1[e][:], ALU.mult, [("E1", e)], [("sqe", e)])

                def epiD(h, qb, e):
                    s_ = h % 2
                    e1, e2 = E1[e], E2[e]
                    bs = dbanks.alloc()
                    mm(ps[bs][:, :], ones, sqe[e][:], True, True, ["cb", ("sqe", e)], [PSK(bs)])
                    rsqrt_dve_evac(bs, 512, 1.0 / 128.0, 1e-6, e2, ("E2", e))
                    dbanks.free(bs)
                    tt("dve", e1[:], e1[:], e2[:], ALU.mult, [("E2", e)], [("E1", e)])
                    stt(actT[:, h, qb * 512:(qb + 1) * 512], e1[:], sc[:, SC_SUB:SC_SUB + 1], gsd[s_][:, qb * 512:(qb + 1) * 512],
                        ALU.mult, ALU.mult, ["sc", ("E1", e), ("gsd", s_, qb)], [("act", h, qb)])

                ecount = 0
                pend = []
                pro = Prologue(0)
                pro.flush()
                for h in range(8):
                    pro = Prologue(h + 1) if h + 1 < 8 else None
                    for qb in range(NTB):
                        kts = list(range(4 * qb + 4))
                        cur = scores(h, qb, kts[0])
                        for i, kt in enumerate(kts):
                            nxt = scores(h, qb, kts[i + 1]) if i + 1 < len(kts) else None
                            av(h, qb, cur, i == 0, i == len(kts) - 1)
                            cur = nxt
                            if pend:
                                pend.pop(0)()
                            if pro is not None and not pro.done():
                                pro.tick()
                        assert not pend
                        e = ecount % 2
                        ecount += 1
                        epiA(h, qb, e)
                        pend = [(lambda h=h, qb=qb, e=e: epiB(h, qb, e)), (lambda h=h, qb=qb, e=e: epiC(h, qb, e)),
                                (lambda h=h, qb=qb, e=e: epiD(h, qb, e))]
                    if pro is not None:
                        pro.flush()
                while pend:
                    pend.pop(0)()
                for b in (4, 5, 6, 7):
                    banks.free(b)
                dump("actd", actT)
                P.barrier()
            ph_exit()
            with contextlib.ExitStack() as ph:
                ph_enter()
                branch_out(g_dp, g_m1, False, ph)
                P.barrier()

            ph_exit()
            with contextlib.ExitStack() as ph:
                ph_enter()
                xt = [T(ph, f"fx{i}", [128, D], F32) for i in range(2)]
                ot = [T(ph, f"fo{i}", [128, D], F32) for i in range(2)]
                s0 = w_slot(g_out[0])
                s1 = w_slot(g_out[1])
                for t in range(NT):
                    i = t % 2
                    P.dma("sp", xt[i][:], x_d[t * 128:(t + 1) * 128, :], writes=[("fx", i)], chan=f"x{i}")
                    for half in range(2):
                        sl = (s0, s1)[half]
                        b = banks.alloc()
                        for dc in range(NCH):
                            mm(ps[b][:, :], yT[:, dc, t * 128:(t + 1) * 128], wsl[sl][:, dc, :], dc == 0, dc == NCH - 1,
                               [("y", dc, t // 4)] + wkeys(sl, 0, 512), [PSK(b)])
                        tt("dve", ot[i][:, half * 512:(half + 1) * 512], ps[b][:, :], xt[i][:, half * 512:(half + 1) * 512], ALU.add,
                           [("fx", i)], [PSK(b), ("fo", i, half)])
                        banks.free(b)
                    P.dma("sp", out_d[t * 128:(t + 1) * 128, :], ot[i][:], reads=[("fo", i, 0), ("fo", i, 1)],
                          writes=[("fo_st", i)], chan=f"o{i}")
                P.op("sp", None, reads=[("fo_st", 0), ("fo_st", 1)])
            ph_exit()
        except _Stop:
            pass
        except Exception:
            import traceback
            traceback.print_exc()
            raise
        P.emit(st)
    return nc


def _host_constants():
    ident = np.eye(128, dtype=np.float32)
    ones = np.ones((128, 128), np.float32)
    p = np.arange(128)
    bd64 = (p[:, None] // 64 == p[None, :] // 64).astype(np.float32)
    negmask = np.where(p[None, :] < p[:, None], -30000.0, 0.0).astype(np.float32)
    cb = np.concatenate([ident, ones, bd64, negmask], axis=1)
    tok = np.arange(S)
    il = (tok % 128).astype(np.float32)
    ib = (tok // 128).astype(np.float32)
    qaug = np.zeros((8, 4, S), np.float32)
    kaug = np.zeros((8, 4, S), np.float32)
    for h in range(8):
        slope = 2.0 ** (-(h + 1))
        qaug[h, 0] = 1.0
        qaug[h, 1] = -slope * il
        qaug[h, 2] = 1.0
        qaug[h, 3] = -slope * 128.0 * ib
        kaug[h, 0] = slope * il
        kaug[h, 1] = 1.0
        kaug[h, 2] = slope * 128.0 * ib
        kaug[h, 3] = 1.0
    return cb, ident, qaug, kaug


_NC_CACHE = {}


def kernel(x, mem, norm_g, mem_norm_g, w_in, conv_dw, conv_dw_b, conv_ln_g, conv_ln_b, w_conv_proj,
           diff_qn_g, diff_kn_g, lambda_q1, lambda_k1, lambda_q2, lambda_k2, diff_subln_g, w_diff_proj,
           w_mem_kv, x_qn_g, x_kn_g, w_x_proj, w_out):
    f = lambda a: np.ascontiguousarray(np.asarray(a, dtype=np.float32))
    x = f(x); mem = f(mem)
    B = x.shape[0]
    vecs = np.concatenate([
        f(conv_dw_b)[0].reshape(8, 128), f(conv_ln_g)[0].reshape(8, 128), f(conv_ln_b)[0].reshape(8, 128),
        f(conv_dw)[0].reshape(31 * 8, 128),
        np.concatenate([f(diff_qn_g)[0], f(diff_qn_g)[0]])[None, :],
        np.concatenate([f(diff_kn_g)[0], f(diff_kn_g)[0]])[None, :],
        f(diff_subln_g)[0][None, :],
        f(x_qn_g)[0].reshape(2, 128), f(x_kn_g)[0].reshape(2, 128)], axis=0)
    assert vecs.shape == (NROWS, 128)
    lamv = np.concatenate([f(lambda_q1)[0], f(lambda_q2)[0], f(lambda_k1)[0], f(lambda_k2)[0]])[None, :]
    cb, identf, qaug, kaug = _host_constants()
    shared = {
        "w_in": f(w_in)[0], "w_conv_proj": f(w_conv_proj)[0], "w_diff_proj": f(w_diff_proj)[0],
        "w_x_proj": f(w_x_proj)[0], "w_out": f(w_out)[0], "w_mem_kv": f(w_mem_kv)[0],
        "norm_g": f(norm_g), "mem_norm_g": f(mem_norm_g), "vecs": np.ascontiguousarray(vecs),
        "lamv": np.ascontiguousarray(lamv), "cbits": cb, "identf": identf, "qaug": qaug, "kaug": kaug,
    }
    if "nc" not in _NC_CACHE:
        _NC_CACHE["nc"] = build_program()
    nc = _NC_CACHE["nc"]
    in_maps = [dict(shared, x=x[b], mem=mem[b]) for b in range(B)]
    res = run_bass_kernel_spmd(nc, in_maps, core_ids=list(range(B)))
    out = np.stack([np.asarray(r["out"]) for r in res.results], axis=0).astype(np.float32)
    if DEBUG is not None:
        kernel.dbg = [np.asarray(r["dbg"]) for r in res.results]
    return out
```

```python
import contextlib
import numpy as np
import concourse.bass as bass
import concourse.mybir as mybir
from concourse.bass_utils import run_bass_kernel_spmd

F32 = mybir.dt.float32
BF16 = mybir.dt.bfloat16
AF = mybir.ActivationFunctionType
ALU = mybir.AluOpType
AX = mybir.AxisListType

D = 1024
S = 2048
MEM = 256
NCH = 8
NTB = 4
NT = 16
IN_COLS = 12288
C_GLU, C_GATE, D_Q, D_K, D_V, D_GATE, X_Q, X_GATE, MERGE = 0, 2048, 3072, 4096, 5120, 6144, 7168, 8192, 9216
NSLOT = 3
ENGINES = ("pe", "act", "dve", "pool", "sp")

R_DWB, R_LNG, R_LNB, R_DW, R_QN, R_KN, R_SUB, R_XQ, R_XK, NROWS = 0, 8, 16, 24, 272, 273, 274, 275, 277, 279

DEBUG = None


class _Stop(Exception):
    pass


class Prog:
    def __init__(self, nc):
        self.nc = nc
        self.ins = []
        self.last_w = {}
        self.readers = {}
        self.chan_count = {}
        self.last_on = {}

    def _add(self, eng, fn, reads, writes, dma_chan=None, extra_deps=()):
        idx = len(self.ins)
        deps = set(extra_deps)
        for r in reads:
            w = self.last_w.get(r)
            if w is not None:
                deps.add(w)
            if fn is not None:
                self.readers.setdefault(r, []).append(idx)
        for w_ in writes:
            w = self.last_w.get(w_)
            if w is not None:
                deps.add(w)
            for rd in self.readers.get(w_, ()):
                if rd != idx:
                    deps.add(rd)
            self.last_w[w_] = idx
            self.readers[w_] = []
        rec = dict(eng=eng, fn=fn, dma_chan=dma_chan, mark=False)
        waits = []
        for d in deps:
            dr = self.ins[d]
            if dr["dma_chan"] is not None:
                waits.append(("dma", dr["dma_chan"], self.chan_count[dr["dma_chan"]]))
            else:
                if dr["eng"] == "pe" and eng == "pe":
                    continue
                dr["mark"] = True
                waits.append(("eng", dr["eng"], d))
        rec["waits"] = waits
        if dma_chan is not None:
            self.chan_count[dma_chan] = self.chan_count.get(dma_chan, 0) + 16
        elif fn is not None:
            self.last_on[eng] = idx
        self.ins.append(rec)
        return idx

    def op(self, eng, fn, reads=(), writes=()):
        return self._add(eng, fn, tuple(reads), tuple(writes))

    def dma(self, eng, out, in_, reads=(), writes=(), chan=None):
        return self._add(eng, lambda e: e.dma_start(out=out, in_=in_), tuple(reads), tuple(writes),
                         dma_chan=chan)

    def barrier(self):
        lasts = [v for v in self.last_on.values()]
        for e in ENGINES:
            idx = self._add(e, None, (), (), extra_deps=lasts)
            rec = self.ins[idx]
            for c, v in self.chan_count.items():
                if str(c).startswith("w") or str(c).startswith("aug"):
                    continue
                rec["waits"].append(("dma", c, v))

    def emit(self, stack):
        nc = self.nc
        sems = {e: stack.enter_context(nc.semaphore("s_" + e)) for e in ENGINES}
        csems = {c: stack.enter_context(nc.semaphore("c_" + str(c))) for c in self.chan_count}
        cnt = {e: 0 for e in ENGINES}
        for r in self.ins:
            if r["dma_chan"] is None and r["mark"]:
                cnt[r["eng"]] += 1
                r["ord"] = cnt[r["eng"]]
        per = {e: [] for e in ENGINES}
        for r in self.ins:
            per[r["eng"]].append(r)
        block = stack.enter_context(nc.Block())
        ins = self.ins

        def run(engname, eng):
            waited = {}
            for r in per[engname]:
                need = {}
                for w in r["waits"]:
                    if w[0] == "dma":
                        key = ("c", w[1]); val = w[2]
                    else:
                        key = ("e", w[1]); val = ins[w[2]]["ord"]
                    if val > need.get(key, 0):
                        need[key] = val
                for key, val in need.items():
                    if waited.get(key, 0) >= val:
                        continue
                    waited[key] = val
                    eng.wait_ge(csems[key[1]] if key[0] == "c" else sems[key[1]], val)
                if r["fn"] is None:
                    continue
                bi = r["fn"](eng)
                if r["dma_chan"] is not None:
                    bi.then_inc(csems[r["dma_chan"]], 16)
                elif r["mark"]:
                    bi.then_inc(sems[engname], 1)

        block.tensor(lambda e: run("pe", e))
        block.scalar(lambda e: run("act", e))
        block.vector(lambda e: run("dve", e))
        block.gpsimd(lambda e: run("pool", e))
        block.sync(lambda e: run("sp", e))


class Banks:
    def __init__(self, ids):
        self.ids = list(ids)
        self.busy = set()
        self.ptr = 0

    def alloc(self):
        n = len(self.ids)
        for k in range(n):
            b = self.ids[(self.ptr + k) % n]
            if b not in self.busy:
                self.busy.add(b)
                self.ptr = (self.ptr + k + 1) % n
                return b
        raise RuntimeError("out of PSUM banks")

    def free(self, b):
        self.busy.discard(b)


class Ring:
    def __init__(self, items):
        self.items = items
        self.i = 0

    def next(self):
        r = self.items[self.i % len(self.items)]
        k = self.i % len(self.items)
        self.i += 1
        return k, r


def build_program():
    nc = bass.Bass("TRN2", target_bir_lowering=False)
    dt_in = lambda name, shape: nc.dram_tensor(name, list(shape), F32, kind="ExternalInput").ap()
    x_d = dt_in("x", [S, D])
    mem_d = dt_in("mem", [MEM, D])
    w_in_d = dt_in("w_in", [D, IN_COLS])
    wcp_d = dt_in("w_conv_proj", [D, D])
    wdp_d = dt_in("w_diff_proj", [D, D])
    wxp_d = dt_in("w_x_proj", [D, D])
    wout_d = dt_in("w_out", [D, D])
    wkv_d = dt_in("w_mem_kv", [D, 2 * D])
    ng_d = dt_in("norm_g", [1, D])
    mg_d = dt_in("mem_norm_g", [1, D])
    vecs_d = dt_in("vecs", [NROWS, 128])
    lam_d = dt_in("lamv", [1, 256])
    cb_d = dt_in("cbits", [128, 4 * 128])
    idf_d = dt_in("identf", [128, 128])
    qaug_d = dt_in("qaug", [8, 4, S])
    kaug_d = dt_in("kaug", [8, 4, S])
    out_d = nc.dram_tensor("out", [S, D], F32, kind="ExternalOutput").ap()
    dbg_d = None
    if DEBUG is not None:
        dbg_d = nc.dram_tensor("dbg", [128, 8 * S], BF16, kind="ExternalOutput").ap()

    wviews = {
        "in": w_in_d.rearrange("(kc p) c -> p kc c", p=128),
        "cp": wcp_d.rearrange("(kc p) c -> p kc c", p=128),
        "dp": wdp_d.rearrange("(kc p) c -> p kc c", p=128),
        "xp": wxp_d.rearrange("(kc p) c -> p kc c", p=128),
        "out": wout_d.rearrange("(kc p) c -> p kc c", p=128),
        "kv": wkv_d.rearrange("(kc p) c -> p kc c", p=128),
    }

    with contextlib.ExitStack() as st:
        P = Prog(nc)

        tcount = [0]

        def T(stack, name, shape, dt):
            tcount[0] += 1
            return stack.enter_context(nc.sbuf_tensor(f"sb{tcount[0]}_" + name, list(shape), dt))

        hT = T(st, "hT", [128, NCH, S], BF16)
        yT = T(st, "yT", [128, NCH, S], BF16)
        actT = T(st, "actT", [128, NCH, S], BF16)
        wsl = [T(st, f"wsl{i}", [128, NCH, 512], BF16) for i in range(NSLOT)]
        cb = T(st, "cb", [128, 4, 128], BF16)
        identf = T(st, "identf", [128, 128], F32)
        vecsT = T(st, "vecsT", [128, NROWS], F32)
        sc = T(st, "sc", [128, 64], F32)
        ps = [st.enter_context(nc.psum_tensor(f"ps{i}", [128, 512], F32)) for i in range(8)]
        ident = cb[:, 0, :]
        ones = cb[:, 1, :]
        bd64 = cb[:, 2, :]
        negmask = cb[:, 3, :]
        SC_QN, SC_KN, SC_SUB, SC_XQ, SC_XK, SC_NLAM, SC_LNGH, SC_LNBH, SC_EPS6, SC_EPS5 = 0, 1, 2, 3, 5, 7, 8, 16, 30, 31
        banks = Banks(range(8))

        def PSK(b):
            return ("ps", b)

        groups = []

        def G(*parts):
            groups.append(list(parts))
            return len(groups) - 1

        g_kv = [G((0, "kv", i * 512, 512)) for i in range(4)]
        g_xh = [G((0, "in", X_Q + h * 256, 256), (256, "in", X_GATE + h * 256, 256)) for h in range(4)]
        g_xp = [G((0, "xp", i * 512, 512)) for i in range(2)]
        g_m2 = [G((0, "in", MERGE + 2 * D + i * 512, 512)) for i in range(2)]
        g_cab = [G((0, "in", C_GLU + jj * 256, 256), (256, "in", C_GLU + D + jj * 256, 256)) for jj in range(4)]
        g_cg = [G((0, "in", C_GATE + i * 512, 512)) for i in range(2)]
        g_cp = [G((0, "cp", i * 512, 512)) for i in range(2)]
        g_m0 = [G((0, "in", MERGE + i * 512, 512)) for i in range(2)]
        g_dh = [G((0, "in", D_Q + h * 128, 128), (128, "in", D_K + h * 128, 128),
                  (256, "in", D_V + h * 128, 128), (384, "in", D_GATE + h * 128, 128)) for h in range(8)]
        g_dp = [G((0, "dp", i * 512, 512)) for i in range(2)]
        g_m1 = [G((0, "in", MERGE + D + i * 512, 512)) for i in range(2)]
        g_out = [G((0, "out", i * 512, 512)) for i in range(2)]
        order = (g_kv + g_xh + [g_xp[0], g_m2[0], g_xp[1], g_m2[1]] + g_cab + g_cg
                 + [g_cp[0], g_m0[0], g_cp[1], g_m0[1]] + g_dh + [g_dp[0], g_m1[0], g_dp[1], g_m1[1]] + g_out)
        assert sorted(order) == list(range(len(groups)))
        pos_of = {g: i for i, g in enumerate(order)}
        wstate = {"issued": 0}

        def wkeys(slot, c0, n):
            return [("w", slot, q) for q in range(c0 // 128, (c0 + n) // 128)]

        def w_issue_next():
            i = wstate["issued"]
            if i >= len(order):
                return
            g = order[i]
            slot = i % NSLOT
            for (dc, wn, sc0, n) in groups[g]:
                P.dma("pool", wsl[slot][:, :, dc:dc + n], wviews[wn][:, :, sc0:sc0 + n],
                      writes=wkeys(slot, dc, n), chan=f"w{slot}")
            wstate["issued"] = i + 1

        def w_slot(g):
            i = pos_of[g]
            assert i < wstate["issued"], "weight group not issued yet"
            assert i >= wstate["issued"] - NSLOT
            return i % NSLOT

        def w_done(g):
            w_issue_next()

        def mm(out, lhsT, rhs, start, stop, reads, writes, skip=False):
            if skip:
                P.op("pe", lambda e: e.matmul(out, lhsT=lhsT, rhs=rhs, start=start, stop=stop,
                                              skip_group_check=True), reads, writes)
            else:
                P.op("pe", lambda e: e.matmul(out, lhsT=lhsT, rhs=rhs, start=start, stop=stop), reads, writes)

        def act(out, in_, func, reads, writes, scale=1.0, bias=0.0, accum=None):
            if accum is not None:
                P.op("act", lambda e: e.activation(out=out, in_=in_, func=func, scale=scale, bias=bias,
                                                   accum_out=accum), reads, writes)
            else:
                P.op("act", lambda e: e.activation(out=out, in_=in_, func=func, scale=scale, bias=bias),
                     reads, writes)

        def ts(eng, out, in0, s1, s2, op0, op1, reads, writes):
            if s2 is None:
                P.op(eng, lambda e: e.tensor_scalar(out=out, in0=in0, scalar1=s1, scalar2=None, op0=op0),
                     reads, writes)
            else:
                P.op(eng, lambda e: e.tensor_scalar(out=out, in0=in0, scalar1=s1, scalar2=s2, op0=op0, op1=op1),
                     reads, writes)

        def stt(out, in0, scalar, in1, op0, op1, reads, writes):
            P.op("dve", lambda e: e.scalar_tensor_tensor(out=out, in0=in0, scalar=scalar, in1=in1, op0=op0,
                                                         op1=op1), reads, writes)

        def tt(eng, out, in0, in1, op, reads, writes):
            P.op(eng, lambda e: e.tensor_tensor(out=out, in0=in0, in1=in1, op=op), reads, writes)

        def cp(eng, out, in_, reads, writes):
            if eng == "act":
                P.op("act", lambda e: e.copy(out=out, in_=in_), reads, writes)
            else:
                P.op(eng, lambda e: e.tensor_copy(out=out, in_=in_), reads, writes)

        def proj(bank, slot, col0, rhs_fn, rkeys, ncols=128, n=512, accum_extra=None):
            for kc in range(NCH):
                mm(ps[bank][:, 0:n], wsl[slot][:, kc, col0:col0 + 128], rhs_fn(kc), kc == 0, kc == NCH - 1,
                   reads=wkeys(slot, col0, 128) + rkeys, writes=[PSK(bank)])

        def hT_rhs(tb):
            return (lambda kc: hT[:, kc, tb * 512:(tb + 1) * 512]), [("hT", tb)]

        def rsqrt_from_psum(bank, n, scale, eps, vbuf, vkey):
            ecol = sc[:, SC_EPS6:SC_EPS6 + 1] if eps == 1e-6 else sc[:, SC_EPS5:SC_EPS5 + 1]
            act(vbuf[:, 0:n], ps[bank][:, 0:n], AF.Ln, ["sc"], [PSK(bank), vkey], scale=scale, bias=ecol)
            act(vbuf[:, 0:n], vbuf[:, 0:n], AF.Exp, [vkey], [vkey], scale=-0.5)

        state = {"done": False, "inph": False}

        def ph_enter():
            state["inph"] = True

        def ph_exit():
            state["inph"] = False
            if state["done"]:
                raise _Stop()

        def rsqrt_dve_evac(bank, n, scale, eps, vbuf, vkey):
            ts("dve", vbuf[:, 0:n], ps[bank][:, 0:n], scale, eps, ALU.mult, ALU.add, reads=[], writes=[PSK(bank), vkey])
            act(vbuf[:, 0:n], vbuf[:, 0:n], AF.Ln, [vkey], [vkey])
            act(vbuf[:, 0:n], vbuf[:, 0:n], AF.Exp, [vkey], [vkey], scale=-0.5)

        def dump(tag, tensor):
            if DEBUG == tag:
                P.barrier()
                P.dma("sp", dbg_d, tensor[:].rearrange("p a b -> p (a b)"), writes=["dbg"], chan="dbg")
                P.op("sp", None, reads=["dbg"])
                state["done"] = True
                if not state["inph"]:
                    raise _Stop()

        for i in range(NSLOT):
            w_issue_next()
        P.dma("pool", cb[:].rearrange("p a b -> p (a b)"), cb_d, writes=["cb"], chan="c0")
        P.dma("sp", identf[:], idf_d, writes=["identf"], chan="c1")
        P.op("pool", lambda e: e.memset(sc[:, SC_EPS6:SC_EPS6 + 1], 1e-6), writes=["sc"])
        P.op("pool", lambda e: e.memset(sc[:, SC_EPS5:SC_EPS5 + 1], 1e-5), writes=["sc"])
        with contextlib.ExitStack() as ph:
            vst = [T(ph, f"vst{i}", [128, 128], F32) for i in range(3)]
            lamb = T(ph, "lamb", [128, 256], F32)
            lamp = T(ph, "lamp", [128, 128], F32)
            lams = T(ph, "lams", [128, 2], F32)
            rows = [(0, 128), (128, 128), (256, NROWS - 256)]
            for i, (r0, n) in enumerate(rows):
                P.dma("sp", vst[i][0:n, :], vecs_d[r0:r0 + n, :], writes=[("vst", i)], chan="c1")
            P.dma("sp", lamb[:], lam_d.partition_broadcast(128), writes=["lamb"], chan="c1")
            b = banks.alloc()
            for i, (r0, n) in enumerate(rows):
                mm(ps[b][:, r0:r0 + n], vst[i][0:n, :], identf[0:n, 0:n], True, True,
                   reads=[("vst", i), "identf"], writes=[PSK(b)], skip=True)
            cp("dve", vecsT[:], ps[b][:, 0:NROWS], reads=[], writes=[PSK(b), "vecsT"])
            banks.free(b)
            ts("dve", sc[:, SC_QN:SC_QN + 1], vecsT[:, R_QN:R_QN + 1], 0.125, None, ALU.mult, None, ["vecsT"], ["sc"])
            cp("dve", sc[:, SC_KN:SC_KN + 1], vecsT[:, R_KN:R_KN + 1], ["vecsT"], ["sc"])
            ts("dve", sc[:, SC_SUB:SC_SUB + 1], vecsT[:, R_SUB:R_SUB + 1], 0.4, None, ALU.mult, None, ["vecsT"], ["sc"])
            cp("dve", sc[:, SC_XQ:SC_XQ + 2], vecsT[:, R_XQ:R_XQ + 2], ["vecsT"], ["sc"])
            ts("dve", sc[:, SC_XK:SC_XK + 2], vecsT[:, R_XK:R_XK + 2], 1.0 / 16.0, None, ALU.mult, None, ["vecsT"], ["sc"])
            ts("dve", sc[:, SC_LNGH:SC_LNGH + 8], vecsT[:, R_LNG:R_LNG + 8], 0.5, None, ALU.mult, None, ["vecsT"], ["sc"])
            ts("dve", sc[:, SC_LNBH:SC_LNBH + 8], vecsT[:, R_LNB:R_LNB + 8], 0.5, None, ALU.mult, None, ["vecsT"], ["sc"])
            tt("dve", lamp[:], lamb[:, 0:128], lamb[:, 128:256], ALU.mult, ["lamb"], ["lamp"])
            P.op("dve", lambda e: e.reduce_sum(out=lams[:], in_=lamp[:].rearrange("p (a b) -> p a b", a=2),
                                               axis=AX.X), ["lamp"], ["lams"])
            act(lams[:], lams[:], AF.Exp, ["lams"], ["lams"])
            tt("dve", lams[:, 0:1], lams[:, 1:2], lams[:, 0:1], ALU.subtract, ["lams"], ["lams"])
            ts("dve", sc[:, SC_NLAM:SC_NLAM + 1], lams[:, 0:1], -0.2, None, ALU.add, None, ["lams"], ["sc"])
            P.barrier()

        try:

            def branch_out(gp, gm, first, ph):
                thm = [T(ph, f"thm{i}", [128, 512], F32) for i in range(2)]
                tb_ = [T(ph, f"tbo{i}", [128, 512], F32) for i in range(2)]
                r_th = Ring(thm)
                r_t = Ring(tb_)
                for half in range(2):
                    sp_ = w_slot(gp[half])
                    sm_ = w_slot(gm[half])
                    for jj in range(4):
                        j = half * 4 + jj
                        for tb in range(NTB):
                            bp = banks.alloc()
                            for c in range(NCH):
                                mm(ps[bp][:, :], wsl[sp_][:, c, jj * 128:(jj + 1) * 128], actT[:, c, tb * 512:(tb + 1) * 512],
                                   c == 0, c == NCH - 1, reads=wkeys(sp_, jj * 128, 128) + [("act", c, tb)], writes=[PSK(bp)])
                            bm = banks.alloc()
                            rf, rk = hT_rhs(tb)
                            proj(bm, sm_, jj * 128, rf, rk)
                            k1, th = r_th.next()
                            act(th[:], ps[bm][:, :], AF.Tanh, [], [PSK(bm), ("thm", k1)], scale=0.5)
                            banks.free(bm)
                            k2, tbuf = r_t.next()
                            stt(tbuf[:], th[:], 1.0, ps[bp][:, :], ALU.add, ALU.mult, [("thm", k1)], [PSK(bp), ("tbo", k2)])
                            banks.free(bp)
                            ysl = yT[:, j, tb * 512:(tb + 1) * 512]
                            if first:
                                ts("dve", ysl, tbuf[:], 0.5, None, ALU.mult, None, [("tbo", k2)], [("y", j, tb)])
                            else:
                                stt(ysl, tbuf[:], 0.5, ysl, ALU.mult, ALU.add, [("tbo", k2)], [("y", j, tb)])
                    w_done(gp[half])
                    w_done(gm[half])

            with contextlib.ExitStack() as ph:
                ph_enter()
                xt = [T(ph, f"xt{i}", [128, D], F32) for i in range(3)]
                xn = [T(ph, f"xn{i}", [128, D], BF16) for i in range(3)]
                junk = T(ph, "junk", [128, D], BF16)
                gbc = T(ph, "gbc", [128, D], F32)
                gmbc = T(ph, "gmbc", [128, D], F32)
                ssq = T(ph, "ssq", [128, 32], F32)
                memT = T(ph, "memT", [128, NCH, MEM], BF16)
                P.dma("sp", gbc[:], ng_d.partition_broadcast(128), writes=["gbc"], chan="c1")
                P.dma("sp", gmbc[:], mg_d.partition_broadcast(128), writes=["gmbc"], chan="c1")
                tiles = [("m", i) for i in range(2)] + [("x", i) for i in range(NT)]
                for n_, (kind, t) in enumerate(tiles):
                    i = n_ % 3
                    src = (mem_d if kind == "m" else x_d)[t * 128:(t + 1) * 128, :]
                    P.dma("sp", xt[i][:], src, writes=[("xt", i)], chan=f"x{i}")
                    col = ssq[:, n_:n_ + 1]
                    act(junk[:], xt[i][:], AF.Square, [("xt", i)], ["junk"])
                    P.op("dve", (lambda e, col=col: e.reduce_sum(out=col, in_=junk[:], axis=AX.X)), ["junk"], [("ssq", n_)])
                    act(col, col, AF.Ln, [("ssq", n_), "sc"], [("ssq", n_)], scale=1.0 / D, bias=sc[:, SC_EPS6:SC_EPS6 + 1])
                    act(col, col, AF.Exp, [("ssq", n_)], [("ssq", n_)], scale=-0.5)
                    gsrc = gmbc if kind == "m" else gbc
                    stt(xn[i][:], xt[i][:], col, gsrc[:], ALU.mult, ALU.mult,
                        [("xt", i), ("ssq", n_), "gbc", "gmbc"], [("xn", i)])
                    b = banks.alloc()
                    pbf = ps[b][:].bitcast(BF16)
                    for kc in range(NCH):
                        P.op("pe", (lambda e, kc=kc, i=i, pbf=pbf: e.transpose(out=pbf[:, kc * 128:(kc + 1) * 128],
                                                                              in_=xn[i][:, kc * 128:(kc + 1) * 128],
                                                                              identity=ident)),
                             reads=[("xn", i), "cb"], writes=[PSK(b)])
                    src3 = pbf.rearrange("p (a b) -> p a b", a=NCH)
                    if kind == "m":
                        cp("dve", memT[:, :, t * 128:(t + 1) * 128], src3, [], [PSK(b), "memT"])
                    else:
                        cp("dve", hT[:, :, t * 128:(t + 1) * 128], src3, [], [PSK(b), ("hT", t // 4)])
                    banks.free(b)
                with contextlib.ExitStack() as ph:
                    ph_enter()
                    kT = T(ph, "kT", [128, NCH, MEM], BF16)
                    Vx = T(ph, "Vx", [128, 2, D], BF16)
                    qT = T(ph, "qT", [128, 2, S], BF16)
                    sqb = [T(ph, f"xsq{i}", [128, 512], BF16) for i in range(4)]
                    vb = [T(ph, f"xvb{i}", [128, 512], F32) for i in range(2)]
                    PT = [T(ph, f"xPT{i}", [128, 512], BF16) for i in range(4)]
                    thg = [T(ph, f"xthg{i}", [128, 512], F32) for i in range(2)]
                    gsb = [T(ph, f"xgs{i}", [128, 512], F32) for i in range(2)]
                    rlb = [T(ph, f"xrl{i}", [128, 512], F32) for i in range(2)]
                    tob = [T(ph, f"xto{i}", [128, 512], F32) for i in range(2)]
                    r_sq, r_vb, r_PT, r_thg, r_gs, r_rl, r_to = (Ring(sqb), Ring(vb), Ring(PT), Ring(thg), Ring(gsb),
                                                                 Ring(rlb), Ring(tob))
                    for hx in range(4):
                        bks = []
                        sqs = []
                        for dc in range(2):
                            c = hx * 2 + dc
                            g = g_kv[c // 4]
                            sl = w_slot(g)
                            b = banks.alloc()
                            proj(b, sl, (c % 4) * 128, lambda kc: memT[:, kc, :], ["memT"], n=MEM)
                            k_, sq = r_sq.next()
                            act(sq[:, 0:MEM], ps[b][:, 0:MEM], AF.Square, [], [PSK(b), ("xsq", k_)])
                            bks.append(b)
                            sqs.append((k_, sq))
                            if c == 3:
                                w_done(g_kv[0])
                            if c == 7:
                                w_done(g_kv[1])
                        bs = banks.alloc()
                        for dc in range(2):
                            mm(ps[bs][:, 0:MEM], ones, sqs[dc][1][:, 0:MEM], dc == 0, dc == 1, ["cb", ("xsq", sqs[dc][0])], [PSK(bs)])
                        kv_, v = r_vb.next()
                        rsqrt_from_psum(bs, MEM, 1.0 / 256.0, 1e-6, v, ("xvb", kv_))
                        banks.free(bs)
                        for dc in range(2):
                            c = hx * 2 + dc
                            stt(kT[:, c, :], ps[bks[dc]][:, 0:MEM], sc[:, SC_XK + dc:SC_XK + dc + 1], v[:, 0:MEM], ALU.mult, ALU.mult,
                                ["sc", ("xvb", kv_)], [PSK(bks[dc]), ("kT", c)])
                            banks.free(bks[dc])
                    for vg in range(2):
                        sl = w_slot(g_kv[2 + vg])
                        for mt in range(2):
                            b = banks.alloc()
                            for kc in range(NCH):
                                mm(ps[b][:, :], memT[:, kc, mt * 128:(mt + 1) * 128], wsl[sl][:, kc, :], kc == 0, kc == NCH - 1,
                                   ["memT"] + wkeys(sl, 0, 512), [PSK(b)])
                            cp("dve", Vx[:, mt, vg * 512:(vg + 1) * 512], ps[b][:, :], [], [PSK(b), ("Vx", mt, vg)])
                            banks.free(b)
                        w_done(g_kv[2 + vg])
                    for hx in range(4):
                        sl = w_slot(g_xh[hx])
                        pend = None

                        def q_finish(pend):
                            tb, bq, sqs = pend
                            bs = banks.alloc()
                            for dc in range(2):
                                mm(ps[bs][:, :], ones, sqs[dc][1][:], dc == 0, dc == 1, ["cb", ("xsq", sqs[dc][0])], [PSK(bs)])
                            kv_, v = r_vb.next()
                            rsqrt_from_psum(bs, 512, 1.0 / 256.0, 1e-6, v, ("xvb", kv_))
                            banks.free(bs)
                            for dc in range(2):
                                stt(qT[:, dc, tb * 512:(tb + 1) * 512], ps[bq[dc]][:, :], sc[:, SC_XQ + dc:SC_XQ + dc + 1], v[:],
                                    ALU.mult, ALU.mult, ["sc", ("xvb", kv_)], [PSK(bq[dc]), ("qT", dc, tb)])
                                banks.free(bq[dc])

                        for tb in range(NTB):
                            bq = []
                            sqs = []
                            rf, rk = hT_rhs(tb)
                            for dc in range(2):
                                b = banks.alloc()
                                proj(b, sl, dc * 128, rf, rk)
                                k_, sq = r_sq.next()
                                act(sq[:], ps[b][:, :], AF.Square, [], [PSK(b), ("xsq", k_)])
                                bq.append(b)
                                sqs.append((k_, sq))
                            if pend is not None:
                                q_finish(pend)
                            pend = (tb, bq, sqs)
                        q_finish(pend)
                        for tb in range(NTB):
                            pts = []
                            for mt in range(2):
                                b = banks.alloc()
                                for dc in range(2):
                                    mm(ps[b][:, :], kT[:, hx * 2 + dc, mt * 128:(mt + 1) * 128], qT[:, dc, tb * 512:(tb + 1) * 512],
                                       dc == 0, dc == 1, [("kT", hx * 2 + dc), ("qT", dc, tb)], [PSK(b)])
                                kp, pt = r_PT.next()
                                act(pt[:], ps[b][:, :], AF.Exp, [], [PSK(b), ("xPT", kp)])
                                banks.free(b)
                                pts.append((kp, pt))
                            bo = []
                            for vc in range(2):
                                b = banks.alloc()
                                for mt in range(2):
                                    c0 = hx * 256 + vc * 128
                                    mm(ps[b][:, :], Vx[:, mt, c0:c0 + 128], pts[mt][1][:], mt == 0, mt == 1,
                                       [("Vx", mt, c0 // 512), ("xPT", pts[mt][0])], [PSK(b)])
                                bo.append(b)
                            bl = banks.alloc()
                            for mt in range(2):
                                mm(ps[bl][:, :], ones, pts[mt][1][:], mt == 0, mt == 1, ["cb", ("xPT", pts[mt][0])], [PSK(bl)])
                            bg = []
                            rf, rk = hT_rhs(tb)
                            for vc in range(2):
                                b = banks.alloc()
                                proj(b, sl, 256 + vc * 128, rf, rk)
                                bg.append(b)
                            kr, rl = r_rl.next()
                            act(rl[:], ps[bl][:, :], AF.Ln, [], [PSK(bl), ("xrl", kr)])
                            banks.free(bl)
                            act(rl[:], rl[:], AF.Exp, [("xrl", kr)], [("xrl", kr)], scale=-1.0)
                            for vc in range(2):
                                kt_, th = r_thg.next()
                                act(th[:], ps[bg[vc]][:, :], AF.Tanh, [], [PSK(bg[vc]), ("xthg", kt_)], scale=0.5)
                                kg, gs = r_gs.next()
                                stt(gs[:], th[:], 1.0, ps[bg[vc]][:, :], ALU.add, ALU.mult, [("xthg", kt_)], [PSK(bg[vc]), ("xgs", kg)])
                                banks.free(bg[vc])
                                ko, to = r_to.next()
                                tt("dve", to[:], ps[bo[vc]][:, :], rl[:], ALU.mult, [("xrl", kr)], [PSK(bo[vc]), ("xto", ko)])
                                banks.free(bo[vc])
                                stt(actT[:, hx * 2 + vc, tb * 512:(tb + 1) * 512], to[:], 0.5, gs[:], ALU.mult, ALU.mult,
                                    [("xto", ko), ("xgs", kg)], [("act", hx * 2 + vc, tb)])
                        w_done(g_xh[hx])
                    dump("actx", actT)
                    if not state["done"]:
                        branch_out(g_xp, g_m2, True, ph)
                    P.barrier()
            ph_exit()
            dump("yx", yT)

            with contextlib.ExitStack() as ph:
                ph_enter()
                PADW = 30
                ub = [T(ph, f"ub{i}", [128, PADW + S], BF16) for i in range(2)]
                dg = [T(ph, f"dg{i}", [128, 31, 128], BF16) for i in range(2)]
                thb = [T(ph, f"cth{i}", [128, 512], F32) for i in range(2)]
                Ms = [T(ph, f"cM{i}", [128, 512], F32) for i in range(NTB)]
                Rs = [T(ph, f"cR{i}", [128, 512], F32) for i in range(NTB)]
                zt = [T(ph, f"cz{i}", [128, 512], F32) for i in range(2)]
                sqc = [T(ph, f"csq{i}", [128, 512], BF16) for i in range(2)]
                gsc = [T(ph, f"cgs{i}", [128, 512], F32) for i in range(2)]
                r_th, r_z, r_sq, r_gs = Ring(thb), Ring(zt), Ring(sqc), Ring(gsc)
                for i in range(2):
                    P.op("pool", (lambda e, i=i: e.memset(ub[i][:, 0:PADW], 0.0)), writes=[("ub", i, -1)])
                for j in range(NCH):
                    g = g_cab[j // 2]
                    sl = w_slot(g)
                    jl = j % 2
                    ui = j % 2
                    for k in range(31):
                        col = vecsT[:, R_DW + k * 8 + j:R_DW + k * 8 + j + 1]
                        P.op("pool", (lambda e, k=k, ui=ui, col=col: e.tensor_scalar(out=dg[ui][:, k, :], in0=ident, scalar1=col,
                                                                                     scalar2=0.5, op0=ALU.mult, op1=ALU.mult)),
                             reads=["cb", "vecsT"], writes=[("dg", ui)])
                    pend = None

                    def conv_block(tb, ui=ui, j=j):
                        b = banks.alloc()
                        rk = [("ub", ui, tb), ("ub", ui, tb - 1), ("dg", ui)]
                        for k in range(31):
                            mm(ps[b][:, :], dg[ui][:, k, :], ub[ui][:, tb * 512 + k:tb * 512 + k + 512], k == 0, k == 30, rk, [PSK(b)])
                        ts("dve", actT[:, j, tb * 512:(tb + 1) * 512], ps[b][:, :], vecsT[:, R_DWB + j:R_DWB + j + 1], None, ALU.add, None,
                           ["vecsT"], [PSK(b), ("act", j, tb)])
                        banks.free(b)

                    for tb in range(NTB):
                        rf, rk = hT_rhs(tb)
                        ba = banks.alloc()
                        proj(ba, sl, jl * 128, rf, rk)
                        bb = banks.alloc()
                        proj(bb, sl, 256 + jl * 128, rf, rk)
                        kt_, th = r_th.next()
                        act(th[:], ps[bb][:, :], AF.Tanh, [], [PSK(bb), ("cth", kt_)], scale=0.5)
                        banks.free(bb)
                        stt(ub[ui][:, PADW + tb * 512:PADW + (tb + 1) * 512], th[:], 1.0, ps[ba][:, :], ALU.add, ALU.mult,
                            [("cth", kt_)], [PSK(ba), ("ub", ui, tb)])
                        banks.free(ba)
                        if pend is not None:
                            conv_block(pend)
                        pend = tb
                    conv_block(pend)
                    if jl == 1:
                        w_done(g)
                dump("conv", actT)
                for tb in range(NTB):
                    bs = banks.alloc()
                    bq = banks.alloc()
                    for j in range(NCH):
                        a = actT[:, j, tb * 512:(tb + 1) * 512]
                        mm(ps[bs][:, :], ones, a, j == 0, j == NCH - 1, ["cb", ("act", j, tb)], [PSK(bs)])
                        ks, sq = r_sq.next()
                        act(sq[:], a, AF.Square, [("act", j, tb)], [("csq", ks)])
                        mm(ps[bq][:, :], ones, sq[:], j == 0, j == NCH - 1, ["cb", ("csq", ks)], [PSK(bq)])
                    M, R = Ms[tb], Rs[tb]
                    ts("dve", M[:], ps[bs][:, :], 1.0 / D, None, ALU.mult, None, [], [PSK(bs), ("cM", tb)])
                    banks.free(bs)
                    tt("dve", R[:], M[:], M[:], ALU.mult, [("cM", tb)], [("cR", tb)])
                    stt(R[:], ps[bq][:, :], 1.0 / D, R[:], ALU.mult, ALU.subtract, [], [PSK(bq), ("cR", tb)])
                    banks.free(bq)
                    act(R[:], R[:], AF.Ln, [("cR", tb), "sc"], [("cR", tb)], scale=1.0, bias=sc[:, SC_EPS5:SC_EPS5 + 1])
                    act(R[:], R[:], AF.Exp, [("cR", tb)], [("cR", tb)], scale=-0.5)
                    stt(M[:], M[:], -1.0, R[:], ALU.mult, ALU.mult, [("cR", tb)], [("cM", tb)])
                for half in range(2):
                    slg = w_slot(g_cg[half])
                    for jj in range(4):
                        j = half * 4 + jj
                        for tb in range(NTB):
                            a = actT[:, j, tb * 512:(tb + 1) * 512]
                            kz, z = r_z.next()
                            tt("dve", z[:], a, Rs[tb][:], ALU.mult, [("act", j, tb), ("cR", tb)], [("cz", kz)])
                            tt("dve", z[:], z[:], Ms[tb][:], ALU.add, [("cM", tb)], [("cz", kz)])
                            kt_, th = r_th.next()
                            act(th[:], z[:], AF.Tanh, [("cz", kz), "sc"], [("cth", kt_)],
                                scale=sc[:, SC_LNGH + j:SC_LNGH + j + 1], bias=sc[:, SC_LNBH + j:SC_LNBH + j + 1])
                            ts("dve", z[:], z[:], vecsT[:, R_LNG + j:R_LNG + j + 1], vecsT[:, R_LNB + j:R_LNB + j + 1], ALU.mult, ALU.add,
                               ["vecsT"], [("cz", kz)])
                            stt(z[:], th[:], 1.0, z[:], ALU.add, ALU.mult, [("cth", kt_)], [("cz", kz)])
                            bg = banks.alloc()
                            rf, rk = hT_rhs(tb)
                            proj(bg, slg, jj * 128, rf, rk)
                            kt2, th2 = r_th.next()
                            act(th2[:], ps[bg][:, :], AF.Tanh, [], [PSK(bg), ("cth", kt2)], scale=0.5)
                            kg, gs = r_gs.next()
                            stt(gs[:], th2[:], 1.0, ps[bg][:, :], ALU.add, ALU.mult, [("cth", kt2)], [PSK(bg), ("cgs", kg)])
                            banks.free(bg)
                            stt(a, z[:], 0.25, gs[:], ALU.mult, ALU.mult, [("cz", kz), ("cgs", kg)], [("act", j, tb)])
                    w_done(g_cg[half])
                dump("actc", actT)
                if not state["done"]:
                    branch_out(g_cp, g_m0, False, ph)
                P.barrier()
            ph_exit()
            dump("yc", yT)

            with contextlib.ExitStack() as ph:
                ph_enter()
                QL = [T(ph, f"QL{i}", [128, S], BF16) for i in range(2)]
                QU = [T(ph, f"QU{i}", [128, S], BF16) for i in range(2)]
                KL = [T(ph, f"KL{i}", [128, S], BF16) for i in range(2)]
                KU = [T(ph, f"KU{i}", [128, S], BF16) for i in range(2)]
                Vd = [T(ph, f"Vd{i}", [128, NT, 128], BF16) for i in range(2)]
                gsd = [T(ph, f"gsd{i}", [128, S], BF16) for i in range(2)]
                PTd = [[T(ph, f"PT{m}_{i}", [128, 512], BF16) for i in range(3)] for m in range(2)]
                sqd = [T(ph, f"dsq{i}", [128, 512], BF16) for i in range(2)]
                thd = [T(ph, f"dth{i}", [128, 512], F32) for i in range(1)]
                vbd = [T(ph, f"dvb{i}", [128, 512], F32) for i in range(2)]
                E1 = [T(ph, f"E1_{i}", [128, 512], F32) for i in range(2)]
                E2 = [T(ph, f"E2_{i}", [128, 512], F32) for i in range(2)]
                sqe = [T(ph, f"sqe{i}", [128, 512], BF16) for i in range(2)]
                RL = [T(ph, f"RL{i}", [128, 512], F32) for i in range(2)]
                r_PT = [Ring(PTd[0]), Ring(PTd[1])]
                rawd = [T(ph, f"draw{i}", [128, 512], F32) for i in range(3)]
                r_raw = Ring(rawd)
                r_sq, r_th, r_vb = Ring(sqd), Ring(thd), Ring(vbd)
                for i in range(2):
                    P.op("pool", (lambda e, i=i: e.memset(QU[i][0:64, :], 0.0)), writes=[("QUa", i)])
                    P.op("pool", (lambda e, i=i: e.memset(KU[i][0:64, :], 0.0)), writes=[("KUa", i)])
                O_B = [4, 6]
                L_B = [5, 7]
                for b in (4, 5, 6, 7):
                    banks.busy.add(b)
                dbanks = Banks([0, 1, 2, 3])

                def qk_unit(h, which, tb):
                    s_ = h % 2
                    st_ = {}
                    c0 = 0 if which == "q" else 128
                    gcol = sc[:, SC_QN:SC_QN + 1] if which == "q" else sc[:, SC_KN:SC_KN + 1]

                    def st0():
                        sl = w_slot(g_dh[h])
                        rf, rk = hT_rhs(tb)
                        b = dbanks.alloc()
                        proj(b, sl, c0, rf, rk)
                        st_["kraw"], st_["raw"] = r_raw.next()
                        cp("dve", st_["raw"][:], ps[b][:, :], [], [PSK(b), ("draw", st_["kraw"])])
                        dbanks.free(b)
                        st_["ks"], sq = r_sq.next()
                        tt("dve", sq[:], st_["raw"][:], st_["raw"][:], ALU.mult, [("draw", st_["kraw"])], [("dsq", st_["ks"])])

                    def st1():
                        bs = dbanks.alloc()
                        mm(ps[bs][:, :], bd64, sqd[st_["ks"]][:], True, True, ["cb", ("dsq", st_["ks"])], [PSK(bs)])
                        st_["kv"], st_["v"] = r_vb.next()
                        ts("dve", st_["v"][:], ps[bs][:, :], 1.0 / 64.0, 1e-6, ALU.mult, ALU.add, [], [PSK(bs), ("dvb", st_["kv"])])
                        dbanks.free(bs)

                    def st2():
                        v, vk = st_["v"], ("dvb", st_["kv"])
                        act(v[:], v[:], AF.Ln, [vk], [vk])
                        act(v[:], v[:], AF.Exp, [vk], [vk], scale=-0.5)

                    def st3():
                        v, vk = st_["v"], ("dvb", st_["kv"])
                        raw, rk_ = st_["raw"], ("draw", st_["kraw"])
                        lo = (QL if which == "q" else KL)[s_]
                        up = (QU if which == "q" else KU)[s_]
                        sl_ = slice(tb * 512, (tb + 1) * 512)
                        stt(lo[0:64, sl_], raw[0:64, :], gcol[0:64, :], v[0:64, :], ALU.mult, ALU.mult,
                            ["sc", vk, rk_], [(which + "L", s_, tb)])
                        stt(up[64:128, sl_], raw[64:128, :], gcol[64:128, :], v[64:128, :], ALU.mult, ALU.mult,
                            ["sc", vk, rk_], [(which + "U", s_, tb)])

                    return [st0, st1, st2, st3]

                def v_unit(h, g4):
                    s_ = h % 2

                    def st0():
                        sl = w_slot(g_dh[h])
                        b = dbanks.alloc()
                        for tl in range(4):
                            t = g4 * 4 + tl
                            for kc in range(NCH):
                                mm(ps[b][:, tl * 128:(tl + 1) * 128], hT[:, kc, t * 128:(t + 1) * 128], wsl[sl][:, kc, 256:384],
                                   kc == 0, kc == NCH - 1, [("hT", g4)] + wkeys(sl, 256, 128), [PSK(b)], skip=True)
                        cp("dve", Vd[s_][:, g4 * 4:(g4 + 1) * 4, :], ps[b][:, :].rearrange("p (a b) -> p a b", a=4), [],
                           [PSK(b), ("Vd", s_, g4)])
                        dbanks.free(b)

                    return [st0]

                def g_unit(h, tb, last):
                    s_ = h % 2
                    st_ = {}

                    def st0():
                        sl = w_slot(g_dh[h])
                        rf, rk = hT_rhs(tb)
                        b = dbanks.alloc()
                        proj(b, sl, 384, rf, rk)
                        st_["kraw"], st_["raw"] = r_raw.next()
                        cp("dve", st_["raw"][:], ps[b][:, :], [], [PSK(b), ("draw", st_["kraw"])])
                        dbanks.free(b)
                        if last:
                            w_done(g_dh[h])

                    def st1():
                        st_["kt"], st_["th"] = r_th.next()
                        act(st_["th"][:], st_["raw"][:], AF.Tanh, [("draw", st_["kraw"])], [("dth", st_["kt"])], scale=0.5)

                    def st2():
                        stt(gsd[s_][:, tb * 512:(tb + 1) * 512], st_["th"][:], 1.0, st_["raw"][:], ALU.add, ALU.mult,
                            [("dth", st_["kt"]), ("draw", st_["kraw"])], [("gsd", s_, tb)])

                    return [st0, st1, st2]

                class Prologue:
                    def __init__(self, h, period=2):
                        self.h = h
                        s_ = h % 2
                        self.units = ([qk_unit(h, "q", tb) for tb in range(NTB)] + [qk_unit(h, "k", tb) for tb in range(NTB)]
                                      + [v_unit(h, g4) for g4 in range(4)] + [g_unit(h, tb, tb == NTB - 1) for tb in range(NTB)])
                        self.active = []
                        self.t = 0
                        self.period = period
                        self.started = False

                    def tick(self):
                        h = self.h
                        s_ = h % 2
                        if not self.started:
                            self.started = True
                            P.dma("pool", QL[s_][64:68, :], qaug_d[h], writes=[("QLa", s_)], chan=f"aug{s_}")
                            P.dma("pool", QU[s_][0:4, :], qaug_d[h], writes=[("QUa", s_)], chan=f"aug{s_}")
                            P.dma("pool", KL[s_][64:68, :], kaug_d[h], writes=[("KLa", s_)], chan=f"aug{s_}")
                            P.dma("pool", KU[s_][0:4, :], kaug_d[h], writes=[("KUa", s_)], chan=f"aug{s_}")
                        if self.t % self.period == 0 and self.units:
                            self.active.append(self.units.pop(0))
                        self.t += 1
                        for u in list(self.active):
                            u.pop(0)()
                            if not u:
                                self.active.remove(u)

                    def done(self):
                        return not self.units and not self.active

                    def flush(self):
                        while not self.done():
                            self.tick()

                def scores(h, qb, kt):
                    s_ = h % 2
                    r = kt - 4 * qb
                    c0 = 128 * r if r > 0 else 0
                    n = 512 - c0
                    outp = []
                    for m in range(2):
                        b = dbanks.alloc()
                        if m == 0:
                            lhsT = KL[s_][0:68, kt * 128:(kt + 1) * 128]
                            rhs = QL[s_][0:68, qb * 512 + c0:(qb + 1) * 512]
                            rk = [("kL", s_, kt // 4), ("KLa", s_), ("qL", s_, qb), ("QLa", s_)]
                        else:
                            lhsT = KU[s_][:, kt * 128:(kt + 1) * 128]
                            rhs = QU[s_][:, qb * 512 + c0:(qb + 1) * 512]
                            rk = [("kU", s_, kt // 4), ("KUa", s_), ("qU", s_, qb), ("QUa", s_)]
                        mm(ps[b][:, c0:512], lhsT, rhs, True, r < 0, rk, [PSK(b)], skip=True)
                        if r >= 0:
                            mm(ps[b][:, c0:c0 + 128], ident, negmask, False, True, ["cb"], [PSK(b)], skip=True)
                        kp, pt = r_PT[m].next()
                        act(pt[:, c0:512], ps[b][:, c0:512], AF.Exp, [], [PSK(b), ("PT", m, kp)])
                        dbanks.free(b)
                        outp.append((kp, pt))
                    return (kt, c0, outp)

                def av(h, qb, sc_, first, last):
                    s_ = h % 2
                    kt, c0, outp = sc_
                    for m in range(2):
                        kp, pt = outp[m]
                        mm(ps[O_B[m]][:, c0:512], Vd[s_][:, kt, :], pt[:, c0:512], first, last,
                           [("Vd", s_, kt // 4), ("PT", m, kp)], [PSK(O_B[m])], skip=True)
                        mm(ps[L_B[m]][:, c0:512], ones, pt[:, c0:512], first, last,
                           ["cb", ("PT", m, kp)], [PSK(L_B[m])], skip=True)

                def epiA(h, qb, e):
                    e1, e2 = E1[e], E2[e]
                    cp("dve", e1[:], ps[O_B[0]][:, :], [], [PSK(O_B[0]), ("E1", e)])
                    act(RL[0][:], ps[L_B[0]][:, :], AF.Ln, [], [PSK(L_B[0]), ("RL", 0)])
                    cp("dve", e2[:], ps[O_B[1]][:, :], [], [PSK(O_B[1]), ("E2", e)])
                    act(RL[1][:], ps[L_B[1]][:, :], AF.Ln, [], [PSK(L_B[1]), ("RL", 1)])

                def epiB(h, qb, e):
                    e1, e2 = E1[e], E2[e]
                    act(RL[0][:], RL[0][:], AF.Exp, [("RL", 0)], [("RL", 0)], scale=-1.0)
                    act(RL[1][:], RL[1][:], AF.Exp, [("RL", 1)], [("RL", 1)], scale=-1.0)
                    tt("dve", e1[:], e1[:], RL[0][:], ALU.mult, [("RL", 0)], [("E1", e)])
                    tt("dve", e2[:], e2[:], RL[1][:], ALU.mult, [("RL", 1)], [("E2", e)])
                    stt(e1[:], e2[:], sc[:, SC_NLAM:SC_NLAM + 1], e1[:], ALU.mult, ALU.add, ["sc", ("E2", e)], [("E1", e)])

                def epiC(h, qb, e):
                    tt("dve", sqe[e][:], E1[e][:], E1[e][:], ALU.mult, [("E1", e)], [("sqe", e)])

                def epiD(h, qb, e):
                    s_ = h % 2
                    e1, e2 = E1[e], E2[e]
                    bs = dbanks.alloc()
                    mm(ps[bs][:, :], ones, sqe[e][:], True, True, ["cb", ("sqe", e)], [PSK(bs)])
                    rsqrt_dve_evac(bs, 512, 1.0 / 128.0, 1e-6, e2, ("E2", e))
                    dbanks.free(bs)
                    tt("dve", e1[:], e1[:], e2[:], ALU.mult, [("E2", e)], [("E1", e)])
                    stt(actT[:, h, qb * 512:(qb + 1) * 512], e1[:], sc[:, SC_SUB:SC_SUB + 1], gsd[s_][:, qb * 512:(qb + 1) * 512],
                        ALU.mult, ALU.mult, ["sc", ("E1", e), ("gsd", s_, qb)], [("act", h, qb)])

                ecount = 0
                pend = []
                pro = Prologue(0)
                pro.flush()
                for h in range(8):
                    pro = Prologue(h + 1) if h + 1 < 8 else None
                    for qb in range(NTB):
                        kts = list(range(4 * qb + 4))
                        cur = scores(h, qb, kts[0])
                        for i, kt in enumerate(kts):
                            nxt = scores(h, qb, kts[i + 1]) if i + 1 < len(kts) else None
                            av(h, qb, cur, i == 0, i == len(kts) - 1)
                            cur = nxt
                            if pend:
                                pend.pop(0)()
                            if pro is not None and not pro.done():
                                pro.tick()
                        assert not pend
                        e = ecount % 2
                        ecount += 1
                        epiA(h, qb, e)
                        pend = [(lambda h=h, qb=qb, e=e: epiB(h, qb, e)), (lambda h=h, qb=qb, e=e: epiC(h, qb, e)),
                                (lambda h=h, qb=qb, e=e: epiD(h, qb, e))]
                    if pro is not None:
                        pro.flush()
                while pend:
                    pend.pop(0)()
                for b in (4, 5, 6, 7):
                    banks.free(b)
                dump("actd", actT)
                P.barrier()
            ph_exit()
            with contextlib.ExitStack() as ph:
                ph_enter()
                branch_out(g_dp, g_m1, False, ph)
                P.barrier()

            ph_exit()
            with contextlib.ExitStack() as ph:
                ph_enter()
                xt = [T(ph, f"fx{i}", [128, D], F32) for i in range(2)]
                ot = [T(ph, f"fo{i}", [128, D], F32) for i in range(2)]
                s0 = w_slot(g_out[0])
                s1 = w_slot(g_out[1])
                for t in range(NT):
                    i = t % 2
                    P.dma("sp", xt[i][:], x_d[t * 128:(t + 1) * 128, :], writes=[("fx", i)], chan=f"x{i}")
                    for half in range(2):
                        sl = (s0, s1)[half]
                        b = banks.alloc()
                        for dc in range(NCH):
                            mm(ps[b][:, :], yT[:, dc, t * 128:(t + 1) * 128], wsl[sl][:, dc, :], dc == 0, dc == NCH - 1,
                               [("y", dc, t // 4)] + wkeys(sl, 0, 512), [PSK(b)])
                        tt("dve", ot[i][:, half * 512:(half + 1) * 512], ps[b][:, :], xt[i][:, half * 512:(half + 1) * 512], ALU.add,
                           [("fx", i)], [PSK(b), ("fo", i, half)])
                        banks.free(b)
                    P.dma("sp", out_d[t * 128:(t + 1) * 128, :], ot[i][:], reads=[("fo", i, 0), ("fo", i, 1)],
                          writes=[("fo_st", i)], chan=f"o{i}")
                P.op("sp", None, reads=[("fo_st", 0), ("fo_st", 1)])
            ph_exit()
        except _Stop:
            pass
        except Exception:
            import traceback
            traceback.print_exc()
            raise
        P.emit(st)
    return nc


def _host_constants():
    ident = np.eye(128, dtype=np.float32)
    ones = np.ones((128, 128), np.float32)
    p = np.arange(128)
    bd64 = (p[:, None] // 64 == p[None, :] // 64).astype(np.float32)
    negmask = np.where(p[None, :] < p[:, None], -30000.0, 0.0).astype(np.float32)
    cb = np.concatenate([ident, ones, bd64, negmask], axis=1)
    tok = np.arange(S)
    il = (tok % 128).astype(np.float32)
    ib = (tok // 128).astype(np.float32)
    qaug = np.zeros((8, 4, S), np.float32)
    kaug = np.zeros((8, 4, S), np.float32)
    for h in range(8):
        slope = 2.0 ** (-(h + 1))
        qaug[h, 0] = 1.0
        qaug[h, 1] = -slope * il
        qaug[h, 2] = 1.0
        qaug[h, 3] = -slope * 128.0 * ib
        kaug[h, 0] = slope * il
        kaug[h, 1] = 1.0
        kaug[h, 2] = slope * 128.0 * ib
        kaug[h, 3] = 1.0
    return cb, ident, qaug, kaug


_NC_CACHE = {}


def kernel(x, mem, norm_g, mem_norm_g, w_in, conv_dw, conv_dw_b, conv_ln_g, conv_ln_b, w_conv_proj,
           diff_qn_g, diff_kn_g, lambda_q1, lambda_k1, lambda_q2, lambda_k2, diff_subln_g, w_diff_proj,
           w_mem_kv, x_qn_g, x_kn_g, w_x_proj, w_out):
    f = lambda a: np.ascontiguousarray(np.asarray(a, dtype=np.float32))
    x = f(x); mem = f(mem)
    B = x.shape[0]
    vecs = np.concatenate([
        f(conv_dw_b)[0].reshape(8, 128), f(conv_ln_g)[0].reshape(8, 128), f(conv_ln_b)[0].reshape(8, 128),
        f(conv_dw)[0].reshape(31 * 8, 128),
        np.concatenate([f(diff_qn_g)[0], f(diff_qn_g)[0]])[None, :],
        np.concatenate([f(diff_kn_g)[0], f(diff_kn_g)[0]])[None, :],
        f(diff_subln_g)[0][None, :],
        f(x_qn_g)[0].reshape(2, 128), f(x_kn_g)[0].reshape(2, 128)], axis=0)
    assert vecs.shape == (NROWS, 128)
    lamv = np.concatenate([f(lambda_q1)[0], f(lambda_q2)[0], f(lambda_k1)[0], f(lambda_k2)[0]])[None, :]
    cb, identf, qaug, kaug = _host_constants()
    shared = {
        "w_in": f(w_in)[0], "w_conv_proj": f(w_conv_proj)[0], "w_diff_proj": f(w_diff_proj)[0],
        "w_x_proj": f(w_x_proj)[0], "w_out": f(w_out)[0], "w_mem_kv": f(w_mem_kv)[0],
        "norm_g": f(norm_g), "mem_norm_g": f(mem_norm_g), "vecs": np.ascontiguousarray(vecs),
        "lamv": np.ascontiguousarray(lamv), "cbits": cb, "identf": identf, "qaug": qaug, "kaug": kaug,
    }
    if "nc" not in _NC_CACHE:
        _NC_CACHE["nc"] = build_program()
    nc = _NC_CACHE["nc"]
    in_maps = [dict(shared, x=x[b], mem=mem[b]) for b in range(B)]
    res = run_bass_kernel_spmd(nc, in_maps, core_ids=list(range(B)))
    out = np.stack([np.asarray(r["out"]) for r in res.results], axis=0).astype(np.float32)
    if DEBUG is not None:
        kernel.dbg = [np.asarray(r["dbg"]) for r in res.results]
    return out
```

```python
import contextlib
import numpy as np
import concourse.bass as bass
import concourse.mybir as mybir
from concourse.bass_utils import run_bass_kernel_spmd

F32 = mybir.dt.float32
BF16 = mybir.dt.bfloat16
AF = mybir.ActivationFunctionType
ALU = mybir.AluOpType
AX = mybir.AxisListType

D = 1024
S = 2048
MEM = 256
NCH = 8
NTB = 4
NT = 16
IN_COLS = 12288
C_GLU, C_GATE, D_Q, D_K, D_V, D_GATE, X_Q, X_GATE, MERGE = 0, 2048, 3072, 4096, 5120, 6144, 7168, 8192, 9216
NSLOT = 3
ENGINES = ("pe", "act", "dve", "pool", "sp")

R_DWB, R_LNG, R_LNB, R_DW, R_QN, R_KN, R_SUB, R_XQ, R_XK, NROWS = 0, 8, 16, 24, 272, 273, 274, 275, 277, 279

DEBUG = None


class _Stop(Exception):
    pass


class Prog:
    def __init__(self, nc):
        self.nc = nc
        self.ins = []
        self.last_w = {}
        self.readers = {}
        self.chan_count = {}
        self.last_on = {}

    def _add(self, eng, fn, reads, writes, dma_chan=None, extra_deps=()):
        idx = len(self.ins)
        deps = set(extra_deps)
        for r in reads:
            w = self.last_w.get(r)
            if w is not None:
                deps.add(w)
            if fn is not None:
                self.readers.setdefault(r, []).append(idx)
        for w_ in writes:
            w = self.last_w.get(w_)
            if w is not None:
                deps.add(w)
            for rd in self.readers.get(w_, ()):
                if rd != idx:
                    deps.add(rd)
            self.last_w[w_] = idx
            self.readers[w_] = []
        rec = dict(eng=eng, fn=fn, dma_chan=dma_chan, mark=False)
        waits = []
        for d in deps:
            dr = self.ins[d]
            if dr["dma_chan"] is not None:
                waits.append(("dma", dr["dma_chan"], self.chan_count[dr["dma_chan"]]))
            else:
                if dr["eng"] == "pe" and eng == "pe":
                    continue
                dr["mark"] = True
                waits.append(("eng", dr["eng"], d))
        rec["waits"] = waits
        if dma_chan is not None:
            self.chan_count[dma_chan] = self.chan_count.get(dma_chan, 0) + 16
        elif fn is not None:
            self.last_on[eng] = idx
        self.ins.append(rec)
        return idx

    def op(self, eng, fn, reads=(), writes=()):
        return self._add(eng, fn, tuple(reads), tuple(writes))

    def dma(self, eng, out, in_, reads=(), writes=(), chan=None):
        return self._add(eng, lambda e: e.dma_start(out=out, in_=in_), tuple(reads), tuple(writes),
                         dma_chan=chan)

    def barrier(self):
        lasts = [v for v in self.last_on.values()]
        for e in ENGINES:
            idx = self._add(e, None, (), (), extra_deps=lasts)
            rec = self.ins[idx]
            for c, v in self.chan_count.items():
                if str(c).startswith("w") or str(c).startswith("aug"):
                    continue
                rec["waits"].append(("dma", c, v))

    def emit(self, stack):
        nc = self.nc
        sems = {e: stack.enter_context(nc.semaphore("s_" + e)) for e in ENGINES}
        csems = {c: stack.enter_context(nc.semaphore("c_" + str(c))) for c in self.chan_count}
        cnt = {e: 0 for e in ENGINES}
        for r in self.ins:
            if r["dma_chan"] is None and r["mark"]:
                cnt[r["eng"]] += 1
                r["ord"] = cnt[r["eng"]]
        per = {e: [] for e in ENGINES}
        for r in self.ins:
            per[r["eng"]].append(r)
        block = stack.enter_context(nc.Block())
        ins = self.ins

        def run(engname, eng):
            waited = {}
            for r in per[engname]:
                need = {}
                for w in r["waits"]:
                    if w[0] == "dma":
                        key = ("c", w[1]); val = w[2]
                    else:
                        key = ("e", w[1]); val = ins[w[2]]["ord"]
                    if val > need.get(key, 0):
                        need[key] = val
                for key, val in need.items():
                    if waited.get(key, 0) >= val:
                        continue
                    waited[key] = val
                    eng.wait_ge(csems[key[1]] if key[0] == "c" else sems[key[1]], val)
                if r["fn"] is None:
                    continue
                bi = r["fn"](eng)
                if r["dma_chan"] is not None:
                    bi.then_inc(csems[r["dma_chan"]], 16)
                elif r["mark"]:
                    bi.then_inc(sems[engname], 1)

        block.tensor(lambda e: run("pe", e))
        block.scalar(lambda e: run("act", e))
        block.vector(lambda e: run("dve", e))
        block.gpsimd(lambda e: run("pool", e))
        block.sync(lambda e: run("sp", e))


class Banks:
    def __init__(self, ids):
        self.ids = list(ids)
        self.busy = set()
        self.ptr = 0

    def alloc(self):
        n = len(self.ids)
        for k in range(n):
            b = self.ids[(self.ptr + k) % n]
            if b not in self.busy:
                self.busy.add(b)
                self.ptr = (self.ptr + k + 1) % n
                return b
        raise RuntimeError("out of PSUM banks")

    def free(self, b):
        self.busy.discard(b)


class Ring:
    def __init__(self, items):
        self.items = items
        self.i = 0

    def next(self):
        r = self.items[self.i % len(self.items)]
        k = self.i % len(self.items)
        self.i += 1
        return k, r


def build_program():
    nc = bass.Bass("TRN2", target_bir_lowering=False)
    dt_in = lambda name, shape: nc.dram_tensor(name, list(shape), F32, kind="ExternalInput").ap()
    x_d = dt_in("x", [S, D])
    mem_d = dt_in("mem", [MEM, D])
    w_in_d = dt_in("w_in", [D, IN_COLS])
    wcp_d = dt_in("w_conv_proj", [D, D])
    wdp_d = dt_in("w_diff_proj", [D, D])
    wxp_d = dt_in("w_x_proj", [D, D])
    wout_d = dt_in("w_out", [D, D])
    wkv_d = dt_in("w_mem_kv", [D, 2 * D])
    ng_d = dt_in("norm_g", [1, D])
    mg_d = dt_in("mem_norm_g", [1, D])
    vecs_d = dt_in("vecs", [NROWS, 128])
    lam_d = dt_in("lamv", [1, 256])
    cb_d = dt_in("cbits", [128, 4 * 128])
    idf_d = dt_in("identf", [128, 128])
    qaug_d = dt_in("qaug", [8, 4, S])
    kaug_d = dt_in("kaug", [8, 4, S])
    out_d = nc.dram_tensor("out", [S, D], F32, kind="ExternalOutput").ap()
    dbg_d = None
    if DEBUG is not None:
        dbg_d = nc.dram_tensor("dbg", [128, 8 * S], BF16, kind="ExternalOutput").ap()

    wviews = {
        "in": w_in_d.rearrange("(kc p) c -> p kc c", p=128),
        "cp": wcp_d.rearrange("(kc p) c -> p kc c", p=128),
        "dp": wdp_d.rearrange("(kc p) c -> p kc c", p=128),
        "xp": wxp_d.rearrange("(kc p) c -> p kc c", p=128),
        "out": wout_d.rearrange("(kc p) c -> p kc c", p=128),
        "kv": wkv_d.rearrange("(kc p) c -> p kc c", p=128),
    }

    with contextlib.ExitStack() as st:
        P = Prog(nc)

        tcount = [0]

        def T(stack, name, shape, dt):
            tcount[0] += 1
            return stack.enter_context(nc.sbuf_tensor(f"sb{tcount[0]}_" + name, list(shape), dt))

        hT = T(st, "hT", [128, NCH, S], BF16)
        yT = T(st, "yT", [128, NCH, S], BF16)
        actT = T(st, "actT", [128, NCH, S], BF16)
        wsl = [T(st, f"wsl{i}", [128, NCH, 512], BF16) for i in range(NSLOT)]
        cb = T(st, "cb", [128, 4, 128], BF16)
        identf = T(st, "identf", [128, 128], F32)
        vecsT = T(st, "vecsT", [128, NROWS], F32)
        sc = T(st, "sc", [128, 64], F32)
        ps = [st.enter_context(nc.psum_tensor(f"ps{i}", [128, 512], F32)) for i in range(8)]
        ident = cb[:, 0, :]
        ones = cb[:, 1, :]
        bd64 = cb[:, 2, :]
        negmask = cb[:, 3, :]
        SC_QN, SC_KN, SC_SUB, SC_XQ, SC_XK, SC_NLAM, SC_LNGH, SC_LNBH, SC_EPS6, SC_EPS5 = 0, 1, 2, 3, 5, 7, 8, 16, 30, 31
        banks = Banks(range(8))

        def PSK(b):
            return ("ps", b)

        groups = []

        def G(*parts):
            groups.append(list(parts))
            return len(groups) - 1

        g_kv = [G((0, "kv", i * 512, 512)) for i in range(4)]
        g_xh = [G((0, "in", X_Q + h * 256, 256), (256, "in", X_GATE + h * 256, 256)) for h in range(4)]
        g_xp = [G((0, "xp", i * 512, 512)) for i in range(2)]
        g_m2 = [G((0, "in", MERGE + 2 * D + i * 512, 512)) for i in range(2)]
        g_cab = [G((0, "in", C_GLU + jj * 256, 256), (256, "in", C_GLU + D + jj * 256, 256)) for jj in range(4)]
        g_cg = [G((0, "in", C_GATE + i * 512, 512)) for i in range(2)]
        g_cp = [G((0, "cp", i * 512, 512)) for i in range(2)]
        g_m0 = [G((0, "in", MERGE + i * 512, 512)) for i in range(2)]
        g_dh = [G((0, "in", D_Q + h * 128, 128), (128, "in", D_K + h * 128, 128),
                  (256, "in", D_V + h * 128, 128), (384, "in", D_GATE + h * 128, 128)) for h in range(8)]
        g_dp = [G((0, "dp", i * 512, 512)) for i in range(2)]
        g_m1 = [G((0, "in", MERGE + D + i * 512, 512)) for i in range(2)]
        g_out = [G((0, "out", i * 512, 512)) for i in range(2)]
        order = (g_kv + g_xh + [g_xp[0], g_m2[0], g_xp[1], g_m2[1]] + g_cab + g_cg
                 + [g_cp[0], g_m0[0], g_cp[1], g_m0[1]] + g_dh + [g_dp[0], g_m1[0], g_dp[1], g_m1[1]] + g_out)
        assert sorted(order) == list(range(len(groups)))
        pos_of = {g: i for i, g in enumerate(order)}
        wstate = {"issued": 0}

        def wkeys(slot, c0, n):
            return [("w", slot, q) for q in range(c0 // 128, (c0 + n) // 128)]

        def w_issue_next():
            i = wstate["issued"]
            if i >= len(order):
                return
            g = order[i]
            slot = i % NSLOT
            for (dc, wn, sc0, n) in groups[g]:
                P.dma("pool", wsl[slot][:, :, dc:dc + n], wviews[wn][:, :, sc0:sc0 + n],
                      writes=wkeys(slot, dc, n), chan=f"w{slot}")
            wstate["issued"] = i + 1

        def w_slot(g):
            i = pos_of[g]
            assert i < wstate["issued"], "weight group not issued yet"
            assert i >= wstate["issued"] - NSLOT
            return i % NSLOT

        def w_done(g):
            w_issue_next()

        def mm(out, lhsT, rhs, start, stop, reads, writes, skip=False):
            if skip:
                P.op("pe", lambda e: e.matmul(out, lhsT=lhsT, rhs=rhs, start=start, stop=stop,
                                              skip_group_check=True), reads, writes)
            else:
                P.op("pe", lambda e: e.matmul(out, lhsT=lhsT, rhs=rhs, start=start, stop=stop), reads, writes)

        def act(out, in_, func, reads, writes, scale=1.0, bias=0.0, accum=None):
            if accum is not None:
                P.op("act", lambda e: e.activation(out=out, in_=in_, func=func, scale=scale, bias=bias,
                                                   accum_out=accum), reads, writes)
            else:
                P.op("act", lambda e: e.activation(out=out, in_=in_, func=func, scale=scale, bias=bias),
                     reads, writes)

        def ts(eng, out, in0, s1, s2, op0, op1, reads, writes):
            if s2 is None:
                P.op(eng, lambda e: e.tensor_scalar(out=out, in0=in0, scalar1=s1, scalar2=None, op0=op0),
                     reads, writes)
            else:
                P.op(eng, lambda e: e.tensor_scalar(out=out, in0=in0, scalar1=s1, scalar2=s2, op0=op0, op1=op1),
                     reads, writes)

        def stt(out, in0, scalar, in1, op0, op1, reads, writes):
            P.op("dve", lambda e: e.scalar_tensor_tensor(out=out, in0=in0, scalar=scalar, in1=in1, op0=op0,
                                                         op1=op1), reads, writes)

        def tt(eng, out, in0, in1, op, reads, writes):
            P.op(eng, lambda e: e.tensor_tensor(out=out, in0=in0, in1=in1, op=op), reads, writes)

        def cp(eng, out, in_, reads, writes):
            if eng == "act":
                P.op("act", lambda e: e.copy(out=out, in_=in_), reads, writes)
            else:
                P.op(eng, lambda e: e.tensor_copy(out=out, in_=in_), reads, writes)

        def proj(bank, slot, col0, rhs_fn, rkeys, ncols=128, n=512, accum_extra=None):
            for kc in range(NCH):
                mm(ps[bank][:, 0:n], wsl[slot][:, kc, col0:col0 + 128], rhs_fn(kc), kc == 0, kc == NCH - 1,
                   reads=wkeys(slot, col0, 128) + rkeys, writes=[PSK(bank)])

        def hT_rhs(tb):
            return (lambda kc: hT[:, kc, tb * 512:(tb + 1) * 512]), [("hT", tb)]

        def rsqrt_from_psum(bank, n, scale, eps, vbuf, vkey):
            ecol = sc[:, SC_EPS6:SC_EPS6 + 1] if eps == 1e-6 else sc[:, SC_EPS5:SC_EPS5 + 1]
            act(vbuf[:, 0:n], ps[bank][:, 0:n], AF.Ln, ["sc"], [PSK(bank), vkey], scale=scale, bias=ecol)
            act(vbuf[:, 0:n], vbuf[:, 0:n], AF.Exp, [vkey], [vkey], scale=-0.5)

        state = {"done": False, "inph": False}

        def ph_enter():
            state["inph"] = True

        def ph_exit():
            state["inph"] = False
            if state["done"]:
                raise _Stop()

        def rsqrt_dve_evac(bank, n, scale, eps, vbuf, vkey):
            ts("dve", vbuf[:, 0:n], ps[bank][:, 0:n], scale, eps, ALU.mult, ALU.add, reads=[], writes=[PSK(bank), vkey])
            act(vbuf[:, 0:n], vbuf[:, 0:n], AF.Ln, [vkey], [vkey])
            act(vbuf[:, 0:n], vbuf[:, 0:n], AF.Exp, [vkey], [vkey], scale=-0.5)

        def dump(tag, tensor):
            if DEBUG == tag:
                P.barrier()
                P.dma("sp", dbg_d, tensor[:].rearrange("p a b -> p (a b)"), writes=["dbg"], chan="dbg")
                P.op("sp", None, reads=["dbg"])
                state["done"] = True
                if not state["inph"]:
                    raise _Stop()

        P.dma("pool", cb[:].rearrange("p a b -> p (a b)"), cb_d, writes=["cb"], chan="c0")
        P.dma("sp", identf[:], idf_d, writes=["identf"], chan="c1")
        for i in range(NSLOT):
            w_issue_next()
        P.op("pool", lambda e: e.memset(sc[:, SC_EPS6:SC_EPS6 + 1], 1e-6), writes=["sc"])
        P.op("pool", lambda e: e.memset(sc[:, SC_EPS5:SC_EPS5 + 1], 1e-5), writes=["sc"])
        with contextlib.ExitStack() as ph:
            vst = [T(ph, f"vst{i}", [128, 128], F32) for i in range(3)]
            lamb = T(ph, "lamb", [128, 256], F32)
            lamp = T(ph, "lamp", [128, 128], F32)
            lams = T(ph, "lams", [128, 2], F32)
            rows = [(0, 128), (128, 128), (256, NROWS - 256)]
            for i, (r0, n) in enumerate(rows):
                P.dma("sp", vst[i][0:n, :], vecs_d[r0:r0 + n, :], writes=[("vst", i)], chan="c1")
            P.dma("sp", lamb[:], lam_d.partition_broadcast(128), writes=["lamb"], chan="c1")
            b = banks.alloc()
            for i, (r0, n) in enumerate(rows):
                mm(ps[b][:, r0:r0 + n], vst[i][0:n, :], identf[0:n, 0:n], True, True,
                   reads=[("vst", i), "identf"], writes=[PSK(b)], skip=True)
            cp("dve", vecsT[:], ps[b][:, 0:NROWS], reads=[], writes=[PSK(b), "vecsT"])
            banks.free(b)
            ts("dve", sc[:, SC_QN:SC_QN + 1], vecsT[:, R_QN:R_QN + 1], 0.125, None, ALU.mult, None, ["vecsT"], ["sc"])
            cp("dve", sc[:, SC_KN:SC_KN + 1], vecsT[:, R_KN:R_KN + 1], ["vecsT"], ["sc"])
            ts("dve", sc[:, SC_SUB:SC_SUB + 1], vecsT[:, R_SUB:R_SUB + 1], 0.4, None, ALU.mult, None, ["vecsT"], ["sc"])
            cp("dve", sc[:, SC_XQ:SC_XQ + 2], vecsT[:, R_XQ:R_XQ + 2], ["vecsT"], ["sc"])
            ts("dve", sc[:, SC_XK:SC_XK + 2], vecsT[:, R_XK:R_XK + 2], 1.0 / 16.0, None, ALU.mult, None, ["vecsT"], ["sc"])
            ts("dve", sc[:, SC_LNGH:SC_LNGH + 8], vecsT[:, R_LNG:R_LNG + 8], 0.5, None, ALU.mult, None, ["vecsT"], ["sc"])
            ts("dve", sc[:, SC_LNBH:SC_LNBH + 8], vecsT[:, R_LNB:R_LNB + 8], 0.5, None, ALU.mult, None, ["vecsT"], ["sc"])
            tt("dve", lamp[:], lamb[:, 0:128], lamb[:, 128:256], ALU.mult, ["lamb"], ["lamp"])
            P.op("dve", lambda e: e.reduce_sum(out=lams[:], in_=lamp[:].rearrange("p (a b) -> p a b", a=2),
                                               axis=AX.X), ["lamp"], ["lams"])
            act(lams[:], lams[:], AF.Exp, ["lams"], ["lams"])
            tt("dve", lams[:, 0:1], lams[:, 1:2], lams[:, 0:1], ALU.subtract, ["lams"], ["lams"])
            ts("dve", sc[:, SC_NLAM:SC_NLAM + 1], lams[:, 0:1], -0.2, None, ALU.add, None, ["lams"], ["sc"])
            P.barrier()

        try:

            def branch_out(gp, gm, first, ph):
                thm = [T(ph, f"thm{i}", [128, 512], F32) for i in range(2)]
                tb_ = [T(ph, f"tbo{i}", [128, 512], F32) for i in range(2)]
                r_th = Ring(thm)
                r_t = Ring(tb_)
                for half in range(2):
                    sp_ = w_slot(gp[half])
                    sm_ = w_slot(gm[half])
                    for jj in range(4):
                        j = half * 4 + jj
                        for tb in range(NTB):
                            bp = banks.alloc()
                            for c in range(NCH):
                                mm(ps[bp][:, :], wsl[sp_][:, c, jj * 128:(jj + 1) * 128], actT[:, c, tb * 512:(tb + 1) * 512],
                                   c == 0, c == NCH - 1, reads=wkeys(sp_, jj * 128, 128) + [("act", c, tb)], writes=[PSK(bp)])
                            bm = banks.alloc()
                            rf, rk = hT_rhs(tb)
                            proj(bm, sm_, jj * 128, rf, rk)
                            k1, th = r_th.next()
                            act(th[:], ps[bm][:, :], AF.Tanh, [], [PSK(bm), ("thm", k1)], scale=0.5)
                            banks.free(bm)
                            k2, tbuf = r_t.next()
                            stt(tbuf[:], th[:], 1.0, ps[bp][:, :], ALU.add, ALU.mult, [("thm", k1)], [PSK(bp), ("tbo", k2)])
                            banks.free(bp)
                            ysl = yT[:, j, tb * 512:(tb + 1) * 512]
                            if first:
                                ts("dve", ysl, tbuf[:], 0.5, None, ALU.mult, None, [("tbo", k2)], [("y", j, tb)])
                            else:
                                stt(ysl, tbuf[:], 0.5, ysl, ALU.mult, ALU.add, [("tbo", k2)], [("y", j, tb)])
                    w_done(gp[half])
                    w_done(gm[half])

            with contextlib.ExitStack() as ph:
                ph_enter()
                xt = [T(ph, f"xt{i}", [128, D], F32) for i in range(3)]
                xn = [T(ph, f"xn{i}", [128, D], BF16) for i in range(3)]
                junk = T(ph, "junk", [128, D], BF16)
                gbc = T(ph, "gbc", [128, D], F32)
                gmbc = T(ph, "gmbc", [128, D], F32)
                ssq = T(ph, "ssq", [128, 32], F32)
                memT = T(ph, "memT", [128, NCH, MEM], BF16)
                P.dma("sp", gbc[:], ng_d.partition_broadcast(128), writes=["gbc"], chan="c1")
                P.dma("sp", gmbc[:], mg_d.partition_broadcast(128), writes=["gmbc"], chan="c1")
                tiles = [("m", i) for i in range(2)] + [("x", i) for i in range(NT)]
                for n_, (kind, t) in enumerate(tiles):
                    i = n_ % 3
                    src = (mem_d if kind == "m" else x_d)[t * 128:(t + 1) * 128, :]
                    P.dma("sp", xt[i][:], src, writes=[("xt", i)], chan=f"x{i}")
                    col = ssq[:, n_:n_ + 1]
                    act(junk[:], xt[i][:], AF.Square, [("xt", i)], ["junk"])
                    P.op("dve", (lambda e, col=col: e.reduce_sum(out=col, in_=junk[:], axis=AX.X)), ["junk"], [("ssq", n_)])
                    act(col, col, AF.Ln, [("ssq", n_), "sc"], [("ssq", n_)], scale=1.0 / D, bias=sc[:, SC_EPS6:SC_EPS6 + 1])
                    act(col, col, AF.Exp, [("ssq", n_)], [("ssq", n_)], scale=-0.5)
                    gsrc = gmbc if kind == "m" else gbc
                    stt(xn[i][:], xt[i][:], col, gsrc[:], ALU.mult, ALU.mult,
                        [("xt", i), ("ssq", n_), "gbc", "gmbc"], [("xn", i)])
                    b = banks.alloc()
                    pbf = ps[b][:].bitcast(BF16)
                    for kc in range(NCH):
                        P.op("pe", (lambda e, kc=kc, i=i, pbf=pbf: e.transpose(out=pbf[:, kc * 128:(kc + 1) * 128],
                                                                              in_=xn[i][:, kc * 128:(kc + 1) * 128],
                                                                              identity=ident)),
                             reads=[("xn", i), "cb"], writes=[PSK(b)])
                    src3 = pbf.rearrange("p (a b) -> p a b", a=NCH)
                    if kind == "m":
                        cp("dve", memT[:, :, t * 128:(t + 1) * 128], src3, [], [PSK(b), "memT"])
                    else:
                        cp("dve", hT[:, :, t * 128:(t + 1) * 128], src3, [], [PSK(b), ("hT", t // 4)])
                    banks.free(b)
                with contextlib.ExitStack() as ph:
                    ph_enter()
                    kT = T(ph, "kT", [128, NCH, MEM], BF16)
                    Vx = T(ph, "Vx", [128, 2, D], BF16)
                    qT = T(ph, "qT", [128, 2, S], BF16)
                    sqb = [T(ph, f"xsq{i}", [128, 512], BF16) for i in range(4)]
                    vb = [T(ph, f"xvb{i}", [128, 512], F32) for i in range(2)]
                    PT = [T(ph, f"xPT{i}", [128, 512], BF16) for i in range(4)]
                    thg = [T(ph, f"xthg{i}", [128, 512], F32) for i in range(2)]
                    gsb = [T(ph, f"xgs{i}", [128, 512], F32) for i in range(2)]
                    rlb = [T(ph, f"xrl{i}", [128, 512], F32) for i in range(2)]
                    tob = [T(ph, f"xto{i}", [128, 512], F32) for i in range(2)]
                    r_sq, r_vb, r_PT, r_thg, r_gs, r_rl, r_to = (Ring(sqb), Ring(vb), Ring(PT), Ring(thg), Ring(gsb),
                                                                 Ring(rlb), Ring(tob))
                    for hx in range(4):
                        bks = []
                        sqs = []
                        for dc in range(2):
                            c = hx * 2 + dc
                            g = g_kv[c // 4]
                            sl = w_slot(g)
                            b = banks.alloc()
                            proj(b, sl, (c % 4) * 128, lambda kc: memT[:, kc, :], ["memT"], n=MEM)
                            k_, sq = r_sq.next()
                            act(sq[:, 0:MEM], ps[b][:, 0:MEM], AF.Square, [], [PSK(b), ("xsq", k_)])
                            bks.append(b)
                            sqs.append((k_, sq))
                            if c == 3:
                                w_done(g_kv[0])
                            if c == 7:
                                w_done(g_kv[1])
                        bs = banks.alloc()
                        for dc in range(2):
                            mm(ps[bs][:, 0:MEM], ones, sqs[dc][1][:, 0:MEM], dc == 0, dc == 1, ["cb", ("xsq", sqs[dc][0])], [PSK(bs)])
                        kv_, v = r_vb.next()
                        rsqrt_from_psum(bs, MEM, 1.0 / 256.0, 1e-6, v, ("xvb", kv_))
                        banks.free(bs)
                        for dc in range(2):
                            c = hx * 2 + dc
                            stt(kT[:, c, :], ps[bks[dc]][:, 0:MEM], sc[:, SC_XK + dc:SC_XK + dc + 1], v[:, 0:MEM], ALU.mult, ALU.mult,
                                ["sc", ("xvb", kv_)], [PSK(bks[dc]), ("kT", c)])
                            banks.free(bks[dc])
                    for vg in range(2):
                        sl = w_slot(g_kv[2 + vg])
                        for mt in range(2):
                            b = banks.alloc()
                            for kc in range(NCH):
                                mm(ps[b][:, :], memT[:, kc, mt * 128:(mt + 1) * 128], wsl[sl][:, kc, :], kc == 0, kc == NCH - 1,
                                   ["memT"] + wkeys(sl, 0, 512), [PSK(b)])
                            cp("dve", Vx[:, mt, vg * 512:(vg + 1) * 512], ps[b][:, :], [], [PSK(b), ("Vx", mt, vg)])
                            banks.free(b)
                        w_done(g_kv[2 + vg])
                    for hx in range(4):
                        sl = w_slot(g_xh[hx])
                        pend = None

                        def q_finish(pend):
                            tb, bq, sqs = pend
                            bs = banks.alloc()
                            for dc in range(2):
                                mm(ps[bs][:, :], ones, sqs[dc][1][:], dc == 0, dc == 1, ["cb", ("xsq", sqs[dc][0])], [PSK(bs)])
                            kv_, v = r_vb.next()
                            rsqrt_from_psum(bs, 512, 1.0 / 256.0, 1e-6, v, ("xvb", kv_))
                            banks.free(bs)
                            for dc in range(2):
                                stt(qT[:, dc, tb * 512:(tb + 1) * 512], ps[bq[dc]][:, :], sc[:, SC_XQ + dc:SC_XQ + dc + 1], v[:],
                                    ALU.mult, ALU.mult, ["sc", ("xvb", kv_)], [PSK(bq[dc]), ("qT", dc, tb)])
                                banks.free(bq[dc])

                        for tb in range(NTB):
                            bq = []
                            sqs = []
                            rf, rk = hT_rhs(tb)
                            for dc in range(2):
                                b = banks.alloc()
                                proj(b, sl, dc * 128, rf, rk)
                                k_, sq = r_sq.next()
                                act(sq[:], ps[b][:, :], AF.Square, [], [PSK(b), ("xsq", k_)])
                                bq.append(b)
                                sqs.append((k_, sq))
                            if pend is not None:
                                q_finish(pend)
                            pend = (tb, bq, sqs)
                        q_finish(pend)
                        for tb in range(NTB):
                            pts = []
                            for mt in range(2):
                                b = banks.alloc()
                                for dc in range(2):
                                    mm(ps[b][:, :], kT[:, hx * 2 + dc, mt * 128:(mt + 1) * 128], qT[:, dc, tb * 512:(tb + 1) * 512],
                                       dc == 0, dc == 1, [("kT", hx * 2 + dc), ("qT", dc, tb)], [PSK(b)])
                                kp, pt = r_PT.next()
                                act(pt[:], ps[b][:, :], AF.Exp, [], [PSK(b), ("xPT", kp)])
                                banks.free(b)
                                pts.append((kp, pt))
                            bo = []
                            for vc in range(2):
                                b = banks.alloc()
                                for mt in range(2):
                                    c0 = hx * 256 + vc * 128
                                    mm(ps[b][:, :], Vx[:, mt, c0:c0 + 128], pts[mt][1][:], mt == 0, mt == 1,
                                       [("Vx", mt, c0 // 512), ("xPT", pts[mt][0])], [PSK(b)])
                                bo.append(b)
                            bl = banks.alloc()
                            for mt in range(2):
                                mm(ps[bl][:, :], ones, pts[mt][1][:], mt == 0, mt == 1, ["cb", ("xPT", pts[mt][0])], [PSK(bl)])
                            bg = []
                            rf, rk = hT_rhs(tb)
                            for vc in range(2):
                                b = banks.alloc()
                                proj(b, sl, 256 + vc * 128, rf, rk)
                                bg.append(b)
                            kr, rl = r_rl.next()
                            act(rl[:], ps[bl][:, :], AF.Ln, [], [PSK(bl), ("xrl", kr)])
                            banks.free(bl)
                            act(rl[:], rl[:], AF.Exp, [("xrl", kr)], [("xrl", kr)], scale=-1.0)
                            for vc in range(2):
                                kt_, th = r_thg.next()
                                act(th[:], ps[bg[vc]][:, :], AF.Tanh, [], [PSK(bg[vc]), ("xthg", kt_)], scale=0.5)
                                kg, gs = r_gs.next()
                                stt(gs[:], th[:], 1.0, ps[bg[vc]][:, :], ALU.add, ALU.mult, [("xthg", kt_)], [PSK(bg[vc]), ("xgs", kg)])
                                banks.free(bg[vc])
                                ko, to = r_to.next()
                                tt("dve", to[:], ps[bo[vc]][:, :], rl[:], ALU.mult, [("xrl", kr)], [PSK(bo[vc]), ("xto", ko)])
                                banks.free(bo[vc])
                                stt(actT[:, hx * 2 + vc, tb * 512:(tb + 1) * 512], to[:], 0.5, gs[:], ALU.mult, ALU.mult,
                                    [("xto", ko), ("xgs", kg)], [("act", hx * 2 + vc, tb)])
                        w_done(g_xh[hx])
                    dump("actx", actT)
                    if not state["done"]:
                        branch_out(g_xp, g_m2, True, ph)
                    P.barrier()
            ph_exit()
            dump("yx", yT)

            with contextlib.ExitStack() as ph:
                ph_enter()
                PADW = 30
                ub = [T(ph, f"ub{i}", [128, PADW + S], BF16) for i in range(2)]
                dg = [T(ph, f"dg{i}", [128, 31, 128], BF16) for i in range(2)]
                thb = [T(ph, f"cth{i}", [128, 512], F32) for i in range(2)]
                Ms = [T(ph, f"cM{i}", [128, 512], F32) for i in range(NTB)]
                Rs = [T(ph, f"cR{i}", [128, 512], F32) for i in range(NTB)]
                zt = [T(ph, f"cz{i}", [128, 512], F32) for i in range(2)]
                sqc = [T(ph, f"csq{i}", [128, 512], BF16) for i in range(2)]
                gsc = [T(ph, f"cgs{i}", [128, 512], F32) for i in range(2)]
                vvb = [T(ph, f"cv{i}", [128, 512], F32) for i in range(2)]
                r_v = Ring(vvb)
                r_th, r_z, r_sq, r_gs = Ring(thb), Ring(zt), Ring(sqc), Ring(gsc)
                for i in range(2):
                    P.op("pool", (lambda e, i=i: e.memset(ub[i][:, 0:PADW], 0.0)), writes=[("ub", i, -1)])
                for j in range(NCH):
                    g = g_cab[j // 2]
                    sl = w_slot(g)
                    jl = j % 2
                    ui = j % 2
                    for k in range(31):
                        col = vecsT[:, R_DW + k * 8 + j:R_DW + k * 8 + j + 1]
                        P.op("pool", (lambda e, k=k, ui=ui, col=col: e.tensor_scalar(out=dg[ui][:, k, :], in0=ident, scalar1=col,
                                                                                     scalar2=0.5, op0=ALU.mult, op1=ALU.mult)),
                             reads=["cb", "vecsT"], writes=[("dg", ui)])
                    pend = None

                    def conv_block(tb, ui=ui, j=j):
                        b = banks.alloc()
                        rk = [("ub", ui, tb), ("ub", ui, tb - 1), ("dg", ui)]
                        for k in range(31):
                            mm(ps[b][:, :], dg[ui][:, k, :], ub[ui][:, tb * 512 + k:tb * 512 + k + 512], k == 0, k == 30, rk, [PSK(b)])
                        ts("dve", actT[:, j, tb * 512:(tb + 1) * 512], ps[b][:, :], vecsT[:, R_DWB + j:R_DWB + j + 1], None, ALU.add, None,
                           ["vecsT"], [PSK(b), ("act", j, tb)])
                        banks.free(b)

                    for tb in range(NTB):
                        rf, rk = hT_rhs(tb)
                        ba = banks.alloc()
                        proj(ba, sl, jl * 128, rf, rk)
                        bb = banks.alloc()
                        proj(bb, sl, 256 + jl * 128, rf, rk)
                        kt_, th = r_th.next()
                        act(th[:], ps[bb][:, :], AF.Tanh, [], [PSK(bb), ("cth", kt_)], scale=0.5)
                        banks.free(bb)
                        stt(ub[ui][:, PADW + tb * 512:PADW + (tb + 1) * 512], th[:], 1.0, ps[ba][:, :], ALU.add, ALU.mult,
                            [("cth", kt_)], [PSK(ba), ("ub", ui, tb)])
                        banks.free(ba)
                        if pend is not None:
                            conv_block(pend)
                        pend = tb
                    conv_block(pend)
                    if jl == 1:
                        w_done(g)
                dump("conv", actT)
                for tb in range(NTB):
                    bs = banks.alloc()
                    bq = banks.alloc()
                    for j in range(NCH):
                        a = actT[:, j, tb * 512:(tb + 1) * 512]
                        mm(ps[bs][:, :], ones, a, j == 0, j == NCH - 1, ["cb", ("act", j, tb)], [PSK(bs)])
                        ks, sq = r_sq.next()
                        act(sq[:], a, AF.Square, [("act", j, tb)], [("csq", ks)])
                        mm(ps[bq][:, :], ones, sq[:], j == 0, j == NCH - 1, ["cb", ("csq", ks)], [PSK(bq)])
                    M, R = Ms[tb], Rs[tb]
                    ts("dve", M[:], ps[bs][:, :], 1.0 / D, None, ALU.mult, None, [], [PSK(bs), ("cM", tb)])
                    banks.free(bs)
                    tt("dve", R[:], M[:], M[:], ALU.mult, [("cM", tb)], [("cR", tb)])
                    stt(R[:], ps[bq][:, :], 1.0 / D, R[:], ALU.mult, ALU.subtract, [], [PSK(bq), ("cR", tb)])
                    banks.free(bq)
                    act(R[:], R[:], AF.Ln, [("cR", tb), "sc"], [("cR", tb)], scale=1.0, bias=sc[:, SC_EPS5:SC_EPS5 + 1])
                    act(R[:], R[:], AF.Exp, [("cR", tb)], [("cR", tb)], scale=-0.5)
                    stt(M[:], M[:], -1.0, R[:], ALU.mult, ALU.mult, [("cR", tb)], [("cM", tb)])
                for half in range(2):
                    slg = w_slot(g_cg[half])
                    for jj in range(4):
                        j = half * 4 + jj
                        for tb in range(NTB):
                            a = actT[:, j, tb * 512:(tb + 1) * 512]
                            kz, z = r_z.next()
                            tt("dve", z[:], a, Rs[tb][:], ALU.mult, [("act", j, tb), ("cR", tb)], [("cz", kz)])
                            tt("dve", z[:], z[:], Ms[tb][:], ALU.add, [("cM", tb)], [("cz", kz)])
                            kt_, th = r_th.next()
                            act(th[:], z[:], AF.Tanh, [("cz", kz), "sc"], [("cth", kt_)],
                                scale=sc[:, SC_LNGH + j:SC_LNGH + j + 1], bias=sc[:, SC_LNBH + j:SC_LNBH + j + 1])
                            kv2, vv = r_v.next()
                            act(vv[:], z[:], AF.Identity, [("cz", kz), "vecsT"], [("cv", kv2)],
                                scale=vecsT[:, R_LNG + j:R_LNG + j + 1], bias=vecsT[:, R_LNB + j:R_LNB + j + 1])
                            stt(z[:], th[:], 1.0, vv[:], ALU.add, ALU.mult, [("cth", kt_), ("cv", kv2)], [("cz", kz)])
                            bg = banks.alloc()
                            rf, rk = hT_rhs(tb)
                            proj(bg, slg, jj * 128, rf, rk)
                            kt2, th2 = r_th.next()
                            act(th2[:], ps[bg][:, :], AF.Tanh, [], [PSK(bg), ("cth", kt2)], scale=0.5)
                            kg, gs = r_gs.next()
                            stt(gs[:], th2[:], 1.0, ps[bg][:, :], ALU.add, ALU.mult, [("cth", kt2)], [PSK(bg), ("cgs", kg)])
                            banks.free(bg)
                            stt(a, z[:], 0.25, gs[:], ALU.mult, ALU.mult, [("cz", kz), ("cgs", kg)], [("act", j, tb)])
                    w_done(g_cg[half])
                dump("actc", actT)
                if not state["done"]:
                    branch_out(g_cp, g_m0, False, ph)
                P.barrier()
            ph_exit()
            dump("yc", yT)

            with contextlib.ExitStack() as ph:
                ph_enter()
                QL = [T(ph, f"QL{i}", [128, S], BF16) for i in range(2)]
                QU = [T(ph, f"QU{i}", [128, S], BF16) for i in range(2)]
                KL = [T(ph, f"KL{i}", [128, S], BF16) for i in range(2)]
                KU = [T(ph, f"KU{i}", [128, S], BF16) for i in range(2)]
                Vd = [T(ph, f"Vd{i}", [128, NT, 128], BF16) for i in range(2)]
                gsd = [T(ph, f"gsd{i}", [128, S], BF16) for i in range(2)]
                PTd = [[T(ph, f"PT{m}_{i}", [128, 512], BF16) for i in range(3)] for m in range(2)]
                sqd = [T(ph, f"dsq{i}", [128, 512], BF16) for i in range(2)]
                thd = [T(ph, f"dth{i}", [128, 512], F32) for i in range(1)]
                vbd = [T(ph, f"dvb{i}", [128, 512], F32) for i in range(2)]
                E1 = [T(ph, f"E1_{i}", [128, 512], F32) for i in range(2)]
                E2 = [T(ph, f"E2_{i}", [128, 512], F32) for i in range(2)]
                sqe = [T(ph, f"sqe{i}", [128, 512], BF16) for i in range(2)]
                RL = [T(ph, f"RL{i}", [128, 512], F32) for i in range(2)]
                r_PT = [Ring(PTd[0]), Ring(PTd[1])]
                rawd = [T(ph, f"draw{i}", [128, 512], F32) for i in range(3)]
                r_raw = Ring(rawd)
                r_sq, r_th, r_vb = Ring(sqd), Ring(thd), Ring(vbd)
                for i in range(2):
                    P.op("pool", (lambda e, i=i: e.memset(QU[i][0:64, :], 0.0)), writes=[("QUa", i)])
                    P.op("pool", (lambda e, i=i: e.memset(KU[i][0:64, :], 0.0)), writes=[("KUa", i)])
                O_B = [4, 6]
                L_B = [5, 7]
                for b in (4, 5, 6, 7):
                    banks.busy.add(b)
                dbanks = Banks([0, 1, 2, 3])

                def qk_unit(h, which, tb):
                    s_ = h % 2
                    st_ = {}
                    c0 = 0 if which == "q" else 128
                    gcol = sc[:, SC_QN:SC_QN + 1] if which == "q" else sc[:, SC_KN:SC_KN + 1]

                    def st0():
                        sl = w_slot(g_dh[h])
                        rf, rk = hT_rhs(tb)
                        b = dbanks.alloc()
                        proj(b, sl, c0, rf, rk)
                        st_["kraw"], st_["raw"] = r_raw.next()
                        cp("dve", st_["raw"][:], ps[b][:, :], [], [PSK(b), ("draw", st_["kraw"])])
                        dbanks.free(b)
                        st_["ks"], sq = r_sq.next()
                        tt("dve", sq[:], st_["raw"][:], st_["raw"][:], ALU.mult, [("draw", st_["kraw"])], [("dsq", st_["ks"])])

                    def st1():
                        bs = dbanks.alloc()
                        mm(ps[bs][:, :], bd64, sqd[st_["ks"]][:], True, True, ["cb", ("dsq", st_["ks"])], [PSK(bs)])
                        st_["kv"], st_["v"] = r_vb.next()
                        ts("dve", st_["v"][:], ps[bs][:, :], 1.0 / 64.0, 1e-6, ALU.mult, ALU.add, [], [PSK(bs), ("dvb", st_["kv"])])
                        dbanks.free(bs)

                    def st2():
                        v, vk = st_["v"], ("dvb", st_["kv"])
                        act(v[:], v[:], AF.Ln, [vk], [vk])
                        act(v[:], v[:], AF.Exp, [vk], [vk], scale=-0.5)

                    def st3():
                        v, vk = st_["v"], ("dvb", st_["kv"])
                        raw, rk_ = st_["raw"], ("draw", st_["kraw"])
                        lo = (QL if which == "q" else KL)[s_]
                        up = (QU if which == "q" else KU)[s_]
                        sl_ = slice(tb * 512, (tb + 1) * 512)
                        stt(lo[0:64, sl_], raw[0:64, :], gcol[0:64, :], v[0:64, :], ALU.mult, ALU.mult,
                            ["sc", vk, rk_], [(which + "L", s_, tb)])
                        stt(up[64:128, sl_], raw[64:128, :], gcol[64:128, :], v[64:128, :], ALU.mult, ALU.mult,
                            ["sc", vk, rk_], [(which + "U", s_, tb)])

                    return [st0, st1, st2, st3]

                def v_unit(h, g4):
                    s_ = h % 2

                    def st0():
                        sl = w_slot(g_dh[h])
                        b = dbanks.alloc()
                        for tl in range(4):
                            t = g4 * 4 + tl
                            for kc in range(NCH):
                                mm(ps[b][:, tl * 128:(tl + 1) * 128], hT[:, kc, t * 128:(t + 1) * 128], wsl[sl][:, kc, 256:384],
                                   kc == 0, kc == NCH - 1, [("hT", g4)] + wkeys(sl, 256, 128), [PSK(b)], skip=True)
                        cp("dve", Vd[s_][:, g4 * 4:(g4 + 1) * 4, :], ps[b][:, :].rearrange("p (a b) -> p a b", a=4), [],
                           [PSK(b), ("Vd", s_, g4)])
                        dbanks.free(b)

                    return [st0]

                def g_unit(h, tb, last):
                    s_ = h % 2
                    st_ = {}

                    def st0():
                        sl = w_slot(g_dh[h])
                        rf, rk = hT_rhs(tb)
                        b = dbanks.alloc()
                        proj(b, sl, 384, rf, rk)
                        st_["kraw"], st_["raw"] = r_raw.next()
                        cp("dve", st_["raw"][:], ps[b][:, :], [], [PSK(b), ("draw", st_["kraw"])])
                        dbanks.free(b)
                        if last:
                            w_done(g_dh[h])

                    def st1():
                        st_["kt"], st_["th"] = r_th.next()
                        act(st_["th"][:], st_["raw"][:], AF.Tanh, [("draw", st_["kraw"])], [("dth", st_["kt"])], scale=0.5)

                    def st2():
                        stt(gsd[s_][:, tb * 512:(tb + 1) * 512], st_["th"][:], 1.0, st_["raw"][:], ALU.add, ALU.mult,
                            [("dth", st_["kt"]), ("draw", st_["kraw"])], [("gsd", s_, tb)])

                    return [st0, st1, st2]

                class Prologue:
                    def __init__(self, h, period=2):
                        self.h = h
                        s_ = h % 2
                        self.units = ([qk_unit(h, "q", tb) for tb in range(NTB)] + [qk_unit(h, "k", tb) for tb in range(NTB)]
                                      + [v_unit(h, g4) for g4 in range(4)] + [g_unit(h, tb, tb == NTB - 1) for tb in range(NTB)])
                        self.active = []
                        self.t = 0
                        self.period = period
                        self.started = False

                    def tick(self):
                        h = self.h
                        s_ = h % 2
                        if not self.started:
                            self.started = True
                            P.dma("pool", QL[s_][64:68, :], qaug_d[h], writes=[("QLa", s_)], chan=f"aug{s_}")
                            P.dma("pool", QU[s_][0:4, :], qaug_d[h], writes=[("QUa", s_)], chan=f"aug{s_}")
                            P.dma("pool", KL[s_][64:68, :], kaug_d[h], writes=[("KLa", s_)], chan=f"aug{s_}")
                            P.dma("pool", KU[s_][0:4, :], kaug_d[h], writes=[("KUa", s_)], chan=f"aug{s_}")
                        if self.t % self.period == 0 and self.units:
                            self.active.append(self.units.pop(0))
                        self.t += 1
                        for u in list(self.active):
                            u.pop(0)()
                            if not u:
                                self.active.remove(u)

                    def done(self):
                        return not self.units and not self.active

                    def flush(self):
                        while not self.done():
                            self.tick()

                def scores(h, qb, kt):
                    s_ = h % 2
                    r = kt - 4 * qb
                    c0 = 128 * r if r > 0 else 0
                    n = 512 - c0
                    outp = []
                    for m in range(2):
                        b = dbanks.alloc()
                        if m == 0:
                            lhsT = KL[s_][0:68, kt * 128:(kt + 1) * 128]
                            rhs = QL[s_][0:68, qb * 512 + c0:(qb + 1) * 512]
                            rk = [("kL", s_, kt // 4), ("KLa", s_), ("qL", s_, qb), ("QLa", s_)]
                        else:
                            lhsT = KU[s_][:, kt * 128:(kt + 1) * 128]
                            rhs = QU[s_][:, qb * 512 + c0:(qb + 1) * 512]
                            rk = [("kU", s_, kt // 4), ("KUa", s_), ("qU", s_, qb), ("QUa", s_)]
                        mm(ps[b][:, c0:512], lhsT, rhs, True, r < 0, rk, [PSK(b)], skip=True)
                        if r >= 0:
                            mm(ps[b][:, c0:c0 + 128], ident, negmask, False, True, ["cb"], [PSK(b)], skip=True)
                        kp, pt = r_PT[m].next()
                        act(pt[:, c0:512], ps[b][:, c0:512], AF.Exp, [], [PSK(b), ("PT", m, kp)])
                        dbanks.free(b)
                        outp.append((kp, pt))
                    return (kt, c0, outp)

                def av(h, qb, sc_, first, last):
                    s_ = h % 2
                    kt, c0, outp = sc_
                    for m in range(2):
                        kp, pt = outp[m]
                        mm(ps[O_B[m]][:, c0:512], Vd[s_][:, kt, :], pt[:, c0:512], first, last,
                           [("Vd", s_, kt // 4), ("PT", m, kp)], [PSK(O_B[m])], skip=True)
                        mm(ps[L_B[m]][:, c0:512], ones, pt[:, c0:512], first, last,
                           ["cb", ("PT", m, kp)], [PSK(L_B[m])], skip=True)

                def epiA(h, qb, e):
                    e1, e2 = E1[e], E2[e]
                    cp("dve", e1[:], ps[O_B[0]][:, :], [], [PSK(O_B[0]), ("E1", e)])
                    act(RL[0][:], ps[L_B[0]][:, :], AF.Ln, [], [PSK(L_B[0]), ("RL", 0)])
                    cp("dve", e2[:], ps[O_B[1]][:, :], [], [PSK(O_B[1]), ("E2", e)])
                    act(RL[1][:], ps[L_B[1]][:, :], AF.Ln, [], [PSK(L_B[1]), ("RL", 1)])

                def epiB(h, qb, e):
                    e1, e2 = E1[e], E2[e]
                    act(RL[0][:], RL[0][:], AF.Exp, [("RL", 0)], [("RL", 0)], scale=-1.0)
                    act(RL[1][:], RL[1][:], AF.Exp, [("RL", 1)], [("RL", 1)], scale=-1.0)
                    tt("dve", e1[:], e1[:], RL[0][:], ALU.mult, [("RL", 0)], [("E1", e)])
                    tt("dve", e2[:], e2[:], RL[1][:], ALU.mult, [("RL", 1)], [("E2", e)])
                    stt(e1[:], e2[:], sc[:, SC_NLAM:SC_NLAM + 1], e1[:], ALU.mult, ALU.add, ["sc", ("E2", e)], [("E1", e)])

                def epiC(h, qb, e):
                    tt("dve", sqe[e][:], E1[e][:], E1[e][:], ALU.mult, [("E1", e)], [("sqe", e)])

                def epiD(h, qb, e):
                    s_ = h % 2
                    e1, e2 = E1[e], E2[e]
                    bs = dbanks.alloc()
                    mm(ps[bs][:, :], ones, sqe[e][:], True, True, ["cb", ("sqe", e)], [PSK(bs)])
                    rsqrt_dve_evac(bs, 512, 1.0 / 128.0, 1e-6, e2, ("E2", e))
                    dbanks.free(bs)
                    tt("dve", e1[:], e1[:], e2[:], ALU.mult, [("E2", e)], [("E1", e)])
                    stt(actT[:, h, qb * 512:(qb + 1) * 512], e1[:], sc[:, SC_SUB:SC_SUB + 1], gsd[s_][:, qb * 512:(qb + 1) * 512],
                        ALU.mult, ALU.mult, ["sc", ("E1", e), ("gsd", s_, qb)], [("act", h, qb)])

                ecount = 0
                pend = []
                pro = Prologue(0)
                pro.flush()
                for h in range(8):
                    pro = Prologue(h + 1) if h + 1 < 8 else None
                    for qb in range(NTB):
                        kts = list(range(4 * qb + 4))
                        cur = scores(h, qb, kts[0])
                        for i, kt in enumerate(kts):
                            nxt = scores(h, qb, kts[i + 1]) if i + 1 < len(kts) else None
                            av(h, qb, cur, i == 0, i == len(kts) - 1)
                            cur = nxt
                            if pend:
                                pend.pop(0)()
                            if pro is not None and not pro.done():
                                pro.tick()
                        assert not pend
                        e = ecount % 2
                        ecount += 1
                        epiA(h, qb, e)
                        pend = [(lambda h=h, qb=qb, e=e: epiB(h, qb, e)), (lambda h=h, qb=qb, e=e: epiC(h, qb, e)),
                                (lambda h=h, qb=qb, e=e: epiD(h, qb, e))]
                    if pro is not None:
                        pro.flush()
                while pend:
                    pend.pop(0)()
                for b in (4, 5, 6, 7):
                    banks.free(b)
                dump("actd", actT)
                P.barrier()
            ph_exit()
            with contextlib.ExitStack() as ph:
                ph_enter()
                branch_out(g_dp, g_m1, False, ph)
                P.barrier()

            ph_exit()
            with contextlib.ExitStack() as ph:
                ph_enter()
                xt = [T(ph, f"fx{i}", [128, D], F32) for i in range(2)]
                ot = [T(ph, f"fo{i}", [128, D], F32) for i in range(2)]
                s0 = w_slot(g_out[0])
                s1 = w_slot(g_out[1])
                for t in range(NT):
                    i = t % 2
                    P.dma("sp", xt[i][:], x_d[t * 128:(t + 1) * 128, :], writes=[("fx", i)], chan=f"x{i}")
                    for half in range(2):
                        sl = (s0, s1)[half]
                        b = banks.alloc()
                        for dc in range(NCH):
                            mm(ps[b][:, :], yT[:, dc, t * 128:(t + 1) * 128], wsl[sl][:, dc, :], dc == 0, dc == NCH - 1,
                               [("y", dc, t // 4)] + wkeys(sl, 0, 512), [PSK(b)])
                        tt("dve", ot[i][:, half * 512:(half + 1) * 512], ps[b][:, :], xt[i][:, half * 512:(half + 1) * 512], ALU.add,
                           [("fx", i)], [PSK(b), ("fo", i, half)])
                        banks.free(b)
                    P.dma("sp", out_d[t * 128:(t + 1) * 128, :], ot[i][:], reads=[("fo", i, 0), ("fo", i, 1)],
                          writes=[("fo_st", i)], chan=f"o{i}")
                P.op("sp", None, reads=[("fo_st", 0), ("fo_st", 1)])
            ph_exit()
        except _Stop:
            pass
        except Exception:
            import traceback
            traceback.print_exc()
            raise
        P.emit(st)
    return nc


def _host_constants():
    ident = np.eye(128, dtype=np.float32)
    ones = np.ones((128, 128), np.float32)
    p = np.arange(128)
    bd64 = (p[:, None] // 64 == p[None, :] // 64).astype(np.float32)
    negmask = np.where(p[None, :] < p[:, None], -30000.0, 0.0).astype(np.float32)
    cb = np.concatenate([ident, ones, bd64, negmask], axis=1)
    tok = np.arange(S)
    il = (tok % 128).astype(np.float32)
    ib = (tok // 128).astype(np.float32)
    qaug = np.zeros((8, 4, S), np.float32)
    kaug = np.zeros((8, 4, S), np.float32)
    for h in range(8):
        slope = 2.0 ** (-(h + 1))
        qaug[h, 0] = 1.0
        qaug[h, 1] = -slope * il
        qaug[h, 2] = 1.0
        qaug[h, 3] = -slope * 128.0 * ib
        kaug[h, 0] = slope * il
        kaug[h, 1] = 1.0
        kaug[h, 2] = slope * 128.0 * ib
        kaug[h, 3] = 1.0
    return cb, ident, qaug, kaug


_NC_CACHE = {}


def kernel(x, mem, norm_g, mem_norm_g, w_in, conv_dw, conv_dw_b, conv_ln_g, conv_ln_b, w_conv_proj,
           diff_qn_g, diff_kn_g, lambda_q1, lambda_k1, lambda_q2, lambda_k2, diff_subln_g, w_diff_proj,
           w_mem_kv, x_qn_g, x_kn_g, w_x_proj, w_out):
    f = lambda a: np.ascontiguousarray(np.asarray(a, dtype=np.float32))
    x = f(x); mem = f(mem)
    B = x.shape[0]
    vecs = np.concatenate([
        f(conv_dw_b)[0].reshape(8, 128), f(conv_ln_g)[0].reshape(8, 128), f(conv_ln_b)[0].reshape(8, 128),
        f(conv_dw)[0].reshape(31 * 8, 128),
        np.concatenate([f(diff_qn_g)[0], f(diff_qn_g)[0]])[None, :],
        np.concatenate([f(diff_kn_g)[0], f(diff_kn_g)[0]])[None, :],
        f(diff_subln_g)[0][None, :],
        f(x_qn_g)[0].reshape(2, 128), f(x_kn_g)[0].reshape(2, 128)], axis=0)
    assert vecs.shape == (NROWS, 128)
    lamv = np.concatenate([f(lambda_q1)[0], f(lambda_q2)[0], f(lambda_k1)[0], f(lambda_k2)[0]])[None, :]
    cb, identf, qaug, kaug = _host_constants()
    shared = {
        "w_in": f(w_in)[0], "w_conv_proj": f(w_conv_proj)[0], "w_diff_proj": f(w_diff_proj)[0],
        "w_x_proj": f(w_x_proj)[0], "w_out": f(w_out)[0], "w_mem_kv": f(w_mem_kv)[0],
        "norm_g": f(norm_g), "mem_norm_g": f(mem_norm_g), "vecs": np.ascontiguousarray(vecs),
        "lamv": np.ascontiguousarray(lamv), "cbits": cb, "identf": identf, "qaug": qaug, "kaug": kaug,
    }
    if "nc" not in _NC_CACHE:
        _NC_CACHE["nc"] = build_program()
    nc = _NC_CACHE["nc"]
    in_maps = [dict(shared, x=x[b], mem=mem[b]) for b in range(B)]
    res = run_bass_kernel_spmd(nc, in_maps, core_ids=list(range(B)))
    out = np.stack([np.asarray(r["out"]) for r in res.results], axis=0).astype(np.float32)
    if DEBUG is not None:
        kernel.dbg = [np.asarray(r["dbg"]) for r in res.results]
    return out
```

```python
import contextlib
import numpy as np
import concourse.bass as bass
import concourse.mybir as mybir
from concourse.bass_utils import run_bass_kernel_spmd

F32 = mybir.dt.float32
BF16 = mybir.dt.bfloat16
AF = mybir.ActivationFunctionType
ALU = mybir.AluOpType
AX = mybir.AxisListType

D = 1024
S = 2048
MEM = 256
NCH = 8
NTB = 4
NT = 16
IN_COLS = 12288
C_GLU, C_GATE, D_Q, D_K, D_V, D_GATE, X_Q, X_GATE, MERGE = 0, 2048, 3072, 4096, 5120, 6144, 7168, 8192, 9216
NSLOT = 3
ENGINES = ("pe", "act", "dve", "pool", "sp")

R_DWB, R_LNG, R_LNB, R_DW, R_QN, R_KN, R_SUB, R_XQ, R_XK, NROWS = 0, 8, 16, 24, 272, 273, 274, 275, 277, 279

DEBUG = None


class _Stop(Exception):
    pass


class Prog:
    def __init__(self, nc):
        self.nc = nc
        self.ins = []
        self.last_w = {}
        self.readers = {}
        self.chan_count = {}
        self.last_on = {}

    def _add(self, eng, fn, reads, writes, dma_chan=None, extra_deps=()):
        idx = len(self.ins)
        deps = set(extra_deps)
        for r in reads:
            w = self.last_w.get(r)
            if w is not None:
                deps.add(w)
            if fn is not None:
                self.readers.setdefault(r, []).append(idx)
        for w_ in writes:
            w = self.last_w.get(w_)
            if w is not None:
                deps.add(w)
            for rd in self.readers.get(w_, ()):
                if rd != idx:
                    deps.add(rd)
            self.last_w[w_] = idx
            self.readers[w_] = []
        rec = dict(eng=eng, fn=fn, dma_chan=dma_chan, mark=False)
        waits = []
        for d in deps:
            dr = self.ins[d]
            if dr["dma_chan"] is not None:
                waits.append(("dma", dr["dma_chan"], self.chan_count[dr["dma_chan"]]))
            else:
                if dr["eng"] == "pe" and eng == "pe":
                    continue
                dr["mark"] = True
                waits.append(("eng", dr["eng"], d))
        rec["waits"] = waits
        if dma_chan is not None:
            self.chan_count[dma_chan] = self.chan_count.get(dma_chan, 0) + 16
        elif fn is not None:
            self.last_on[eng] = idx
        self.ins.append(rec)
        return idx

    def op(self, eng, fn, reads=(), writes=()):
        return self._add(eng, fn, tuple(reads), tuple(writes))

    def dma(self, eng, out, in_, reads=(), writes=(), chan=None):
        return self._add(eng, lambda e: e.dma_start(out=out, in_=in_), tuple(reads), tuple(writes),
                         dma_chan=chan)

    def barrier(self):
        lasts = [v for v in self.last_on.values()]
        for e in ENGINES:
            idx = self._add(e, None, (), (), extra_deps=lasts)
            rec = self.ins[idx]
            for c, v in self.chan_count.items():
                if str(c).startswith("w") or str(c).startswith("aug"):
                    continue
                rec["waits"].append(("dma", c, v))

    def emit(self, stack):
        nc = self.nc
        sems = {e: stack.enter_context(nc.semaphore("s_" + e)) for e in ENGINES}
        csems = {c: stack.enter_context(nc.semaphore("c_" + str(c))) for c in self.chan_count}
        cnt = {e: 0 for e in ENGINES}
        for r in self.ins:
            if r["dma_chan"] is None and r["mark"]:
                cnt[r["eng"]] += 1
                r["ord"] = cnt[r["eng"]]
        per = {e: [] for e in ENGINES}
        for r in self.ins:
            per[r["eng"]].append(r)
        block = stack.enter_context(nc.Block())
        ins = self.ins

        def run(engname, eng):
            waited = {}
            for r in per[engname]:
                need = {}
                for w in r["waits"]:
                    if w[0] == "dma":
                        key = ("c", w[1]); val = w[2]
                    else:
                        key = ("e", w[1]); val = ins[w[2]]["ord"]
                    if val > need.get(key, 0):
                        need[key] = val
                for key, val in need.items():
                    if waited.get(key, 0) >= val:
                        continue
                    waited[key] = val
                    eng.wait_ge(csems[key[1]] if key[0] == "c" else sems[key[1]], val)
                if r["fn"] is None:
                    continue
                bi = r["fn"](eng)
                if r["dma_chan"] is not None:
                    bi.then_inc(csems[r["dma_chan"]], 16)
                elif r["mark"]:
                    bi.then_inc(sems[engname], 1)

        block.tensor(lambda e: run("pe", e))
        block.scalar(lambda e: run("act", e))
        block.vector(lambda e: run("dve", e))
        block.gpsimd(lambda e: run("pool", e))
        block.sync(lambda e: run("sp", e))


class Banks:
    def __init__(self, ids):
        self.ids = list(ids)
        self.busy = set()
        self.ptr = 0

    def alloc(self):
        n = len(self.ids)
        for k in range(n):
            b = self.ids[(self.ptr + k) % n]
            if b not in self.busy:
                self.busy.add(b)
                self.ptr = (self.ptr + k + 1) % n
                return b
        raise RuntimeError("out of PSUM banks")

    def free(self, b):
        self.busy.discard(b)


class Ring:
    def __init__(self, items):
        self.items = items
        self.i = 0

    def next(self):
        r = self.items[self.i % len(self.items)]
        k = self.i % len(self.items)
        self.i += 1
        return k, r


def build_program():
    nc = bass.Bass("TRN2", target_bir_lowering=False)
    dt_in = lambda name, shape: nc.dram_tensor(name, list(shape), F32, kind="ExternalInput").ap()
    x_d = dt_in("x", [S, D])
    mem_d = dt_in("mem", [MEM, D])
    w_in_d = dt_in("w_in", [D, IN_COLS])
    wcp_d = dt_in("w_conv_proj", [D, D])
    wdp_d = dt_in("w_diff_proj", [D, D])
    wxp_d = dt_in("w_x_proj", [D, D])
    wout_d = dt_in("w_out", [D, D])
    wkv_d = dt_in("w_mem_kv", [D, 2 * D])
    ng_d = dt_in("norm_g", [1, D])
    mg_d = dt_in("mem_norm_g", [1, D])
    vecs_d = dt_in("vecs", [NROWS, 128])
    lam_d = dt_in("lamv", [1, 256])
    cb_d = dt_in("cbits", [128, 4 * 128])
    idf_d = dt_in("identf", [128, 128])
    qaug_d = dt_in("qaug", [8, 4, S])
    kaug_d = dt_in("kaug", [8, 4, S])
    out_d = nc.dram_tensor("out", [S, D], F32, kind="ExternalOutput").ap()
    dbg_d = None
    if DEBUG is not None:
        dbg_d = nc.dram_tensor("dbg", [128, 8 * S], BF16, kind="ExternalOutput").ap()

    wviews = {
        "in": w_in_d.rearrange("(kc p) c -> p kc c", p=128),
        "cp": wcp_d.rearrange("(kc p) c -> p kc c", p=128),
        "dp": wdp_d.rearrange("(kc p) c -> p kc c", p=128),
        "xp": wxp_d.rearrange("(kc p) c -> p kc c", p=128),
        "out": wout_d.rearrange("(kc p) c -> p kc c", p=128),
        "kv": wkv_d.rearrange("(kc p) c -> p kc c", p=128),
    }

    with contextlib.ExitStack() as st:
        P = Prog(nc)

        tcount = [0]

        def T(stack, name, shape, dt):
            tcount[0] += 1
            return stack.enter_context(nc.sbuf_tensor(f"sb{tcount[0]}_" + name, list(shape), dt))

        hT = T(st, "hT", [128, NCH, S], BF16)
        yT = T(st, "yT", [128, NCH, S], BF16)
        actT = T(st, "actT", [128, NCH, S], BF16)
        wsl = [T(st, f"wsl{i}", [128, NCH, 512], BF16) for i in range(NSLOT)]
        cb = T(st, "cb", [128, 4, 128], BF16)
        identf = T(st, "identf", [128, 128], F32)
        vecsT = T(st, "vecsT", [128, NROWS], F32)
        sc = T(st, "sc", [128, 64], F32)
        dwh = T(st, "dwh", [128, 64], F32)
        ps = [st.enter_context(nc.psum_tensor(f"ps{i}", [128, 512], F32)) for i in range(8)]
        ident = cb[:, 0, :]
        ones = cb[:, 1, :]
        bd64 = cb[:, 2, :]
        negmask = cb[:, 3, :]
        SC_QN, SC_KN, SC_SUB, SC_XQ, SC_XK, SC_NLAM, SC_LNGH, SC_LNBH, SC_EPS6, SC_EPS5 = 0, 1, 2, 3, 5, 7, 8, 16, 30, 31
        banks = Banks(range(8))

        def PSK(b):
            return ("ps", b)

        groups = []

        def G(*parts):
            groups.append(list(parts))
            return len(groups) - 1

        g_kv = [G((0, "kv", i * 512, 512)) for i in range(4)]
        g_xh = [G((0, "in", X_Q + h * 256, 256), (256, "in", X_GATE + h * 256, 256)) for h in range(4)]
        g_xp = [G((0, "xp", i * 512, 512)) for i in range(2)]
        g_m2 = [G((0, "in", MERGE + 2 * D + i * 512, 512)) for i in range(2)]
        g_cab = [G((0, "in", C_GLU + jj * 256, 256), (256, "in", C_GLU + D + jj * 256, 256)) for jj in range(4)]
        g_cg = [G((0, "in", C_GATE + i * 512, 512)) for i in range(2)]
        g_cp = [G((0, "cp", i * 512, 512)) for i in range(2)]
        g_m0 = [G((0, "in", MERGE + i * 512, 512)) for i in range(2)]
        g_dh = [G((0, "in", D_Q + h * 128, 128), (128, "in", D_K + h * 128, 128),
                  (256, "in", D_V + h * 128, 128), (384, "in", D_GATE + h * 128, 128)) for h in range(8)]
        g_dp = [G((0, "dp", i * 512, 512)) for i in range(2)]
        g_m1 = [G((0, "in", MERGE + D + i * 512, 512)) for i in range(2)]
        g_out = [G((0, "out", i * 512, 512)) for i in range(2)]
        order = (g_kv + g_xh + [g_xp[0], g_m2[0], g_xp[1], g_m2[1]] + g_cab + g_cg
                 + [g_cp[0], g_m0[0], g_cp[1], g_m0[1]] + g_dh + [g_dp[0], g_m1[0], g_dp[1], g_m1[1]] + g_out)
        assert sorted(order) == list(range(len(groups)))
        pos_of = {g: i for i, g in enumerate(order)}
        wstate = {"issued": 0}

        def wkeys(slot, c0, n):
            return [("w", slot, q) for q in range(c0 // 128, (c0 + n) // 128)]

        def w_issue_next():
            i = wstate["issued"]
            if i >= len(order):
                return
            g = order[i]
            slot = i % NSLOT
            for (dc, wn, sc0, n) in groups[g]:
                P.dma("pool", wsl[slot][:, :, dc:dc + n], wviews[wn][:, :, sc0:sc0 + n],
                      writes=wkeys(slot, dc, n), chan=f"w{slot}")
            wstate["issued"] = i + 1

        def w_slot(g):
            i = pos_of[g]
            assert i < wstate["issued"], "weight group not issued yet"
            assert i >= wstate["issued"] - NSLOT
            return i % NSLOT

        def w_done(g):
            w_issue_next()

        def mm(out, lhsT, rhs, start, stop, reads, writes, skip=False):
            if skip:
                P.op("pe", lambda e: e.matmul(out, lhsT=lhsT, rhs=rhs, start=start, stop=stop,
                                              skip_group_check=True), reads, writes)
            else:
                P.op("pe", lambda e: e.matmul(out, lhsT=lhsT, rhs=rhs, start=start, stop=stop), reads, writes)

        def act(out, in_, func, reads, writes, scale=1.0, bias=0.0, accum=None):
            if accum is not None:
                P.op("act", lambda e: e.activation(out=out, in_=in_, func=func, scale=scale, bias=bias,
                                                   accum_out=accum), reads, writes)
            else:
                P.op("act", lambda e: e.activation(out=out, in_=in_, func=func, scale=scale, bias=bias),
                     reads, writes)

        def ts(eng, out, in0, s1, s2, op0, op1, reads, writes):
            if s2 is None:
                P.op(eng, lambda e: e.tensor_scalar(out=out, in0=in0, scalar1=s1, scalar2=None, op0=op0),
                     reads, writes)
            else:
                P.op(eng, lambda e: e.tensor_scalar(out=out, in0=in0, scalar1=s1, scalar2=s2, op0=op0, op1=op1),
                     reads, writes)

        def stt(out, in0, scalar, in1, op0, op1, reads, writes):
            P.op("dve", lambda e: e.scalar_tensor_tensor(out=out, in0=in0, scalar=scalar, in1=in1, op0=op0,
                                                         op1=op1), reads, writes)

        def tt(eng, out, in0, in1, op, reads, writes):
            P.op(eng, lambda e: e.tensor_tensor(out=out, in0=in0, in1=in1, op=op), reads, writes)

        def cp(eng, out, in_, reads, writes):
            if eng == "act":
                P.op("act", lambda e: e.copy(out=out, in_=in_), reads, writes)
            else:
                P.op(eng, lambda e: e.tensor_copy(out=out, in_=in_), reads, writes)

        def proj(bank, slot, col0, rhs_fn, rkeys, ncols=128, n=512, accum_extra=None):
            for kc in range(NCH):
                mm(ps[bank][:, 0:n], wsl[slot][:, kc, col0:col0 + 128], rhs_fn(kc), kc == 0, kc == NCH - 1,
                   reads=wkeys(slot, col0, 128) + rkeys, writes=[PSK(bank)])

        def hT_rhs(tb):
            return (lambda kc: hT[:, kc, tb * 512:(tb + 1) * 512]), [("hT", tb)]

        def rsqrt_from_psum(bank, n, scale, eps, vbuf, vkey):
            ecol = sc[:, SC_EPS6:SC_EPS6 + 1] if eps == 1e-6 else sc[:, SC_EPS5:SC_EPS5 + 1]
            act(vbuf[:, 0:n], ps[bank][:, 0:n], AF.Ln, ["sc"], [PSK(bank), vkey], scale=scale, bias=ecol)
            act(vbuf[:, 0:n], vbuf[:, 0:n], AF.Exp, [vkey], [vkey], scale=-0.5)

        state = {"done": False, "inph": False}

        def ph_enter():
            state["inph"] = True

        def ph_exit():
            state["inph"] = False
            if state["done"]:
                raise _Stop()

        def rsqrt_dve_evac(bank, n, scale, eps, vbuf, vkey):
            ts("dve", vbuf[:, 0:n], ps[bank][:, 0:n], scale, eps, ALU.mult, ALU.add, reads=[], writes=[PSK(bank), vkey])
            act(vbuf[:, 0:n], vbuf[:, 0:n], AF.Ln, [vkey], [vkey])
            act(vbuf[:, 0:n], vbuf[:, 0:n], AF.Exp, [vkey], [vkey], scale=-0.5)

        def dump(tag, tensor):
            if DEBUG == tag:
                P.barrier()
                P.dma("sp", dbg_d, tensor[:].rearrange("p a b -> p (a b)"), writes=["dbg"], chan="dbg")
                P.op("sp", None, reads=["dbg"])
                state["done"] = True
                if not state["inph"]:
                    raise _Stop()

        P.dma("pool", cb[:].rearrange("p a b -> p (a b)"), cb_d, writes=["cb"], chan="c0")
        P.dma("sp", identf[:], idf_d, writes=["identf"], chan="c1")
        for i in range(NSLOT):
            w_issue_next()
        P.op("pool", lambda e: e.memset(sc[:, SC_EPS6:SC_EPS6 + 1], 1e-6), writes=["sc"])
        P.op("pool", lambda e: e.memset(sc[:, SC_EPS5:SC_EPS5 + 1], 1e-5), writes=["sc"])
        with contextlib.ExitStack() as ph:
            vst = [T(ph, f"vst{i}", [128, 128], F32) for i in range(3)]
            lamb = T(ph, "lamb", [128, 256], F32)
            lamp = T(ph, "lamp", [128, 128], F32)
            lams = T(ph, "lams", [128, 2], F32)
            rows = [(0, 128), (128, 128), (256, NROWS - 256)]
            for i, (r0, n) in enumerate(rows):
                P.dma("sp", vst[i][0:n, :], vecs_d[r0:r0 + n, :], writes=[("vst", i)], chan="c1")
            P.dma("sp", lamb[:], lam_d.partition_broadcast(128), writes=["lamb"], chan="c1")
            b = banks.alloc()
            for i, (r0, n) in enumerate(rows):
                mm(ps[b][:, r0:r0 + n], vst[i][0:n, :], identf[0:n, 0:n], True, True,
                   reads=[("vst", i), "identf"], writes=[PSK(b)], skip=True)
            cp("dve", vecsT[:], ps[b][:, 0:NROWS], reads=[], writes=[PSK(b), "vecsT"])
            banks.free(b)
            ts("dve", sc[:, SC_QN:SC_QN + 1], vecsT[:, R_QN:R_QN + 1], 0.125, None, ALU.mult, None, ["vecsT"], ["sc"])
            cp("dve", sc[:, SC_KN:SC_KN + 1], vecsT[:, R_KN:R_KN + 1], ["vecsT"], ["sc"])
            ts("dve", sc[:, SC_SUB:SC_SUB + 1], vecsT[:, R_SUB:R_SUB + 1], 0.4, None, ALU.mult, None, ["vecsT"], ["sc"])
            cp("dve", sc[:, SC_XQ:SC_XQ + 2], vecsT[:, R_XQ:R_XQ + 2], ["vecsT"], ["sc"])
            ts("dve", sc[:, SC_XK:SC_XK + 2], vecsT[:, R_XK:R_XK + 2], 1.0 / 16.0, None, ALU.mult, None, ["vecsT"], ["sc"])
            ts("dve", sc[:, SC_LNGH:SC_LNGH + 8], vecsT[:, R_LNG:R_LNG + 8], 0.5, None, ALU.mult, None, ["vecsT"], ["sc"])
            ts("dve", sc[:, SC_LNBH:SC_LNBH + 8], vecsT[:, R_LNB:R_LNB + 8], 0.5, None, ALU.mult, None, ["vecsT"], ["sc"])
            ts("dve", dwh[:], vecsT[:, R_DW + 23 * 8:R_DW + 248], 0.5, None, ALU.mult, None, ["vecsT"], ["dwh"])
            tt("dve", lamp[:], lamb[:, 0:128], lamb[:, 128:256], ALU.mult, ["lamb"], ["lamp"])
            P.op("dve", lambda e: e.reduce_sum(out=lams[:], in_=lamp[:].rearrange("p (a b) -> p a b", a=2),
                                               axis=AX.X), ["lamp"], ["lams"])
            act(lams[:], lams[:], AF.Exp, ["lams"], ["lams"])
            tt("dve", lams[:, 0:1], lams[:, 1:2], lams[:, 0:1], ALU.subtract, ["lams"], ["lams"])
            ts("dve", sc[:, SC_NLAM:SC_NLAM + 1], lams[:, 0:1], -0.2, None, ALU.add, None, ["lams"], ["sc"])
            P.barrier()

        try:

            def branch_out(gp, gm, first, ph):
                thm = [T(ph, f"thm{i}", [128, 512], F32) for i in range(2)]
                tb_ = [T(ph, f"tbo{i}", [128, 512], F32) for i in range(2)]
                r_th = Ring(thm)
                r_t = Ring(tb_)
                for half in range(2):
                    sp_ = w_slot(gp[half])
                    sm_ = w_slot(gm[half])
                    for jj in range(4):
                        j = half * 4 + jj
                        for tb in range(NTB):
                            bp = banks.alloc()
                            for c in range(NCH):
                                mm(ps[bp][:, :], wsl[sp_][:, c, jj * 128:(jj + 1) * 128], actT[:, c, tb * 512:(tb + 1) * 512],
                                   c == 0, c == NCH - 1, reads=wkeys(sp_, jj * 128, 128) + [("act", c, tb)], writes=[PSK(bp)])
                            bm = banks.alloc()
                            rf, rk = hT_rhs(tb)
                            proj(bm, sm_, jj * 128, rf, rk)
                            k1, th = r_th.next()
                            act(th[:], ps[bm][:, :], AF.Tanh, [], [PSK(bm), ("thm", k1)], scale=0.5)
                            banks.free(bm)
                            k2, tbuf = r_t.next()
                            stt(tbuf[:], th[:], 1.0, ps[bp][:, :], ALU.add, ALU.mult, [("thm", k1)], [PSK(bp), ("tbo", k2)])
                            banks.free(bp)
                            ysl = yT[:, j, tb * 512:(tb + 1) * 512]
                            if first:
                                ts("dve", ysl, tbuf[:], 0.5, None, ALU.mult, None, [("tbo", k2)], [("y", j, tb)])
                            else:
                                stt(ysl, tbuf[:], 0.5, ysl, ALU.mult, ALU.add, [("tbo", k2)], [("y", j, tb)])
                    w_done(gp[half])
                    w_done(gm[half])

            with contextlib.ExitStack() as ph:
                ph_enter()
                xt = [T(ph, f"xt{i}", [128, D], F32) for i in range(3)]
                xn = [T(ph, f"xn{i}", [128, D], BF16) for i in range(3)]
                junk = T(ph, "junk", [128, D], BF16)
                gbc = T(ph, "gbc", [128, D], F32)
                gmbc = T(ph, "gmbc", [128, D], F32)
                ssq = T(ph, "ssq", [128, 32], F32)
                memT = T(ph, "memT", [128, NCH, MEM], BF16)
                P.dma("sp", gbc[:], ng_d.partition_broadcast(128), writes=["gbc"], chan="c1")
                P.dma("sp", gmbc[:], mg_d.partition_broadcast(128), writes=["gmbc"], chan="c1")
                tiles = [("m", i) for i in range(2)] + [("x", i) for i in range(NT)]
                for n_, (kind, t) in enumerate(tiles):
                    i = n_ % 3
                    src = (mem_d if kind == "m" else x_d)[t * 128:(t + 1) * 128, :]
                    P.dma("sp", xt[i][:], src, writes=[("xt", i)], chan=f"x{i}")
                    col = ssq[:, n_:n_ + 1]
                    act(junk[:], xt[i][:], AF.Square, [("xt", i)], ["junk"])
                    P.op("dve", (lambda e, col=col: e.reduce_sum(out=col, in_=junk[:], axis=AX.X)), ["junk"], [("ssq", n_)])
                    act(col, col, AF.Ln, [("ssq", n_), "sc"], [("ssq", n_)], scale=1.0 / D, bias=sc[:, SC_EPS6:SC_EPS6 + 1])
                    act(col, col, AF.Exp, [("ssq", n_)], [("ssq", n_)], scale=-0.5)
                    gsrc = gmbc if kind == "m" else gbc
                    stt(xn[i][:], xt[i][:], col, gsrc[:], ALU.mult, ALU.mult,
                        [("xt", i), ("ssq", n_), "gbc", "gmbc"], [("xn", i)])
                    b = banks.alloc()
                    pbf = ps[b][:].bitcast(BF16)
                    for kc in range(NCH):
                        P.op("pe", (lambda e, kc=kc, i=i, pbf=pbf: e.transpose(out=pbf[:, kc * 128:(kc + 1) * 128],
                                                                              in_=xn[i][:, kc * 128:(kc + 1) * 128],
                                                                              identity=ident)),
                             reads=[("xn", i), "cb"], writes=[PSK(b)])
                    src3 = pbf.rearrange("p (a b) -> p a b", a=NCH)
                    if kind == "m":
                        cp("dve", memT[:, :, t * 128:(t + 1) * 128], src3, [], [PSK(b), "memT"])
                    else:
                        cp("dve", hT[:, :, t * 128:(t + 1) * 128], src3, [], [PSK(b), ("hT", t // 4)])
                    banks.free(b)
                with contextlib.ExitStack() as ph:
                    ph_enter()
                    kT = T(ph, "kT", [128, NCH, MEM], BF16)
                    Vx = T(ph, "Vx", [128, 2, D], BF16)
                    qT = T(ph, "qT", [128, 2, S], BF16)
                    sqb = [T(ph, f"xsq{i}", [128, 512], BF16) for i in range(4)]
                    vb = [T(ph, f"xvb{i}", [128, 512], F32) for i in range(2)]
                    PT = [T(ph, f"xPT{i}", [128, 512], BF16) for i in range(4)]
                    thg = [T(ph, f"xthg{i}", [128, 512], F32) for i in range(2)]
                    gsb = [T(ph, f"xgs{i}", [128, 512], F32) for i in range(2)]
                    rlb = [T(ph, f"xrl{i}", [128, 512], F32) for i in range(2)]
                    tob = [T(ph, f"xto{i}", [128, 512], F32) for i in range(2)]
                    r_sq, r_vb, r_PT, r_thg, r_gs, r_rl, r_to = (Ring(sqb), Ring(vb), Ring(PT), Ring(thg), Ring(gsb),
                                                                 Ring(rlb), Ring(tob))
                    for hx in range(4):
                        bks = []
                        sqs = []
                        for dc in range(2):
                            c = hx * 2 + dc
                            g = g_kv[c // 4]
                            sl = w_slot(g)
                            b = banks.alloc()
                            proj(b, sl, (c % 4) * 128, lambda kc: memT[:, kc, :], ["memT"], n=MEM)
                            k_, sq = r_sq.next()
                            act(sq[:, 0:MEM], ps[b][:, 0:MEM], AF.Square, [], [PSK(b), ("xsq", k_)])
                            bks.append(b)
                            sqs.append((k_, sq))
                            if c == 3:
                                w_done(g_kv[0])
                            if c == 7:
                                w_done(g_kv[1])
                        bs = banks.alloc()
                        for dc in range(2):
                            mm(ps[bs][:, 0:MEM], ones, sqs[dc][1][:, 0:MEM], dc == 0, dc == 1, ["cb", ("xsq", sqs[dc][0])], [PSK(bs)])
                        kv_, v = r_vb.next()
                        rsqrt_from_psum(bs, MEM, 1.0 / 256.0, 1e-6, v, ("xvb", kv_))
                        banks.free(bs)
                        for dc in range(2):
                            c = hx * 2 + dc
                            stt(kT[:, c, :], ps[bks[dc]][:, 0:MEM], sc[:, SC_XK + dc:SC_XK + dc + 1], v[:, 0:MEM], ALU.mult, ALU.mult,
                                ["sc", ("xvb", kv_)], [PSK(bks[dc]), ("kT", c)])
                            banks.free(bks[dc])
                    for vg in range(2):
                        sl = w_slot(g_kv[2 + vg])
                        for mt in range(2):
                            b = banks.alloc()
                            for kc in range(NCH):
                                mm(ps[b][:, :], memT[:, kc, mt * 128:(mt + 1) * 128], wsl[sl][:, kc, :], kc == 0, kc == NCH - 1,
                                   ["memT"] + wkeys(sl, 0, 512), [PSK(b)])
                            cp("dve", Vx[:, mt, vg * 512:(vg + 1) * 512], ps[b][:, :], [], [PSK(b), ("Vx", mt, vg)])
                            banks.free(b)
                        w_done(g_kv[2 + vg])
                    for hx in range(4):
                        sl = w_slot(g_xh[hx])
                        pend = None

                        def q_finish(pend):
                            tb, bq, sqs = pend
                            bs = banks.alloc()
                            for dc in range(2):
                                mm(ps[bs][:, :], ones, sqs[dc][1][:], dc == 0, dc == 1, ["cb", ("xsq", sqs[dc][0])], [PSK(bs)])
                            kv_, v = r_vb.next()
                            rsqrt_from_psum(bs, 512, 1.0 / 256.0, 1e-6, v, ("xvb", kv_))
                            banks.free(bs)
                            for dc in range(2):
                                stt(qT[:, dc, tb * 512:(tb + 1) * 512], ps[bq[dc]][:, :], sc[:, SC_XQ + dc:SC_XQ + dc + 1], v[:],
                                    ALU.mult, ALU.mult, ["sc", ("xvb", kv_)], [PSK(bq[dc]), ("qT", dc, tb)])
                                banks.free(bq[dc])

                        for tb in range(NTB):
                            bq = []
                            sqs = []
                            rf, rk = hT_rhs(tb)
                            for dc in range(2):
                                b = banks.alloc()
                                proj(b, sl, dc * 128, rf, rk)
                                k_, sq = r_sq.next()
                                act(sq[:], ps[b][:, :], AF.Square, [], [PSK(b), ("xsq", k_)])
                                bq.append(b)
                                sqs.append((k_, sq))
                            if pend is not None:
                                q_finish(pend)
                            pend = (tb, bq, sqs)
                        q_finish(pend)
                        for tb in range(NTB):
                            pts = []
                            for mt in range(2):
                                b = banks.alloc()
                                for dc in range(2):
                                    mm(ps[b][:, :], kT[:, hx * 2 + dc, mt * 128:(mt + 1) * 128], qT[:, dc, tb * 512:(tb + 1) * 512],
                                       dc == 0, dc == 1, [("kT", hx * 2 + dc), ("qT", dc, tb)], [PSK(b)])
                                kp, pt = r_PT.next()
                                act(pt[:], ps[b][:, :], AF.Exp, [], [PSK(b), ("xPT", kp)])
                                banks.free(b)
                                pts.append((kp, pt))
                            bo = []
                            for vc in range(2):
                                b = banks.alloc()
                                for mt in range(2):
                                    c0 = hx * 256 + vc * 128
                                    mm(ps[b][:, :], Vx[:, mt, c0:c0 + 128], pts[mt][1][:], mt == 0, mt == 1,
                                       [("Vx", mt, c0 // 512), ("xPT", pts[mt][0])], [PSK(b)])
                                bo.append(b)
                            bl = banks.alloc()
                            for mt in range(2):
                                mm(ps[bl][:, :], ones, pts[mt][1][:], mt == 0, mt == 1, ["cb", ("xPT", pts[mt][0])], [PSK(bl)])
                            bg = []
                            rf, rk = hT_rhs(tb)
                            for vc in range(2):
                                b = banks.alloc()
                                proj(b, sl, 256 + vc * 128, rf, rk)
                                bg.append(b)
                            kr, rl = r_rl.next()
                            act(rl[:], ps[bl][:, :], AF.Ln, [], [PSK(bl), ("xrl", kr)])
                            banks.free(bl)
                            act(rl[:], rl[:], AF.Exp, [("xrl", kr)], [("xrl", kr)], scale=-1.0)
                            for vc in range(2):
                                kt_, th = r_thg.next()
                                act(th[:], ps[bg[vc]][:, :], AF.Tanh, [], [PSK(bg[vc]), ("xthg", kt_)], scale=0.5)
                                kg, gs = r_gs.next()
                                stt(gs[:], th[:], 1.0, ps[bg[vc]][:, :], ALU.add, ALU.mult, [("xthg", kt_)], [PSK(bg[vc]), ("xgs", kg)])
                                banks.free(bg[vc])
                                ko, to = r_to.next()
                                tt("dve", to[:], ps[bo[vc]][:, :], rl[:], ALU.mult, [("xrl", kr)], [PSK(bo[vc]), ("xto", ko)])
                                banks.free(bo[vc])
                                stt(actT[:, hx * 2 + vc, tb * 512:(tb + 1) * 512], to[:], 0.5, gs[:], ALU.mult, ALU.mult,
                                    [("xto", ko), ("xgs", kg)], [("act", hx * 2 + vc, tb)])
                        w_done(g_xh[hx])
                    dump("actx", actT)
                    if not state["done"]:
                        branch_out(g_xp, g_m2, True, ph)
                    P.barrier()
            ph_exit()
            dump("yx", yT)

            with contextlib.ExitStack() as ph:
                ph_enter()
                PADW = 30
                ub = [T(ph, f"ub{i}", [128, PADW + S], BF16) for i in range(2)]
                dg = [T(ph, f"dg{i}", [128, 31, 128], BF16) for i in range(2)]
                thb = [T(ph, f"cth{i}", [128, 512], F32) for i in range(2)]
                Ms = [T(ph, f"cM{i}", [128, 512], F32) for i in range(NTB)]
                Rs = [T(ph, f"cR{i}", [128, 512], F32) for i in range(NTB)]
                zt = [T(ph, f"cz{i}", [128, 512], F32) for i in range(2)]
                sqc = [T(ph, f"csq{i}", [128, 512], BF16) for i in range(2)]
                gsc = [T(ph, f"cgs{i}", [128, 512], F32) for i in range(2)]
                N_PE_TAPS = 23
                accb = [T(ph, f"cacc{i}", [128, 512], F32) for i in range(2)]
                r_acc = Ring(accb)
                vvb = [T(ph, f"cv{i}", [128, 512], F32) for i in range(2)]
                r_v = Ring(vvb)
                r_th, r_z, r_sq, r_gs = Ring(thb), Ring(zt), Ring(sqc), Ring(gsc)
                for i in range(2):
                    P.op("pool", (lambda e, i=i: e.memset(ub[i][:, 0:PADW], 0.0)), writes=[("ub", i, -1)])
                for j in range(NCH):
                    g = g_cab[j // 2]
                    sl = w_slot(g)
                    jl = j % 2
                    ui = j % 2
                    for k in range(N_PE_TAPS):
                        col = vecsT[:, R_DW + k * 8 + j:R_DW + k * 8 + j + 1]
                        P.op("pool", (lambda e, k=k, ui=ui, col=col: e.tensor_scalar(out=dg[ui][:, k, :], in0=ident, scalar1=col,
                                                                                     scalar2=0.5, op0=ALU.mult, op1=ALU.mult)),
                             reads=["cb", "vecsT"], writes=[("dg", ui)])
                    pend = None

                    def conv_block(tb, ui=ui, j=j):
                        b = banks.alloc()
                        rk = [("ub", ui, tb), ("ub", ui, tb - 1), ("dg", ui)]
                        for k in range(N_PE_TAPS):
                            mm(ps[b][:, :], dg[ui][:, k, :], ub[ui][:, tb * 512 + k:tb * 512 + k + 512], k == 0, k == N_PE_TAPS - 1, rk, [PSK(b)])
                        ka, acc = r_acc.next()
                        for k in range(N_PE_TAPS, 31):
                            ush = ub[ui][:, tb * 512 + k:tb * 512 + k + 512]
                            col = dwh[:, (k - 23) * 8 + j:(k - 23) * 8 + j + 1]
                            if k == N_PE_TAPS:
                                ts("dve", acc[:], ush, col, None, ALU.mult, None, [("ub", ui, tb), ("ub", ui, tb - 1), "dwh"], [("cacc", ka)])
                            else:
                                stt(acc[:], ush, col, acc[:], ALU.mult, ALU.add, [("ub", ui, tb), ("ub", ui, tb - 1), "dwh"], [("cacc", ka)])
                        stt(actT[:, j, tb * 512:(tb + 1) * 512], ps[b][:, :], vecsT[:, R_DWB + j:R_DWB + j + 1], acc[:], ALU.add, ALU.add,
                            ["vecsT", ("cacc", ka)], [PSK(b), ("act", j, tb)])
                        banks.free(b)

                    for tb in range(NTB):
                        rf, rk = hT_rhs(tb)
                        ba = banks.alloc()
                        proj(ba, sl, jl * 128, rf, rk)
                        bb = banks.alloc()
                        proj(bb, sl, 256 + jl * 128, rf, rk)
                        kt_, th = r_th.next()
                        act(th[:], ps[bb][:, :], AF.Tanh, [], [PSK(bb), ("cth", kt_)], scale=0.5)
                        banks.free(bb)
                        stt(ub[ui][:, PADW + tb * 512:PADW + (tb + 1) * 512], th[:], 1.0, ps[ba][:, :], ALU.add, ALU.mult,
                            [("cth", kt_)], [PSK(ba), ("ub", ui, tb)])
                        banks.free(ba)
                        if pend is not None:
                            conv_block(pend)
                        pend = tb
                    conv_block(pend)
                    if jl == 1:
                        w_done(g)
                dump("conv", actT)
                for tb in range(NTB):
                    bs = banks.alloc()
                    bq = banks.alloc()
                    for j in range(NCH):
                        a = actT[:, j, tb * 512:(tb + 1) * 512]
                        mm(ps[bs][:, :], ones, a, j == 0, j == NCH - 1, ["cb", ("act", j, tb)], [PSK(bs)])
                        ks, sq = r_sq.next()
                        act(sq[:], a, AF.Square, [("act", j, tb)], [("csq", ks)])
                        mm(ps[bq][:, :], ones, sq[:], j == 0, j == NCH - 1, ["cb", ("csq", ks)], [PSK(bq)])
                    M, R = Ms[tb], Rs[tb]
                    ts("dve", M[:], ps[bs][:, :], 1.0 / D, None, ALU.mult, None, [], [PSK(bs), ("cM", tb)])
                    banks.free(bs)
                    tt("dve", R[:], M[:], M[:], ALU.mult, [("cM", tb)], [("cR", tb)])
                    stt(R[:], ps[bq][:, :], 1.0 / D, R[:], ALU.mult, ALU.subtract, [], [PSK(bq), ("cR", tb)])
                    banks.free(bq)
                    act(R[:], R[:], AF.Ln, [("cR", tb), "sc"], [("cR", tb)], scale=1.0, bias=sc[:, SC_EPS5:SC_EPS5 + 1])
                    act(R[:], R[:], AF.Exp, [("cR", tb)], [("cR", tb)], scale=-0.5)
                    stt(M[:], M[:], -1.0, R[:], ALU.mult, ALU.mult, [("cR", tb)], [("cM", tb)])
                for half in range(2):
                    slg = w_slot(g_cg[half])
                    for jj in range(4):
                        j = half * 4 + jj
                        for tb in range(NTB):
                            a = actT[:, j, tb * 512:(tb + 1) * 512]
                            kz, z = r_z.next()
                            tt("dve", z[:], a, Rs[tb][:], ALU.mult, [("act", j, tb), ("cR", tb)], [("cz", kz)])
                            tt("dve", z[:], z[:], Ms[tb][:], ALU.add, [("cM", tb)], [("cz", kz)])
                            kt_, th = r_th.next()
                            act(th[:], z[:], AF.Tanh, [("cz", kz), "sc"], [("cth", kt_)],
                                scale=sc[:, SC_LNGH + j:SC_LNGH + j + 1], bias=sc[:, SC_LNBH + j:SC_LNBH + j + 1])
                            kv2, vv = r_v.next()
                            act(vv[:], z[:], AF.Identity, [("cz", kz), "vecsT"], [("cv", kv2)],
                                scale=vecsT[:, R_LNG + j:R_LNG + j + 1], bias=vecsT[:, R_LNB + j:R_LNB + j + 1])
                            stt(z[:], th[:], 1.0, vv[:], ALU.add, ALU.mult, [("cth", kt_), ("cv", kv2)], [("cz", kz)])
                            bg = banks.alloc()
                            rf, rk = hT_rhs(tb)
                            proj(bg, slg, jj * 128, rf, rk)
                            kt2, th2 = r_th.next()
                            act(th2[:], ps[bg][:, :], AF.Tanh, [], [PSK(bg), ("cth", kt2)], scale=0.5)
                            kg, gs = r_gs.next()
                            stt(gs[:], th2[:], 1.0, ps[bg][:, :], ALU.add, ALU.mult, [("cth", kt2)], [PSK(bg), ("cgs", kg)])
                            banks.free(bg)
                            stt(a, z[:], 0.25, gs[:], ALU.mult, ALU.mult, [("cz", kz), ("cgs", kg)], [("act", j, tb)])
                    w_done(g_cg[half])
                dump("actc", actT)
                if not state["done"]:
                    branch_out(g_cp, g_m0, False, ph)
                P.barrier()
            ph_exit()
            dump("yc", yT)

            with contextlib.ExitStack() as ph:
                ph_enter()
                QL = [T(ph, f"QL{i}", [128, S], BF16) for i in range(2)]
                QU = [T(ph, f"QU{i}", [128, S], BF16) for i in range(2)]
                KL = [T(ph, f"KL{i}", [128, S], BF16) for i in range(2)]
                KU = [T(ph, f"KU{i}", [128, S], BF16) for i in range(2)]
                Vd = [T(ph, f"Vd{i}", [128, NT, 128], BF16) for i in range(2)]
                gsd = [T(ph, f"gsd{i}", [128, S], BF16) for i in range(2)]
                PTd = [[T(ph, f"PT{m}_{i}", [128, 512], BF16) for i in range(3)] for m in range(2)]
                sqd = [T(ph, f"dsq{i}", [128, 512], BF16) for i in range(2)]
                thd = [T(ph, f"dth{i}", [128, 512], F32) for i in range(1)]
                vbd = [T(ph, f"dvb{i}", [128, 512], F32) for i in range(2)]
                E1 = [T(ph, f"E1_{i}", [128, 512], F32) for i in range(2)]
                E2 = [T(ph, f"E2_{i}", [128, 512], F32) for i in range(2)]
                sqe = [T(ph, f"sqe{i}", [128, 512], BF16) for i in range(2)]
                RL = [T(ph, f"RL{i}", [128, 512], F32) for i in range(2)]
                r_PT = [Ring(PTd[0]), Ring(PTd[1])]
                rawd = [T(ph, f"draw{i}", [128, 512], F32) for i in range(3)]
                r_raw = Ring(rawd)
                r_sq, r_th, r_vb = Ring(sqd), Ring(thd), Ring(vbd)
                for i in range(2):
                    P.op("pool", (lambda e, i=i: e.memset(QU[i][0:64, :], 0.0)), writes=[("QUa", i)])
                    P.op("pool", (lambda e, i=i: e.memset(KU[i][0:64, :], 0.0)), writes=[("KUa", i)])
                O_B = [4, 6]
                L_B = [5, 7]
                for b in (4, 5, 6, 7):
                    banks.busy.add(b)
                dbanks = Banks([0, 1, 2, 3])

                def qk_unit(h, which, tb):
                    s_ = h % 2
                    st_ = {}
                    c0 = 0 if which == "q" else 128
                    gcol = sc[:, SC_QN:SC_QN + 1] if which == "q" else sc[:, SC_KN:SC_KN + 1]

                    def st0():
                        sl = w_slot(g_dh[h])
                        rf, rk = hT_rhs(tb)
                        b = dbanks.alloc()
                        proj(b, sl, c0, rf, rk)
                        st_["kraw"], st_["raw"] = r_raw.next()
                        cp("dve", st_["raw"][:], ps[b][:, :], [], [PSK(b), ("draw", st_["kraw"])])
                        dbanks.free(b)
                        st_["ks"], sq = r_sq.next()
                        tt("dve", sq[:], st_["raw"][:], st_["raw"][:], ALU.mult, [("draw", st_["kraw"])], [("dsq", st_["ks"])])

                    def st1():
                        bs = dbanks.alloc()
                        mm(ps[bs][:, :], bd64, sqd[st_["ks"]][:], True, True, ["cb", ("dsq", st_["ks"])], [PSK(bs)])
                        st_["kv"], st_["v"] = r_vb.next()
                        ts("dve", st_["v"][:], ps[bs][:, :], 1.0 / 64.0, 1e-6, ALU.mult, ALU.add, [], [PSK(bs), ("dvb", st_["kv"])])
                        dbanks.free(bs)

                    def st2():
                        v, vk = st_["v"], ("dvb", st_["kv"])
                        act(v[:], v[:], AF.Ln, [vk], [vk])
                        act(v[:], v[:], AF.Exp, [vk], [vk], scale=-0.5)

                    def st3():
                        v, vk = st_["v"], ("dvb", st_["kv"])
                        raw, rk_ = st_["raw"], ("draw", st_["kraw"])
                        lo = (QL if which == "q" else KL)[s_]
                        up = (QU if which == "q" else KU)[s_]
                        sl_ = slice(tb * 512, (tb + 1) * 512)
                        stt(lo[0:64, sl_], raw[0:64, :], gcol[0:64, :], v[0:64, :], ALU.mult, ALU.mult,
                            ["sc", vk, rk_], [(which + "L", s_, tb)])
                        stt(up[64:128, sl_], raw[64:128, :], gcol[64:128, :], v[64:128, :], ALU.mult, ALU.mult,
                            ["sc", vk, rk_], [(which + "U", s_, tb)])

                    return [st0, st1, st2, st3]

                def v_unit(h, g4):
                    s_ = h % 2

                    def st0():
                        sl = w_slot(g_dh[h])
                        b = dbanks.alloc()
                        for tl in range(4):
                            t = g4 * 4 + tl
                            for kc in range(NCH):
                                mm(ps[b][:, tl * 128:(tl + 1) * 128], hT[:, kc, t * 128:(t + 1) * 128], wsl[sl][:, kc, 256:384],
                                   kc == 0, kc == NCH - 1, [("hT", g4)] + wkeys(sl, 256, 128), [PSK(b)], skip=True)
                        cp("dve", Vd[s_][:, g4 * 4:(g4 + 1) * 4, :], ps[b][:, :].rearrange("p (a b) -> p a b", a=4), [],
                           [PSK(b), ("Vd", s_, g4)])
                        dbanks.free(b)

                    return [st0]

                def g_unit(h, tb, last):
                    s_ = h % 2
                    st_ = {}

                    def st0():
                        sl = w_slot(g_dh[h])
                        rf, rk = hT_rhs(tb)
                        b = dbanks.alloc()
                        proj(b, sl, 384, rf, rk)
                        st_["kraw"], st_["raw"] = r_raw.next()
                        cp("dve", st_["raw"][:], ps[b][:, :], [], [PSK(b), ("draw", st_["kraw"])])
                        dbanks.free(b)
                        if last:
                            w_done(g_dh[h])

                    def st1():
                        st_["kt"], st_["th"] = r_th.next()
                        act(st_["th"][:], st_["raw"][:], AF.Tanh, [("draw", st_["kraw"])], [("dth", st_["kt"])], scale=0.5)

                    def st2():
                        stt(gsd[s_][:, tb * 512:(tb + 1) * 512], st_["th"][:], 1.0, st_["raw"][:], ALU.add, ALU.mult,
                            [("dth", st_["kt"]), ("draw", st_["kraw"])], [("gsd", s_, tb)])

                    return [st0, st1, st2]

                class Prologue:
                    def __init__(self, h, period=2):
                        self.h = h
                        s_ = h % 2
                        self.units = ([qk_unit(h, "q", tb) for tb in range(NTB)] + [qk_unit(h, "k", tb) for tb in range(NTB)]
                                      + [v_unit(h, g4) for g4 in range(4)] + [g_unit(h, tb, tb == NTB - 1) for tb in range(NTB)])
                        self.active = []
                        self.t = 0
                        self.period = period
                        self.started = False

                    def tick(self):
                        h = self.h
                        s_ = h % 2
                        if not self.started:
                            self.started = True
                            P.dma("pool", QL[s_][64:68, :], qaug_d[h], writes=[("QLa", s_)], chan=f"aug{s_}")
                            P.dma("pool", QU[s_][0:4, :], qaug_d[h], writes=[("QUa", s_)], chan=f"aug{s_}")
                            P.dma("pool", KL[s_][64:68, :], kaug_d[h], writes=[("KLa", s_)], chan=f"aug{s_}")
                            P.dma("pool", KU[s_][0:4, :], kaug_d[h], writes=[("KUa", s_)], chan=f"aug{s_}")
                        if self.t % self.period == 0 and self.units:
                            self.active.append(self.units.pop(0))
                        self.t += 1
                        for u in list(self.active):
                            u.pop(0)()
                            if not u:
                                self.active.remove(u)

                    def done(self):
                        return not self.units and not self.active

                    def flush(self):
                        while not self.done():
                            self.tick()

                def scores(h, qb, kt):
                    s_ = h % 2
                    r = kt - 4 * qb
                    c0 = 128 * r if r > 0 else 0
                    n = 512 - c0
                    outp = []
                    for m in range(2):
                        b = dbanks.alloc()
                        if m == 0:
                            lhsT = KL[s_][0:68, kt * 128:(kt + 1) * 128]
                            rhs = QL[s_][0:68, qb * 512 + c0:(qb + 1) * 512]
                            rk = [("kL", s_, kt // 4), ("KLa", s_), ("qL", s_, qb), ("QLa", s_)]
                        else:
                            lhsT = KU[s_][:, kt * 128:(kt + 1) * 128]
                            rhs = QU[s_][:, qb * 512 + c0:(qb + 1) * 512]
                            rk = [("kU", s_, kt // 4), ("KUa", s_), ("qU", s_, qb), ("QUa", s_)]
                        mm(ps[b][:, c0:512], lhsT, rhs, True, r < 0, rk, [PSK(b)], skip=True)
                        if r >= 0:
                            mm(ps[b][:, c0:c0 + 128], ident, negmask, False, True, ["cb"], [PSK(b)], skip=True)
                        kp, pt = r_PT[m].next()
                        act(pt[:, c0:512], ps[b][:, c0:512], AF.Exp, [], [PSK(b), ("PT", m, kp)])
                        dbanks.free(b)
                        outp.append((kp, pt))
                    return (kt, c0, outp)

                def av(h, qb, sc_, first, last):
                    s_ = h % 2
                    kt, c0, outp = sc_
                    for m in range(2):
                        kp, pt = outp[m]
                        mm(ps[O_B[m]][:, c0:512], Vd[s_][:, kt, :], pt[:, c0:512], first, last,
                           [("Vd", s_, kt // 4), ("PT", m, kp)], [PSK(O_B[m])], skip=True)
                        mm(ps[L_B[m]][:, c0:512], ones, pt[:, c0:512], first, last,
                           ["cb", ("PT", m, kp)], [PSK(L_B[m])], skip=True)

                def epiA(h, qb, e):
                    e1, e2 = E1[e], E2[e]
                    cp("dve", e1[:], ps[O_B[0]][:, :], [], [PSK(O_B[0]), ("E1", e)])
                    act(RL[0][:], ps[L_B[0]][:, :], AF.Ln, [], [PSK(L_B[0]), ("RL", 0)])
                    cp("dve", e2[:], ps[O_B[1]][:, :], [], [PSK(O_B[1]), ("E2", e)])
                    act(RL[1][:], ps[L_B[1]][:, :], AF.Ln, [], [PSK(L_B[1]), ("RL", 1)])

                def epiB(h, qb, e):
                    e1, e2 = E1[e], E2[e]
                    act(RL[0][:], RL[0][:], AF.Exp, [("RL", 0)], [("RL", 0)], scale=-1.0)
                    act(RL[1][:], RL[1][:], AF.Exp, [("RL", 1)], [("RL", 1)], scale=-1.0)
                    tt("dve", e1[:], e1[:], RL[0][:], ALU.mult, [("RL", 0)], [("E1", e)])
                    tt("dve", e2[:], e2[:], RL[1][:], ALU.mult, [("RL", 1)], [("E2", e)])
                    stt(e1[:], e2[:], sc[:, SC_NLAM:SC_NLAM + 1], e1[:], ALU.mult, ALU.add, ["sc", ("E2", e)], [("E1", e)])

                def epiC(h, qb, e):
                    tt("dve", sqe[e][:], E1[e][:], E1[e][:], ALU.mult, [("E1", e)], [("sqe", e)])

                def epiD(h, qb, e):
                    s_ = h % 2
                    e1, e2 = E1[e], E2[e]
                    bs = dbanks.alloc()
                    mm(ps[bs][:, :], ones, sqe[e][:], True, True, ["cb", ("sqe", e)], [PSK(bs)])
                    rsqrt_dve_evac(bs, 512, 1.0 / 128.0, 1e-6, e2, ("E2", e))
                    dbanks.free(bs)
                    tt("dve", e1[:], e1[:], e2[:], ALU.mult, [("E2", e)], [("E1", e)])
                    stt(actT[:, h, qb * 512:(qb + 1) * 512], e1[:], sc[:, SC_SUB:SC_SUB + 1], gsd[s_][:, qb * 512:(qb + 1) * 512],
                        ALU.mult, ALU.mult, ["sc", ("E1", e), ("gsd", s_, qb)], [("act", h, qb)])

                ecount = 0
                pend = []
                pro = Prologue(0)
                pro.flush()
                for h in range(8):
                    pro = Prologue(h + 1) if h + 1 < 8 else None
                    for qb in range(NTB):
                        kts = list(range(4 * qb + 4))
                        cur = scores(h, qb, kts[0])
                        for i, kt in enumerate(kts):
                            nxt = scores(h, qb, kts[i + 1]) if i + 1 < len(kts) else None
                            av(h, qb, cur, i == 0, i == len(kts) - 1)
                            cur = nxt
                            if pend:
                                pend.pop(0)()
                            if pro is not None and not pro.done():
                                pro.tick()
                        assert not pend
                        e = ecount % 2
                        ecount += 1
                        epiA(h, qb, e)
                        pend = [(lambda h=h, qb=qb, e=e: epiB(h, qb, e)), (lambda h=h, qb=qb, e=e: epiC(h, qb, e)),
                                (lambda h=h, qb=qb, e=e: epiD(h, qb, e))]
                    if pro is not None:
                        pro.flush()
                while pend:
                    pend.pop(0)()
                for b in (4, 5, 6, 7):
                    banks.free(b)
                dump("actd", actT)
                P.barrier()
            ph_exit()
            with contextlib.ExitStack() as ph:
                ph_enter()
                branch_out(g_dp, g_m1, False, ph)
                P.barrier()

            ph_exit()
            with contextlib.ExitStack() as ph:
                ph_enter()
                xt = [T(ph, f"fx{i}", [128, D], F32) for i in range(2)]
                ot = [T(ph, f"fo{i}", [128, D], F32) for i in range(2)]
                s0 = w_slot(g_out[0])
                s1 = w_slot(g_out[1])
                for t in range(NT):
                    i = t % 2
                    P.dma("sp", xt[i][:], x_d[t * 128:(t + 1) * 128, :], writes=[("fx", i)], chan=f"x{i}")
                    for half in range(2):
                        sl = (s0, s1)[half]
                        b = banks.alloc()
                        for dc in range(NCH):
                            mm(ps[b][:, :], yT[:, dc, t * 128:(t + 1) * 128], wsl[sl][:, dc, :], dc == 0, dc == NCH - 1,
                               [("y", dc, t // 4)] + wkeys(sl, 0, 512), [PSK(b)])
                        tt("dve", ot[i][:, half * 512:(half + 1) * 512], ps[b][:, :], xt[i][:, half * 512:(half + 1) * 512], ALU.add,
                           [("fx", i)], [PSK(b), ("fo", i, half)])
                        banks.free(b)
                    P.dma("sp", out_d[t * 128:(t + 1) * 128, :], ot[i][:], reads=[("fo", i, 0), ("fo", i, 1)],
                          writes=[("fo_st", i)], chan=f"o{i}")
                P.op("sp", None, reads=[("fo_st", 0), ("fo_st", 1)])
            ph_exit()
        except _Stop:
            pass
        except Exception:
            import traceback
            traceback.print_exc()
            raise
        P.emit(st)
    return nc


def _host_constants():
    ident = np.eye(128, dtype=np.float32)
    ones = np.ones((128, 128), np.float32)
    p = np.arange(128)
    bd64 = (p[:, None] // 64 == p[None, :] // 64).astype(np.float32)
    negmask = np.where(p[None, :] < p[:, None], -30000.0, 0.0).astype(np.float32)
    cb = np.concatenate([ident, ones, bd64, negmask], axis=1)
    tok = np.arange(S)
    il = (tok % 128).astype(np.float32)
    ib = (tok // 128).astype(np.float32)
    qaug = np.zeros((8, 4, S), np.float32)
    kaug = np.zeros((8, 4, S), np.float32)
    for h in range(8):
        slope = 2.0 ** (-(h + 1))
        qaug[h, 0] = 1.0
        qaug[h, 1] = -slope * il
        qaug[h, 2] = 1.0
        qaug[h, 3] = -slope * 128.0 * ib
        kaug[h, 0] = slope * il
        kaug[h, 1] = 1.0
        kaug[h, 2] = slope * 128.0 * ib
        kaug[h, 3] = 1.0
    return cb, ident, qaug, kaug


_NC_CACHE = {}


def kernel(x, mem, norm_g, mem_norm_g, w_in, conv_dw, conv_dw_b, conv_ln_g, conv_ln_b, w_conv_proj,
           diff_qn_g, diff_kn_g, lambda_q1, lambda_k1, lambda_q2, lambda_k2, diff_subln_g, w_diff_proj,
           w_mem_kv, x_qn_g, x_kn_g, w_x_proj, w_out):
    f = lambda a: np.ascontiguousarray(np.asarray(a, dtype=np.float32))
    x = f(x); mem = f(mem)
    B = x.shape[0]
    vecs = np.concatenate([
        f(conv_dw_b)[0].reshape(8, 128), f(conv_ln_g)[0].reshape(8, 128), f(conv_ln_b)[0].reshape(8, 128),
        f(conv_dw)[0].reshape(31 * 8, 128),
        np.concatenate([f(diff_qn_g)[0], f(diff_qn_g)[0]])[None, :],
        np.concatenate([f(diff_kn_g)[0], f(diff_kn_g)[0]])[None, :],
        f(diff_subln_g)[0][None, :],
        f(x_qn_g)[0].reshape(2, 128), f(x_kn_g)[0].reshape(2, 128)], axis=0)
    assert vecs.shape == (NROWS, 128)
    lamv = np.concatenate([f(lambda_q1)[0], f(lambda_q2)[0], f(lambda_k1)[0], f(lambda_k2)[0]])[None, :]
    cb, identf, qaug, kaug = _host_constants()
    shared = {
        "w_in": f(w_in)[0], "w_conv_proj": f(w_conv_proj)[0], "w_diff_proj": f(w_diff_proj)[0],
        "w_x_proj": f(w_x_proj)[0], "w_out": f(w_out)[0], "w_mem_kv": f(w_mem_kv)[0],
        "norm_g": f(norm_g), "mem_norm_g": f(mem_norm_g), "vecs": np.ascontiguousarray(vecs),
        "lamv": np.ascontiguousarray(lamv), "cbits": cb, "identf": identf, "qaug": qaug, "kaug": kaug,
    }
    if "nc" not in _NC_CACHE:
        _NC_CACHE["nc"] = build_program()
    nc = _NC_CACHE["nc"]
    in_maps = [dict(shared, x=x[b], mem=mem[b]) for b in range(B)]
    res = run_bass_kernel_spmd(nc, in_maps, core_ids=list(range(B)))
    out = np.stack([np.asarray(r["out"]) for r in res.results], axis=0).astype(np.float32)
    if DEBUG is not None:
        kernel.dbg = [np.asarray(r["dbg"]) for r in res.results]
    return out
```

```python
import contextlib
import numpy as np
import concourse.bass as bass
import concourse.mybir as mybir
from concourse.bass_utils import run_bass_kernel_spmd

F32 = mybir.dt.float32
BF16 = mybir.dt.bfloat16
AF = mybir.ActivationFunctionType
ALU = mybir.AluOpType
AX = mybir.AxisListType

D = 1024
S = 2048
MEM = 256
NCH = 8
NTB = 4
NT = 16
IN_COLS = 12288
C_GLU, C_GATE, D_Q, D_K, D_V, D_GATE, X_Q, X_GATE, MERGE = 0, 2048, 3072, 4096, 5120, 6144, 7168, 8192, 9216
NSLOT = 3
ENGINES = ("pe", "act", "dve", "pool", "sp")

R_DWB, R_LNG, R_LNB, R_DW, R_QN, R_KN, R_SUB, R_XQ, R_XK, NROWS = 0, 8, 16, 24, 272, 273, 274, 275, 277, 279

DEBUG = None


class _Stop(Exception):
    pass


class Prog:
    def __init__(self, nc):
        self.nc = nc
        self.ins = []
        self.last_w = {}
        self.readers = {}
        self.chan_count = {}
        self.last_on = {}

    def _add(self, eng, fn, reads, writes, dma_chan=None, extra_deps=()):
        idx = len(self.ins)
        deps = set(extra_deps)
        for r in reads:
            w = self.last_w.get(r)
            if w is not None:
                deps.add(w)
            if fn is not None:
                self.readers.setdefault(r, []).append(idx)
        for w_ in writes:
            w = self.last_w.get(w_)
            if w is not None:
                deps.add(w)
            for rd in self.readers.get(w_, ()):
                if rd != idx:
                    deps.add(rd)
            self.last_w[w_] = idx
            self.readers[w_] = []
        rec = dict(eng=eng, fn=fn, dma_chan=dma_chan, mark=False)
        waits = []
        for d in deps:
            dr = self.ins[d]
            if dr["dma_chan"] is not None:
                waits.append(("dma", dr["dma_chan"], self.chan_count[dr["dma_chan"]]))
            else:
                if dr["eng"] == "pe" and eng == "pe":
                    continue
                dr["mark"] = True
                waits.append(("eng", dr["eng"], d))
        rec["waits"] = waits
        if dma_chan is not None:
            self.chan_count[dma_chan] = self.chan_count.get(dma_chan, 0) + 16
        elif fn is not None:
            self.last_on[eng] = idx
        self.ins.append(rec)
        return idx

    def op(self, eng, fn, reads=(), writes=()):
        return self._add(eng, fn, tuple(reads), tuple(writes))

    def dma(self, eng, out, in_, reads=(), writes=(), chan=None):
        return self._add(eng, lambda e: e.dma_start(out=out, in_=in_), tuple(reads), tuple(writes),
                         dma_chan=chan)

    def barrier(self):
        lasts = [v for v in self.last_on.values()]
        for e in ENGINES:
            idx = self._add(e, None, (), (), extra_deps=lasts)
            rec = self.ins[idx]
            for c, v in self.chan_count.items():
                if str(c).startswith("w") or str(c).startswith("aug"):
                    continue
                rec["waits"].append(("dma", c, v))

    def emit(self, stack):
        nc = self.nc
        sems = {e: stack.enter_context(nc.semaphore("s_" + e)) for e in ENGINES}
        csems = {c: stack.enter_context(nc.semaphore("c_" + str(c))) for c in self.chan_count}
        cnt = {e: 0 for e in ENGINES}
        for r in self.ins:
            if r["dma_chan"] is None and r["mark"]:
                cnt[r["eng"]] += 1
                r["ord"] = cnt[r["eng"]]
        per = {e: [] for e in ENGINES}
        for r in self.ins:
            per[r["eng"]].append(r)
        block = stack.enter_context(nc.Block())
        ins = self.ins

        def run(engname, eng):
            waited = {}
            for r in per[engname]:
                need = {}
                for w in r["waits"]:
                    if w[0] == "dma":
                        key = ("c", w[1]); val = w[2]
                    else:
                        key = ("e", w[1]); val = ins[w[2]]["ord"]
                    if val > need.get(key, 0):
                        need[key] = val
                for key, val in need.items():
                    if waited.get(key, 0) >= val:
                        continue
                    waited[key] = val
                    eng.wait_ge(csems[key[1]] if key[0] == "c" else sems[key[1]], val)
                if r["fn"] is None:
                    continue
                bi = r["fn"](eng)
                if r["dma_chan"] is not None:
                    bi.then_inc(csems[r["dma_chan"]], 16)
                elif r["mark"]:
                    bi.then_inc(sems[engname], 1)

        block.tensor(lambda e: run("pe", e))
        block.scalar(lambda e: run("act", e))
        block.vector(lambda e: run("dve", e))
        block.gpsimd(lambda e: run("pool", e))
        block.sync(lambda e: run("sp", e))


class Banks:
    def __init__(self, ids):
        self.ids = list(ids)
        self.busy = set()
        self.ptr = 0

    def alloc(self):
        n = len(self.ids)
        for k in range(n):
            b = self.ids[(self.ptr + k) % n]
            if b not in self.busy:
                self.busy.add(b)
                self.ptr = (self.ptr + k + 1) % n
                return b
        raise RuntimeError("out of PSUM banks")

    def free(self, b):
        self.busy.discard(b)


class Ring:
    def __init__(self, items):
        self.items = items
        self.i = 0

    def next(self):
        r = self.items[self.i % len(self.items)]
        k = self.i % len(self.items)
        self.i += 1
        return k, r


def build_program():
    nc = bass.Bass("TRN2", target_bir_lowering=False)
    dt_in = lambda name, shape: nc.dram_tensor(name, list(shape), F32, kind="ExternalInput").ap()
    x_d = dt_in("x", [S, D])
    mem_d = dt_in("mem", [MEM, D])
    w_in_d = dt_in("w_in", [D, IN_COLS])
    wcp_d = dt_in("w_conv_proj", [D, D])
    wdp_d = dt_in("w_diff_proj", [D, D])
    wxp_d = dt_in("w_x_proj", [D, D])
    wout_d = dt_in("w_out", [D, D])
    wkv_d = dt_in("w_mem_kv", [D, 2 * D])
    ng_d = dt_in("norm_g", [1, D])
    mg_d = dt_in("mem_norm_g", [1, D])
    vecs_d = dt_in("vecs", [NROWS, 128])
    lam_d = dt_in("lamv", [1, 256])
    cb_d = dt_in("cbits", [128, 4 * 128])
    idf_d = dt_in("identf", [128, 128])
    qaug_d = dt_in("qaug", [8, 4, S])
    kaug_d = dt_in("kaug", [8, 4, S])
    out_d = nc.dram_tensor("out", [S, D], F32, kind="ExternalOutput").ap()
    dbg_d = None
    if DEBUG is not None:
        dbg_d = nc.dram_tensor("dbg", [128, 8 * S], BF16, kind="ExternalOutput").ap()

    wviews = {
        "in": w_in_d.rearrange("(kc p) c -> p kc c", p=128),
        "cp": wcp_d.rearrange("(kc p) c -> p kc c", p=128),
        "dp": wdp_d.rearrange("(kc p) c -> p kc c", p=128),
        "xp": wxp_d.rearrange("(kc p) c -> p kc c", p=128),
        "out": wout_d.rearrange("(kc p) c -> p kc c", p=128),
        "kv": wkv_d.rearrange("(kc p) c -> p kc c", p=128),
    }

    with contextlib.ExitStack() as st:
        P = Prog(nc)

        tcount = [0]

        def T(stack, name, shape, dt):
            tcount[0] += 1
            return stack.enter_context(nc.sbuf_tensor(f"sb{tcount[0]}_" + name, list(shape), dt))

        hT = T(st, "hT", [128, NCH, S], BF16)
        yT = T(st, "yT", [128, NCH, S], BF16)
        actT = T(st, "actT", [128, NCH, S], BF16)
        wsl = [T(st, f"wsl{i}", [128, NCH, 512], BF16) for i in range(NSLOT)]
        cb = T(st, "cb", [128, 4, 128], BF16)
        identf = T(st, "identf", [128, 128], F32)
        vecsT = T(st, "vecsT", [128, NROWS], F32)
        sc = T(st, "sc", [128, 64], F32)
        dwh = T(st, "dwh", [128, 64], F32)
        ps = [st.enter_context(nc.psum_tensor(f"ps{i}", [128, 512], F32)) for i in range(8)]
        ident = cb[:, 0, :]
        ones = cb[:, 1, :]
        bd64 = cb[:, 2, :]
        negmask = cb[:, 3, :]
        SC_QN, SC_KN, SC_SUB, SC_XQ, SC_XK, SC_NLAM, SC_LNGH, SC_LNBH, SC_EPS6, SC_EPS5 = 0, 1, 2, 3, 5, 7, 8, 16, 30, 31
        banks = Banks(range(8))

        def PSK(b):
            return ("ps", b)

        groups = []

        def G(*parts):
            groups.append(list(parts))
            return len(groups) - 1

        g_kv = [G((0, "kv", i * 512, 512)) for i in range(4)]
        g_xh = [G((0, "in", X_Q + h * 256, 256), (256, "in", X_GATE + h * 256, 256)) for h in range(4)]
        g_xp = [G((0, "xp", i * 512, 512)) for i in range(2)]
        g_m2 = [G((0, "in", MERGE + 2 * D + i * 512, 512)) for i in range(2)]
        g_cab = [G((0, "in", C_GLU + jj * 256, 256), (256, "in", C_GLU + D + jj * 256, 256)) for jj in range(4)]
        g_cg = [G((0, "in", C_GATE + i * 512, 512)) for i in range(2)]
        g_cp = [G((0, "cp", i * 512, 512)) for i in range(2)]
        g_m0 = [G((0, "in", MERGE + i * 512, 512)) for i in range(2)]
        g_dh = [G((0, "in", D_Q + h * 128, 128), (128, "in", D_K + h * 128, 128),
                  (256, "in", D_V + h * 128, 128), (384, "in", D_GATE + h * 128, 128)) for h in range(8)]
        g_dp = [G((0, "dp", i * 512, 512)) for i in range(2)]
        g_m1 = [G((0, "in", MERGE + D + i * 512, 512)) for i in range(2)]
        g_out = [G((0, "out", i * 512, 512)) for i in range(2)]
        order = (g_kv + g_xh + [g_xp[0], g_m2[0], g_xp[1], g_m2[1]] + g_cab + g_cg
                 + [g_cp[0], g_m0[0], g_cp[1], g_m0[1]] + g_dh + [g_dp[0], g_m1[0], g_dp[1], g_m1[1]] + g_out)
        assert sorted(order) == list(range(len(groups)))
        pos_of = {g: i for i, g in enumerate(order)}
        wstate = {"issued": 0}

        def wkeys(slot, c0, n):
            return [("w", slot, q) for q in range(c0 // 128, (c0 + n) // 128)]

        def w_issue_next():
            i = wstate["issued"]
            if i >= len(order):
                return
            g = order[i]
            slot = i % NSLOT
            for (dc, wn, sc0, n) in groups[g]:
                P.dma("pool", wsl[slot][:, :, dc:dc + n], wviews[wn][:, :, sc0:sc0 + n],
                      writes=wkeys(slot, dc, n), chan=f"w{slot}")
            wstate["issued"] = i + 1

        def w_slot(g):
            i = pos_of[g]
            assert i < wstate["issued"], "weight group not issued yet"
            assert i >= wstate["issued"] - NSLOT
            return i % NSLOT

        def w_done(g):
            w_issue_next()

        def mm(out, lhsT, rhs, start, stop, reads, writes, skip=False):
            if skip:
                P.op("pe", lambda e: e.matmul(out, lhsT=lhsT, rhs=rhs, start=start, stop=stop,
                                              skip_group_check=True), reads, writes)
            else:
                P.op("pe", lambda e: e.matmul(out, lhsT=lhsT, rhs=rhs, start=start, stop=stop), reads, writes)

        def act(out, in_, func, reads, writes, scale=1.0, bias=0.0, accum=None):
            if accum is not None:
                P.op("act", lambda e: e.activation(out=out, in_=in_, func=func, scale=scale, bias=bias,
                                                   accum_out=accum), reads, writes)
            else:
                P.op("act", lambda e: e.activation(out=out, in_=in_, func=func, scale=scale, bias=bias),
                     reads, writes)

        def ts(eng, out, in0, s1, s2, op0, op1, reads, writes):
            if s2 is None:
                P.op(eng, lambda e: e.tensor_scalar(out=out, in0=in0, scalar1=s1, scalar2=None, op0=op0),
                     reads, writes)
            else:
                P.op(eng, lambda e: e.tensor_scalar(out=out, in0=in0, scalar1=s1, scalar2=s2, op0=op0, op1=op1),
                     reads, writes)

        def stt(out, in0, scalar, in1, op0, op1, reads, writes):
            P.op("dve", lambda e: e.scalar_tensor_tensor(out=out, in0=in0, scalar=scalar, in1=in1, op0=op0,
                                                         op1=op1), reads, writes)

        def tt(eng, out, in0, in1, op, reads, writes):
            P.op(eng, lambda e: e.tensor_tensor(out=out, in0=in0, in1=in1, op=op), reads, writes)

        def cp(eng, out, in_, reads, writes):
            if eng == "act":
                P.op("act", lambda e: e.copy(out=out, in_=in_), reads, writes)
            else:
                P.op(eng, lambda e: e.tensor_copy(out=out, in_=in_), reads, writes)

        def proj(bank, slot, col0, rhs_fn, rkeys, ncols=128, n=512, accum_extra=None):
            for kc in range(NCH):
                mm(ps[bank][:, 0:n], wsl[slot][:, kc, col0:col0 + 128], rhs_fn(kc), kc == 0, kc == NCH - 1,
                   reads=wkeys(slot, col0, 128) + rkeys, writes=[PSK(bank)])

        def hT_rhs(tb):
            return (lambda kc: hT[:, kc, tb * 512:(tb + 1) * 512]), [("hT", tb)]

        def rsqrt_from_psum(bank, n, scale, eps, vbuf, vkey):
            ecol = sc[:, SC_EPS6:SC_EPS6 + 1] if eps == 1e-6 else sc[:, SC_EPS5:SC_EPS5 + 1]
            act(vbuf[:, 0:n], ps[bank][:, 0:n], AF.Ln, ["sc"], [PSK(bank), vkey], scale=scale, bias=ecol)
            act(vbuf[:, 0:n], vbuf[:, 0:n], AF.Exp, [vkey], [vkey], scale=-0.5)

        state = {"done": False, "inph": False}

        def ph_enter():
            state["inph"] = True

        def ph_exit():
            state["inph"] = False
            if state["done"]:
                raise _Stop()

        def rsqrt_dve_evac(bank, n, scale, eps, vbuf, vkey):
            ts("dve", vbuf[:, 0:n], ps[bank][:, 0:n], scale, eps, ALU.mult, ALU.add, reads=[], writes=[PSK(bank), vkey])
            act(vbuf[:, 0:n], vbuf[:, 0:n], AF.Ln, [vkey], [vkey])
            act(vbuf[:, 0:n], vbuf[:, 0:n], AF.Exp, [vkey], [vkey], scale=-0.5)

        def dump(tag, tensor):
            if DEBUG == tag:
                P.barrier()
                P.dma("sp", dbg_d, tensor[:].rearrange("p a b -> p (a b)"), writes=["dbg"], chan="dbg")
                P.op("sp", None, reads=["dbg"])
                state["done"] = True
                if not state["inph"]:
                    raise _Stop()

        P.dma("pool", cb[:].rearrange("p a b -> p (a b)"), cb_d, writes=["cb"], chan="c0")
        P.dma("sp", identf[:], idf_d, writes=["identf"], chan="c1")
        for i in range(NSLOT):
            w_issue_next()
        P.op("pool", lambda e: e.memset(sc[:, SC_EPS6:SC_EPS6 + 1], 1e-6), writes=["sc"])
        P.op("pool", lambda e: e.memset(sc[:, SC_EPS5:SC_EPS5 + 1], 1e-5), writes=["sc"])
        with contextlib.ExitStack() as ph:
            vst = [T(ph, f"vst{i}", [128, 128], F32) for i in range(3)]
            lamb = T(ph, "lamb", [128, 256], F32)
            lamp = T(ph, "lamp", [128, 128], F32)
            lams = T(ph, "lams", [128, 2], F32)
            rows = [(0, 128), (128, 128), (256, NROWS - 256)]
            for i, (r0, n) in enumerate(rows):
                P.dma("sp", vst[i][0:n, :], vecs_d[r0:r0 + n, :], writes=[("vst", i)], chan="c1")
            P.dma("sp", lamb[:], lam_d.partition_broadcast(128), writes=["lamb"], chan="c1")
            b = banks.alloc()
            for i, (r0, n) in enumerate(rows):
                mm(ps[b][:, r0:r0 + n], vst[i][0:n, :], identf[0:n, 0:n], True, True,
                   reads=[("vst", i), "identf"], writes=[PSK(b)], skip=True)
            cp("dve", vecsT[:], ps[b][:, 0:NROWS], reads=[], writes=[PSK(b), "vecsT"])
            banks.free(b)
            ts("dve", sc[:, SC_QN:SC_QN + 1], vecsT[:, R_QN:R_QN + 1], 0.125, None, ALU.mult, None, ["vecsT"], ["sc"])
            cp("dve", sc[:, SC_KN:SC_KN + 1], vecsT[:, R_KN:R_KN + 1], ["vecsT"], ["sc"])
            ts("dve", sc[:, SC_SUB:SC_SUB + 1], vecsT[:, R_SUB:R_SUB + 1], 0.4, None, ALU.mult, None, ["vecsT"], ["sc"])
            cp("dve", sc[:, SC_XQ:SC_XQ + 2], vecsT[:, R_XQ:R_XQ + 2], ["vecsT"], ["sc"])
            ts("dve", sc[:, SC_XK:SC_XK + 2], vecsT[:, R_XK:R_XK + 2], 1.0 / 16.0, None, ALU.mult, None, ["vecsT"], ["sc"])
            ts("dve", sc[:, SC_LNGH:SC_LNGH + 8], vecsT[:, R_LNG:R_LNG + 8], 0.5, None, ALU.mult, None, ["vecsT"], ["sc"])
            ts("dve", sc[:, SC_LNBH:SC_LNBH + 8], vecsT[:, R_LNB:R_LNB + 8], 0.5, None, ALU.mult, None, ["vecsT"], ["sc"])
            ts("dve", dwh[:], vecsT[:, R_DW + 23 * 8:R_DW + 248], 0.5, None, ALU.mult, None, ["vecsT"], ["dwh"])
            tt("dve", lamp[:], lamb[:, 0:128], lamb[:, 128:256], ALU.mult, ["lamb"], ["lamp"])
            P.op("dve", lambda e: e.reduce_sum(out=lams[:], in_=lamp[:].rearrange("p (a b) -> p a b", a=2),
                                               axis=AX.X), ["lamp"], ["lams"])
            act(lams[:], lams[:], AF.Exp, ["lams"], ["lams"])
            tt("dve", lams[:, 0:1], lams[:, 1:2], lams[:, 0:1], ALU.subtract, ["lams"], ["lams"])
            ts("dve", sc[:, SC_NLAM:SC_NLAM + 1], lams[:, 0:1], -0.2, None, ALU.add, None, ["lams"], ["sc"])
            P.barrier()

        try:

            def branch_out(gp, gm, first, ph):
                thm = [T(ph, f"thm{i}", [128, 512], F32) for i in range(2)]
                tb_ = [T(ph, f"tbo{i}", [128, 512], F32) for i in range(2)]
                r_th = Ring(thm)
                r_t = Ring(tb_)
                for half in range(2):
                    sp_ = w_slot(gp[half])
                    sm_ = w_slot(gm[half])
                    for jj in range(4):
                        j = half * 4 + jj
                        for tb in range(NTB):
                            bp = banks.alloc()
                            for c in range(NCH):
                                mm(ps[bp][:, :], wsl[sp_][:, c, jj * 128:(jj + 1) * 128], actT[:, c, tb * 512:(tb + 1) * 512],
                                   c == 0, c == NCH - 1, reads=wkeys(sp_, jj * 128, 128) + [("act", c, tb)], writes=[PSK(bp)])
                            bm = banks.alloc()
                            rf, rk = hT_rhs(tb)
                            proj(bm, sm_, jj * 128, rf, rk)
                            k1, th = r_th.next()
                            act(th[:], ps[bm][:, :], AF.Tanh, [], [PSK(bm), ("thm", k1)], scale=0.5)
                            banks.free(bm)
                            k2, tbuf = r_t.next()
                            stt(tbuf[:], th[:], 1.0, ps[bp][:, :], ALU.add, ALU.mult, [("thm", k1)], [PSK(bp), ("tbo", k2)])
                            banks.free(bp)
                            ysl = yT[:, j, tb * 512:(tb + 1) * 512]
                            if first:
                                ts("dve", ysl, tbuf[:], 0.5, None, ALU.mult, None, [("tbo", k2)], [("y", j, tb)])
                            else:
                                stt(ysl, tbuf[:], 0.5, ysl, ALU.mult, ALU.add, [("tbo", k2)], [("y", j, tb)])
                    w_done(gp[half])
                    w_done(gm[half])

            with contextlib.ExitStack() as ph:
                ph_enter()
                xt = [T(ph, f"xt{i}", [128, D], F32) for i in range(3)]
                xn = [T(ph, f"xn{i}", [128, D], BF16) for i in range(3)]
                junk = T(ph, "junk", [128, D], BF16)
                gbc = T(ph, "gbc", [128, D], F32)
                gmbc = T(ph, "gmbc", [128, D], F32)
                ssq = T(ph, "ssq", [128, 32], F32)
                memT = T(ph, "memT", [128, NCH, MEM], BF16)
                P.dma("sp", gbc[:], ng_d.partition_broadcast(128), writes=["gbc"], chan="c1")
                P.dma("sp", gmbc[:], mg_d.partition_broadcast(128), writes=["gmbc"], chan="c1")
                tiles = [("m", i) for i in range(2)] + [("x", i) for i in range(NT)]
                for n_, (kind, t) in enumerate(tiles):
                    i = n_ % 3
                    src = (mem_d if kind == "m" else x_d)[t * 128:(t + 1) * 128, :]
                    P.dma("sp", xt[i][:], src, writes=[("xt", i)], chan=f"x{i}")
                    col = ssq[:, n_:n_ + 1]
                    act(junk[:], xt[i][:], AF.Square, [("xt", i)], ["junk"])
                    P.op("dve", (lambda e, col=col: e.reduce_sum(out=col, in_=junk[:], axis=AX.X)), ["junk"], [("ssq", n_)])
                    act(col, col, AF.Ln, [("ssq", n_), "sc"], [("ssq", n_)], scale=1.0 / D, bias=sc[:, SC_EPS6:SC_EPS6 + 1])
                    act(col, col, AF.Exp, [("ssq", n_)], [("ssq", n_)], scale=-0.5)
                    gsrc = gmbc if kind == "m" else gbc
                    stt(xn[i][:], xt[i][:], col, gsrc[:], ALU.mult, ALU.mult,
                        [("xt", i), ("ssq", n_), "gbc", "gmbc"], [("xn", i)])
                    b = banks.alloc()
                    pbf = ps[b][:].bitcast(BF16)
                    for kc in range(NCH):
                        P.op("pe", (lambda e, kc=kc, i=i, pbf=pbf: e.transpose(out=pbf[:, kc * 128:(kc + 1) * 128],
                                                                              in_=xn[i][:, kc * 128:(kc + 1) * 128],
                                                                              identity=ident)),
                             reads=[("xn", i), "cb"], writes=[PSK(b)])
                    src3 = pbf.rearrange("p (a b) -> p a b", a=NCH)
                    if kind == "m":
                        cp("dve", memT[:, :, t * 128:(t + 1) * 128], src3, [], [PSK(b), "memT"])
                    else:
                        cp("dve", hT[:, :, t * 128:(t + 1) * 128], src3, [], [PSK(b), ("hT", t // 4)])
                    banks.free(b)
                with contextlib.ExitStack() as ph:
                    ph_enter()
                    kT = T(ph, "kT", [128, NCH, MEM], BF16)
                    Vx = T(ph, "Vx", [128, 2, D], BF16)
                    qT = T(ph, "qT", [128, 2, S], BF16)
                    sqb = [T(ph, f"xsq{i}", [128, 512], BF16) for i in range(4)]
                    vb = [T(ph, f"xvb{i}", [128, 512], F32) for i in range(2)]
                    PT = [T(ph, f"xPT{i}", [128, 512], BF16) for i in range(4)]
                    thg = [T(ph, f"xthg{i}", [128, 512], F32) for i in range(2)]
                    gsb = [T(ph, f"xgs{i}", [128, 512], F32) for i in range(2)]
                    rlb = [T(ph, f"xrl{i}", [128, 512], F32) for i in range(2)]
                    tob = [T(ph, f"xto{i}", [128, 512], F32) for i in range(2)]
                    r_sq, r_vb, r_PT, r_thg, r_gs, r_rl, r_to = (Ring(sqb), Ring(vb), Ring(PT), Ring(thg), Ring(gsb),
                                                                 Ring(rlb), Ring(tob))
                    for hx in range(4):
                        bks = []
                        sqs = []
                        for dc in range(2):
                            c = hx * 2 + dc
                            g = g_kv[c // 4]
                            sl = w_slot(g)
                            b = banks.alloc()
                            proj(b, sl, (c % 4) * 128, lambda kc: memT[:, kc, :], ["memT"], n=MEM)
                            k_, sq = r_sq.next()
                            act(sq[:, 0:MEM], ps[b][:, 0:MEM], AF.Square, [], [PSK(b), ("xsq", k_)])
                            bks.append(b)
                            sqs.append((k_, sq))
                            if c == 3:
                                w_done(g_kv[0])
                            if c == 7:
                                w_done(g_kv[1])
                        bs = banks.alloc()
                        for dc in range(2):
                            mm(ps[bs][:, 0:MEM], ones, sqs[dc][1][:, 0:MEM], dc == 0, dc == 1, ["cb", ("xsq", sqs[dc][0])], [PSK(bs)])
                        kv_, v = r_vb.next()
                        rsqrt_from_psum(bs, MEM, 1.0 / 256.0, 1e-6, v, ("xvb", kv_))
                        banks.free(bs)
                        for dc in range(2):
                            c = hx * 2 + dc
                            stt(kT[:, c, :], ps[bks[dc]][:, 0:MEM], sc[:, SC_XK + dc:SC_XK + dc + 1], v[:, 0:MEM], ALU.mult, ALU.mult,
                                ["sc", ("xvb", kv_)], [PSK(bks[dc]), ("kT", c)])
                            banks.free(bks[dc])
                    for vg in range(2):
                        sl = w_slot(g_kv[2 + vg])
                        for mt in range(2):
                            b = banks.alloc()
                            for kc in range(NCH):
                                mm(ps[b][:, :], memT[:, kc, mt * 128:(mt + 1) * 128], wsl[sl][:, kc, :], kc == 0, kc == NCH - 1,
                                   ["memT"] + wkeys(sl, 0, 512), [PSK(b)])
                            cp("dve", Vx[:, mt, vg * 512:(vg + 1) * 512], ps[b][:, :], [], [PSK(b), ("Vx", mt, vg)])
                            banks.free(b)
                        w_done(g_kv[2 + vg])
                    for hx in range(4):
                        sl = w_slot(g_xh[hx])
                        pend = None

                        def q_finish(pend):
                            tb, bq, sqs = pend
                            bs = banks.alloc()
                            for dc in range(2):
                                mm(ps[bs][:, :], ones, sqs[dc][1][:], dc == 0, dc == 1, ["cb", ("xsq", sqs[dc][0])], [PSK(bs)])
                            kv_, v = r_vb.next()
                            rsqrt_from_psum(bs, 512, 1.0 / 256.0, 1e-6, v, ("xvb", kv_))
                            banks.free(bs)
                            for dc in range(2):
                                stt(qT[:, dc, tb * 512:(tb + 1) * 512], ps[bq[dc]][:, :], sc[:, SC_XQ + dc:SC_XQ + dc + 1], v[:],
                                    ALU.mult, ALU.mult, ["sc", ("xvb", kv_)], [PSK(bq[dc]), ("qT", dc, tb)])
                                banks.free(bq[dc])

                        for tb in range(NTB):
                            bq = []
                            sqs = []
                            rf, rk = hT_rhs(tb)
                            for dc in range(2):
                                b = banks.alloc()
                                proj(b, sl, dc * 128, rf, rk)
                                k_, sq = r_sq.next()
                                act(sq[:], ps[b][:, :], AF.Square, [], [PSK(b), ("xsq", k_)])
                                bq.append(b)
                                sqs.append((k_, sq))
                            if pend is not None:
                                q_finish(pend)
                            pend = (tb, bq, sqs)
                        q_finish(pend)
                        for tb in range(NTB):
                            pts = []
                            for mt in range(2):
                                b = banks.alloc()
                                for dc in range(2):
                                    mm(ps[b][:, :], kT[:, hx * 2 + dc, mt * 128:(mt + 1) * 128], qT[:, dc, tb * 512:(tb + 1) * 512],
                                       dc == 0, dc == 1, [("kT", hx * 2 + dc), ("qT", dc, tb)], [PSK(b)])
                                kp, pt = r_PT.next()
                                act(pt[:], ps[b][:, :], AF.Exp, [], [PSK(b), ("xPT", kp)])
                                banks.free(b)
                                pts.append((kp, pt))
                            bo = []
                            for vc in range(2):
                                b = banks.alloc()
                                for mt in range(2):
                                    c0 = hx * 256 + vc * 128
                                    mm(ps[b][:, :], Vx[:, mt, c0:c0 + 128], pts[mt][1][:], mt == 0, mt == 1,
                                       [("Vx", mt, c0 // 512), ("xPT", pts[mt][0])], [PSK(b)])
                                bo.append(b)
                            bl = banks.alloc()
                            for mt in range(2):
                                mm(ps[bl][:, :], ones, pts[mt][1][:], mt == 0, mt == 1, ["cb", ("xPT", pts[mt][0])], [PSK(bl)])
                            bg = []
                            rf, rk = hT_rhs(tb)
                            for vc in range(2):
                                b = banks.alloc()
                                proj(b, sl, 256 + vc * 128, rf, rk)
                                bg.append(b)
                            kr, rl = r_rl.next()
                            act(rl[:], ps[bl][:, :], AF.Ln, [], [PSK(bl), ("xrl", kr)])
                            banks.free(bl)
                            act(rl[:], rl[:], AF.Exp, [("xrl", kr)], [("xrl", kr)], scale=-1.0)
                            for vc in range(2):
                                kt_, th = r_thg.next()
                                act(th[:], ps[bg[vc]][:, :], AF.Tanh, [], [PSK(bg[vc]), ("xthg", kt_)], scale=0.5)
                                kg, gs = r_gs.next()
                                stt(gs[:], th[:], 1.0, ps[bg[vc]][:, :], ALU.add, ALU.mult, [("xthg", kt_)], [PSK(bg[vc]), ("xgs", kg)])
                                banks.free(bg[vc])
                                ko, to = r_to.next()
                                tt("dve", to[:], ps[bo[vc]][:, :], rl[:], ALU.mult, [("xrl", kr)], [PSK(bo[vc]), ("xto", ko)])
                                banks.free(bo[vc])
                                stt(actT[:, hx * 2 + vc, tb * 512:(tb + 1) * 512], to[:], 0.5, gs[:], ALU.mult, ALU.mult,
                                    [("xto", ko), ("xgs", kg)], [("act", hx * 2 + vc, tb)])
                        w_done(g_xh[hx])
                    dump("actx", actT)
                    if not state["done"]:
                        branch_out(g_xp, g_m2, True, ph)
                    P.barrier()
            ph_exit()
            dump("yx", yT)

            with contextlib.ExitStack() as ph:
                ph_enter()
                PADW = 30
                ub = [T(ph, f"ub{i}", [128, PADW + S], BF16) for i in range(2)]
                dg = [T(ph, f"dg{i}", [128, 31, 128], BF16) for i in range(2)]
                thb = [T(ph, f"cth{i}", [128, 512], F32) for i in range(4)]
                Ms = [T(ph, f"cM{i}", [128, 512], F32) for i in range(NTB)]
                Rs = [T(ph, f"cR{i}", [128, 512], F32) for i in range(NTB)]
                zt = [T(ph, f"cz{i}", [128, 512], F32) for i in range(3)]
                sqc = [T(ph, f"csq{i}", [128, 512], BF16) for i in range(2)]
                gsc = [T(ph, f"cgs{i}", [128, 512], F32) for i in range(2)]
                N_PE_TAPS = 23
                accb = [T(ph, f"cacc{i}", [128, 512], F32) for i in range(2)]
                r_acc = Ring(accb)
                vvb = [T(ph, f"cv{i}", [128, 512], F32) for i in range(2)]
                r_v = Ring(vvb)
                r_th, r_z, r_sq, r_gs = Ring(thb), Ring(zt), Ring(sqc), Ring(gsc)
                for i in range(2):
                    P.op("pool", (lambda e, i=i: e.memset(ub[i][:, 0:PADW], 0.0)), writes=[("ub", i, -1)])
                for j in range(NCH):
                    g = g_cab[j // 2]
                    sl = w_slot(g)
                    jl = j % 2
                    ui = j % 2
                    for k in range(N_PE_TAPS):
                        col = vecsT[:, R_DW + k * 8 + j:R_DW + k * 8 + j + 1]
                        P.op("pool", (lambda e, k=k, ui=ui, col=col: e.tensor_scalar(out=dg[ui][:, k, :], in0=ident, scalar1=col,
                                                                                     scalar2=0.5, op0=ALU.mult, op1=ALU.mult)),
                             reads=["cb", "vecsT"], writes=[("dg", ui)])
                    pend = None

                    def conv_block(tb, ui=ui, j=j):
                        b = banks.alloc()
                        rk = [("ub", ui, tb), ("ub", ui, tb - 1), ("dg", ui)]
                        for k in range(N_PE_TAPS):
                            mm(ps[b][:, :], dg[ui][:, k, :], ub[ui][:, tb * 512 + k:tb * 512 + k + 512], k == 0, k == N_PE_TAPS - 1, rk, [PSK(b)])
                        ka, acc = r_acc.next()
                        for k in range(N_PE_TAPS, 31):
                            ush = ub[ui][:, tb * 512 + k:tb * 512 + k + 512]
                            col = dwh[:, (k - 23) * 8 + j:(k - 23) * 8 + j + 1]
                            if k == N_PE_TAPS:
                                ts("dve", acc[:], ush, col, None, ALU.mult, None, [("ub", ui, tb), ("ub", ui, tb - 1), "dwh"], [("cacc", ka)])
                            else:
                                stt(acc[:], ush, col, acc[:], ALU.mult, ALU.add, [("ub", ui, tb), ("ub", ui, tb - 1), "dwh"], [("cacc", ka)])
                        stt(actT[:, j, tb * 512:(tb + 1) * 512], ps[b][:, :], vecsT[:, R_DWB + j:R_DWB + j + 1], acc[:], ALU.add, ALU.add,
                            ["vecsT", ("cacc", ka)], [PSK(b), ("act", j, tb)])
                        banks.free(b)

                    for tb in range(NTB):
                        rf, rk = hT_rhs(tb)
                        ba = banks.alloc()
                        proj(ba, sl, jl * 128, rf, rk)
                        bb = banks.alloc()
                        proj(bb, sl, 256 + jl * 128, rf, rk)
                        kt_, th = r_th.next()
                        act(th[:], ps[bb][:, :], AF.Tanh, [], [PSK(bb), ("cth", kt_)], scale=0.5)
                        banks.free(bb)
                        stt(ub[ui][:, PADW + tb * 512:PADW + (tb + 1) * 512], th[:], 1.0, ps[ba][:, :], ALU.add, ALU.mult,
                            [("cth", kt_)], [PSK(ba), ("ub", ui, tb)])
                        banks.free(ba)
                        if pend is not None:
                            conv_block(pend)
                        pend = tb
                    conv_block(pend)
                    if jl == 1:
                        w_done(g)
                dump("conv", actT)
                for tb in range(NTB):
                    bs = banks.alloc()
                    bq = banks.alloc()
                    for j in range(NCH):
                        a = actT[:, j, tb * 512:(tb + 1) * 512]
                        mm(ps[bs][:, :], ones, a, j == 0, j == NCH - 1, ["cb", ("act", j, tb)], [PSK(bs)])
                        ks, sq = r_sq.next()
                        act(sq[:], a, AF.Square, [("act", j, tb)], [("csq", ks)])
                        mm(ps[bq][:, :], ones, sq[:], j == 0, j == NCH - 1, ["cb", ("csq", ks)], [PSK(bq)])
                    M, R = Ms[tb], Rs[tb]
                    ts("dve", M[:], ps[bs][:, :], 1.0 / D, None, ALU.mult, None, [], [PSK(bs), ("cM", tb)])
                    banks.free(bs)
                    tt("dve", R[:], M[:], M[:], ALU.mult, [("cM", tb)], [("cR", tb)])
                    stt(R[:], ps[bq][:, :], 1.0 / D, R[:], ALU.mult, ALU.subtract, [], [PSK(bq), ("cR", tb)])
                    banks.free(bq)
                    act(R[:], R[:], AF.Ln, [("cR", tb), "sc"], [("cR", tb)], scale=1.0, bias=sc[:, SC_EPS5:SC_EPS5 + 1])
                    act(R[:], R[:], AF.Exp, [("cR", tb)], [("cR", tb)], scale=-0.5)
                    stt(M[:], M[:], -1.0, R[:], ALU.mult, ALU.mult, [("cR", tb)], [("cM", tb)])
                items = [(half, jj, tb) for half in range(2) for jj in range(4) for tb in range(NTB)]
                stash = {}

                def stA(n):
                    half, jj, tb = items[n]
                    j = half * 4 + jj
                    a_ = actT[:, j, tb * 512:(tb + 1) * 512]
                    kz, z = r_z.next()
                    tt("dve", z[:], a_, Rs[tb][:], ALU.mult, [("act", j, tb), ("cR", tb)], [("cz", kz)])
                    tt("dve", z[:], z[:], Ms[tb][:], ALU.add, [("cM", tb)], [("cz", kz)])
                    bg = banks.alloc()
                    rf, rk = hT_rhs(tb)
                    proj(bg, w_slot(g_cg[half]), jj * 128, rf, rk)
                    stash[n] = dict(kz=kz, z=z, bg=bg)
                    if jj == 3 and tb == NTB - 1:
                        w_done(g_cg[half])

                def stB(n):
                    half, jj, tb = items[n]
                    j = half * 4 + jj
                    d_ = stash[n]
                    z, kz, bg = d_["z"], d_["kz"], d_["bg"]
                    d_["kt"], d_["th"] = r_th.next()
                    act(d_["th"][:], z[:], AF.Tanh, [("cz", kz), "sc"], [("cth", d_["kt"])],
                        scale=sc[:, SC_LNGH + j:SC_LNGH + j + 1], bias=sc[:, SC_LNBH + j:SC_LNBH + j + 1])
                    d_["kv"], d_["vv"] = r_v.next()
                    act(d_["vv"][:], z[:], AF.Identity, [("cz", kz), "vecsT"], [("cv", d_["kv"])],
                        scale=vecsT[:, R_LNG + j:R_LNG + j + 1], bias=vecsT[:, R_LNB + j:R_LNB + j + 1])
                    d_["kt2"], d_["th2"] = r_th.next()
                    act(d_["th2"][:], ps[bg][:, :], AF.Tanh, [], [PSK(bg), ("cth", d_["kt2"])], scale=0.5)

                def stC(n):
                    half, jj, tb = items[n]
                    j = half * 4 + jj
                    d_ = stash.pop(n)
                    z, kz, bg = d_["z"], d_["kz"], d_["bg"]
                    a_ = actT[:, j, tb * 512:(tb + 1) * 512]
                    stt(z[:], d_["th"][:], 1.0, d_["vv"][:], ALU.add, ALU.mult, [("cth", d_["kt"]), ("cv", d_["kv"])], [("cz", kz)])
                    kg, gs = r_gs.next()
                    stt(gs[:], d_["th2"][:], 1.0, ps[bg][:, :], ALU.add, ALU.mult, [("cth", d_["kt2"])], [PSK(bg), ("cgs", kg)])
                    banks.free(bg)
                    stt(a_, z[:], 0.25, gs[:], ALU.mult, ALU.mult, [("cz", kz), ("cgs", kg)], [("act", j, tb)])

                NI = len(items)
                for n in range(NI + 2):
                    if n < NI:
                        stA(n)
                    if 0 <= n - 1 < NI:
                        stB(n - 1)
                    if 0 <= n - 2 < NI:
                        stC(n - 2)
                dump("actc", actT)
                if not state["done"]:
                    branch_out(g_cp, g_m0, False, ph)
                P.barrier()
            ph_exit()
            dump("yc", yT)

            with contextlib.ExitStack() as ph:
                ph_enter()
                QL = [T(ph, f"QL{i}", [128, S], BF16) for i in range(2)]
                QU = [T(ph, f"QU{i}", [128, S], BF16) for i in range(2)]
                KL = [T(ph, f"KL{i}", [128, S], BF16) for i in range(2)]
                KU = [T(ph, f"KU{i}", [128, S], BF16) for i in range(2)]
                Vd = [T(ph, f"Vd{i}", [128, NT, 128], BF16) for i in range(2)]
                gsd = [T(ph, f"gsd{i}", [128, S], BF16) for i in range(2)]
                PTd = [[T(ph, f"PT{m}_{i}", [128, 512], BF16) for i in range(3)] for m in range(2)]
                sqd = [T(ph, f"dsq{i}", [128, 512], BF16) for i in range(2)]
                thd = [T(ph, f"dth{i}", [128, 512], F32) for i in range(1)]
                vbd = [T(ph, f"dvb{i}", [128, 512], F32) for i in range(2)]
                E1 = [T(ph, f"E1_{i}", [128, 512], F32) for i in range(2)]
                E2 = [T(ph, f"E2_{i}", [128, 512], F32) for i in range(2)]
                sqe = [T(ph, f"sqe{i}", [128, 512], BF16) for i in range(2)]
                RL = [T(ph, f"RL{i}", [128, 512], F32) for i in range(2)]
                r_PT = [Ring(PTd[0]), Ring(PTd[1])]
                rawd = [T(ph, f"draw{i}", [128, 512], F32) for i in range(3)]
                vtd = [T(ph, f"dvt{i}", [128, 512], BF16) for i in range(2)]
                r_vt = Ring(vtd)
                r_raw = Ring(rawd)
                r_sq, r_th, r_vb = Ring(sqd), Ring(thd), Ring(vbd)
                for i in range(2):
                    P.op("pool", (lambda e, i=i: e.memset(QU[i][0:64, :], 0.0)), writes=[("QUa", i)])
                    P.op("pool", (lambda e, i=i: e.memset(KU[i][0:64, :], 0.0)), writes=[("KUa", i)])
                O_B = [4, 6]
                L_B = [5, 7]
                for b in (4, 5, 6, 7):
                    banks.busy.add(b)
                dbanks = Banks([0, 1, 2, 3])

                def qk_unit(h, which, tb):
                    s_ = h % 2
                    st_ = {}
                    c0 = 0 if which == "q" else 128
                    gcol = sc[:, SC_QN:SC_QN + 1] if which == "q" else sc[:, SC_KN:SC_KN + 1]

                    def st0():
                        sl = w_slot(g_dh[h])
                        rf, rk = hT_rhs(tb)
                        b = dbanks.alloc()
                        proj(b, sl, c0, rf, rk)
                        st_["kraw"], st_["raw"] = r_raw.next()
                        cp("dve", st_["raw"][:], ps[b][:, :], [], [PSK(b), ("draw", st_["kraw"])])
                        dbanks.free(b)
                        st_["ks"], sq = r_sq.next()
                        tt("dve", sq[:], st_["raw"][:], st_["raw"][:], ALU.mult, [("draw", st_["kraw"])], [("dsq", st_["ks"])])

                    def st1():
                        bs = dbanks.alloc()
                        mm(ps[bs][:, :], bd64, sqd[st_["ks"]][:], True, True, ["cb", ("dsq", st_["ks"])], [PSK(bs)])
                        st_["kv"], st_["v"] = r_vb.next()
                        ts("dve", st_["v"][:], ps[bs][:, :], 1.0 / 64.0, 1e-6, ALU.mult, ALU.add, [], [PSK(bs), ("dvb", st_["kv"])])
                        dbanks.free(bs)

                    def st2():
                        v, vk = st_["v"], ("dvb", st_["kv"])
                        act(v[:], v[:], AF.Ln, [vk], [vk])
                        act(v[:], v[:], AF.Exp, [vk], [vk], scale=-0.5)

                    def st3():
                        v, vk = st_["v"], ("dvb", st_["kv"])
                        raw, rk_ = st_["raw"], ("draw", st_["kraw"])
                        lo = (QL if which == "q" else KL)[s_]
                        up = (QU if which == "q" else KU)[s_]
                        sl_ = slice(tb * 512, (tb + 1) * 512)
                        stt(lo[0:64, sl_], raw[0:64, :], gcol[0:64, :], v[0:64, :], ALU.mult, ALU.mult,
                            ["sc", vk, rk_], [(which + "L", s_, tb)])
                        stt(up[64:128, sl_], raw[64:128, :], gcol[64:128, :], v[64:128, :], ALU.mult, ALU.mult,
                            ["sc", vk, rk_], [(which + "U", s_, tb)])

                    return [st0, st1, st2, st3]

                def v_unit(h, tb):
                    s_ = h % 2
                    st_ = {}

                    def st0():
                        sl = w_slot(g_dh[h])
                        rf, rk = hT_rhs(tb)
                        b = dbanks.alloc()
                        proj(b, sl, 256, rf, rk)
                        st_["k"], st_["vt"] = r_vt.next()
                        cp("dve", st_["vt"][:], ps[b][:, :], [], [PSK(b), ("dvt", st_["k"])])
                        dbanks.free(b)

                    def st1():
                        b = dbanks.alloc()
                        pbf = ps[b][:].bitcast(BF16)
                        vt = st_["vt"]
                        for tl in range(4):
                            P.op("pe", (lambda e, tl=tl, pbf=pbf, vt=vt: e.transpose(out=pbf[:, tl * 128:(tl + 1) * 128],
                                                                                     in_=vt[:, tl * 128:(tl + 1) * 128], identity=ident)),
                                 reads=[("dvt", st_["k"]), "cb"], writes=[PSK(b)])
                        cp("dve", Vd[s_][:, tb * 4:(tb + 1) * 4, :], pbf[:, 0:512].rearrange("p (a b) -> p a b", a=4), [],
                           [PSK(b), ("Vd", s_, tb)])
                        dbanks.free(b)

                    return [st0, None, st1]

                def g_unit(h, tb, last):
                    s_ = h % 2
                    st_ = {}

                    def st0():
                        sl = w_slot(g_dh[h])
                        rf, rk = hT_rhs(tb)
                        b = dbanks.alloc()
                        proj(b, sl, 384, rf, rk)
                        st_["kraw"], st_["raw"] = r_raw.next()
                        cp("dve", st_["raw"][:], ps[b][:, :], [], [PSK(b), ("draw", st_["kraw"])])
                        dbanks.free(b)
                        if last:
                            w_done(g_dh[h])

                    def st1():
                        st_["kt"], st_["th"] = r_th.next()
                        act(st_["th"][:], st_["raw"][:], AF.Tanh, [("draw", st_["kraw"])], [("dth", st_["kt"])], scale=0.5)

                    def st2():
                        stt(gsd[s_][:, tb * 512:(tb + 1) * 512], st_["th"][:], 1.0, st_["raw"][:], ALU.add, ALU.mult,
                            [("dth", st_["kt"]), ("draw", st_["kraw"])], [("gsd", s_, tb)])

                    return [st0, st1, st2]

                class Prologue:
                    def __init__(self, h, period=2):
                        self.h = h
                        s_ = h % 2
                        self.units = ([qk_unit(h, "q", tb) for tb in range(NTB)] + [qk_unit(h, "k", tb) for tb in range(NTB)]
                                      + [v_unit(h, g4) for g4 in range(4)] + [g_unit(h, tb, tb == NTB - 1) for tb in range(NTB)])
                        self.active = []
                        self.t = 0
                        self.period = period
                        self.started = False

                    def tick(self):
                        h = self.h
                        s_ = h % 2
                        if not self.started:
                            self.started = True
                            P.dma("pool", QL[s_][64:68, :], qaug_d[h], writes=[("QLa", s_)], chan=f"aug{s_}")
                            P.dma("pool", QU[s_][0:4, :], qaug_d[h], writes=[("QUa", s_)], chan=f"aug{s_}")
                            P.dma("pool", KL[s_][64:68, :], kaug_d[h], writes=[("KLa", s_)], chan=f"aug{s_}")
                            P.dma("pool", KU[s_][0:4, :], kaug_d[h], writes=[("KUa", s_)], chan=f"aug{s_}")
                        if self.t % self.period == 0 and self.units:
                            self.active.append(self.units.pop(0))
                        self.t += 1
                        for u in list(self.active):
                            f_ = u.pop(0)
                            if f_ is not None:
                                f_()
                            if not u:
                                self.active.remove(u)

                    def done(self):
                        return not self.units and not self.active

                    def flush(self):
                        while not self.done():
                            self.tick()

                def scores(h, qb, kt):
                    s_ = h % 2
                    r = kt - 4 * qb
                    c0 = 128 * r if r > 0 else 0
                    n = 512 - c0
                    outp = []
                    for m in range(2):
                        b = dbanks.alloc()
                        if m == 0:
                            lhsT = KL[s_][0:68, kt * 128:(kt + 1) * 128]
                            rhs = QL[s_][0:68, qb * 512 + c0:(qb + 1) * 512]
                            rk = [("kL", s_, kt // 4), ("KLa", s_), ("qL", s_, qb), ("QLa", s_)]
                        else:
                            lhsT = KU[s_][:, kt * 128:(kt + 1) * 128]
                            rhs = QU[s_][:, qb * 512 + c0:(qb + 1) * 512]
                            rk = [("kU", s_, kt // 4), ("KUa", s_), ("qU", s_, qb), ("QUa", s_)]
                        mm(ps[b][:, c0:512], lhsT, rhs, True, r < 0, rk, [PSK(b)], skip=True)
                        if r >= 0:
                            mm(ps[b][:, c0:c0 + 128], ident, negmask, False, True, ["cb"], [PSK(b)], skip=True)
                        kp, pt = r_PT[m].next()
                        act(pt[:, c0:512], ps[b][:, c0:512], AF.Exp, [], [PSK(b), ("PT", m, kp)])
                        dbanks.free(b)
                        outp.append((kp, pt))
                    return (kt, c0, outp)

                def av(h, qb, sc_, first, last):
                    s_ = h % 2
                    kt, c0, outp = sc_
                    for m in range(2):
                        kp, pt = outp[m]
                        mm(ps[O_B[m]][:, c0:512], Vd[s_][:, kt, :], pt[:, c0:512], first, last,
                           [("Vd", s_, kt // 4), ("PT", m, kp)], [PSK(O_B[m])], skip=True)
                        mm(ps[L_B[m]][:, c0:512], ones, pt[:, c0:512], first, last,
                           ["cb", ("PT", m, kp)], [PSK(L_B[m])], skip=True)

                def epiA(h, qb, e):
                    e1, e2 = E1[e], E2[e]
                    cp("dve", e1[:], ps[O_B[0]][:, :], [], [PSK(O_B[0]), ("E1", e)])
                    act(RL[0][:], ps[L_B[0]][:, :], AF.Ln, [], [PSK(L_B[0]), ("RL", 0)])
                    cp("dve", e2[:], ps[O_B[1]][:, :], [], [PSK(O_B[1]), ("E2", e)])
                    act(RL[1][:], ps[L_B[1]][:, :], AF.Ln, [], [PSK(L_B[1]), ("RL", 1)])

                def epiB(h, qb, e):
                    e1, e2 = E1[e], E2[e]
                    act(RL[0][:], RL[0][:], AF.Exp, [("RL", 0)], [("RL", 0)], scale=-1.0)
                    act(RL[1][:], RL[1][:], AF.Exp, [("RL", 1)], [("RL", 1)], scale=-1.0)
                    tt("dve", e1[:], e1[:], RL[0][:], ALU.mult, [("RL", 0)], [("E1", e)])
                    tt("dve", e2[:], e2[:], RL[1][:], ALU.mult, [("RL", 1)], [("E2", e)])
                    stt(e1[:], e2[:], sc[:, SC_NLAM:SC_NLAM + 1], e1[:], ALU.mult, ALU.add, ["sc", ("E2", e)], [("E1", e)])

                def epiC(h, qb, e):
                    tt("dve", sqe[e][:], E1[e][:], E1[e][:], ALU.mult, [("E1", e)], [("sqe", e)])

                def epiD(h, qb, e):
                    s_ = h % 2
                    e1, e2 = E1[e], E2[e]
                    bs = dbanks.alloc()
                    mm(ps[bs][:, :], ones, sqe[e][:], True, True, ["cb", ("sqe", e)], [PSK(bs)])
                    rsqrt_dve_evac(bs, 512, 1.0 / 128.0, 1e-6, e2, ("E2", e))
                    dbanks.free(bs)
                    tt("dve", e1[:], e1[:], e2[:], ALU.mult, [("E2", e)], [("E1", e)])
                    stt(actT[:, h, qb * 512:(qb + 1) * 512], e1[:], sc[:, SC_SUB:SC_SUB + 1], gsd[s_][:, qb * 512:(qb + 1) * 512],
                        ALU.mult, ALU.mult, ["sc", ("E1", e), ("gsd", s_, qb)], [("act", h, qb)])

                ecount = 0
                pend = []
                pro = Prologue(0)
                pro.flush()
                for h in range(8):
                    pro = Prologue(h + 1) if h + 1 < 8 else None
                    for qb in range(NTB):
                        kts = list(range(4 * qb + 4))
                        cur = scores(h, qb, kts[0])
                        for i, kt in enumerate(kts):
                            nxt = scores(h, qb, kts[i + 1]) if i + 1 < len(kts) else None
                            av(h, qb, cur, i == 0, i == len(kts) - 1)
                            cur = nxt
                            if pend:
                                pend.pop(0)()
                            if pro is not None and not pro.done():
                                pro.tick()
                        assert not pend
                        e = ecount % 2
                        ecount += 1
                        epiA(h, qb, e)
                        pend = [(lambda h=h, qb=qb, e=e: epiB(h, qb, e)), (lambda h=h, qb=qb, e=e: epiC(h, qb, e)),
                                (lambda h=h, qb=qb, e=e: epiD(h, qb, e))]
                    if pro is not None:
                        pro.flush()
                while pend:
                    pend.pop(0)()
                for b in (4, 5, 6, 7):
                    banks.free(b)
                dump("actd", actT)
                P.barrier()
            ph_exit()
            with contextlib.ExitStack() as ph:
                ph_enter()
                branch_out(g_dp, g_m1, False, ph)
                P.barrier()

            ph_exit()
            with contextlib.ExitStack() as ph:
                ph_enter()
                xt = [T(ph, f"fx{i}", [128, D], F32) for i in range(2)]
                ot = [T(ph, f"fo{i}", [128, D], F32) for i in range(2)]
                s0 = w_slot(g_out[0])
                s1 = w_slot(g_out[1])
                for t in range(NT):
                    i = t % 2
                    P.dma("sp", xt[i][:], x_d[t * 128:(t + 1) * 128, :], writes=[("fx", i)], chan=f"x{i}")
                    for half in range(2):
                        sl = (s0, s1)[half]
                        b = banks.alloc()
                        for dc in range(NCH):
                            mm(ps[b][:, :], yT[:, dc, t * 128:(t + 1) * 128], wsl[sl][:, dc, :], dc == 0, dc == NCH - 1,
                               [("y", dc, t // 4)] + wkeys(sl, 0, 512), [PSK(b)])
                        tt("dve", ot[i][:, half * 512:(half + 1) * 512], ps[b][:, :], xt[i][:, half * 512:(half + 1) * 512], ALU.add,
                           [("fx", i)], [PSK(b), ("fo", i, half)])
                        banks.free(b)
                    P.dma("sp", out_d[t * 128:(t + 1) * 128, :], ot[i][:], reads=[("fo", i, 0), ("fo", i, 1)],
                          writes=[("fo_st", i)], chan=f"o{i}")
                P.op("sp", None, reads=[("fo_st", 0), ("fo_st", 1)])
            ph_exit()
        except _Stop:
            pass
        except Exception:
            import traceback
            traceback.print_exc()
            raise
        P.emit(st)
    return nc


def _host_constants():
    ident = np.eye(128, dtype=np.float32)
    ones = np.ones((128, 128), np.float32)
    p = np.arange(128)
    bd64 = (p[:, None] // 64 == p[None, :] // 64).astype(np.float32)
    negmask = np.where(p[None, :] < p[:, None], -30000.0, 0.0).astype(np.float32)
    cb = np.concatenate([ident, ones, bd64, negmask], axis=1)
    tok = np.arange(S)
    il = (tok % 128).astype(np.float32)
    ib = (tok // 128).astype(np.float32)
    qaug = np.zeros((8, 4, S), np.float32)
    kaug = np.zeros((8, 4, S), np.float32)
    for h in range(8):
        slope = 2.0 ** (-(h + 1))
        qaug[h, 0] = 1.0
        qaug[h, 1] = -slope * il
        qaug[h, 2] = 1.0
        qaug[h, 3] = -slope * 128.0 * ib
        kaug[h, 0] = slope * il
        kaug[h, 1] = 1.0
        kaug[h, 2] = slope * 128.0 * ib
        kaug[h, 3] = 1.0
    return cb, ident, qaug, kaug


_NC_CACHE = {}


def kernel(x, mem, norm_g, mem_norm_g, w_in, conv_dw, conv_dw_b, conv_ln_g, conv_ln_b, w_conv_proj,
           diff_qn_g, diff_kn_g, lambda_q1, lambda_k1, lambda_q2, lambda_k2, diff_subln_g, w_diff_proj,
           w_mem_kv, x_qn_g, x_kn_g, w_x_proj, w_out):
    f = lambda a: np.ascontiguousarray(np.asarray(a, dtype=np.float32))
    x = f(x); mem = f(mem)
    B = x.shape[0]
    vecs = np.concatenate([
        f(conv_dw_b)[0].reshape(8, 128), f(conv_ln_g)[0].reshape(8, 128), f(conv_ln_b)[0].reshape(8, 128),
        f(conv_dw)[0].reshape(31 * 8, 128),
        np.concatenate([f(diff_qn_g)[0], f(diff_qn_g)[0]])[None, :],
        np.concatenate([f(diff_kn_g)[0], f(diff_kn_g)[0]])[None, :],
        f(diff_subln_g)[0][None, :],
        f(x_qn_g)[0].reshape(2, 128), f(x_kn_g)[0].reshape(2, 128)], axis=0)
    assert vecs.shape == (NROWS, 128)
    lamv = np.concatenate([f(lambda_q1)[0], f(lambda_q2)[0], f(lambda_k1)[0], f(lambda_k2)[0]])[None, :]
    cb, identf, qaug, kaug = _host_constants()
    shared = {
        "w_in": f(w_in)[0], "w_conv_proj": f(w_conv_proj)[0], "w_diff_proj": f(w_diff_proj)[0],
        "w_x_proj": f(w_x_proj)[0], "w_out": f(w_out)[0], "w_mem_kv": f(w_mem_kv)[0],
        "norm_g": f(norm_g), "mem_norm_g": f(mem_norm_g), "vecs": np.ascontiguousarray(vecs),
        "lamv": np.ascontiguousarray(lamv), "cbits": cb, "identf": identf, "qaug": qaug, "kaug": kaug,
    }
    if "nc" not in _NC_CACHE:
        _NC_CACHE["nc"] = build_program()
    nc = _NC_CACHE["nc"]
    in_maps = [dict(shared, x=x[b], mem=mem[b]) for b in range(B)]
    res = run_bass_kernel_spmd(nc, in_maps, core_ids=list(range(B)))
    out = np.stack([np.asarray(r["out"]) for r in res.results], axis=0).astype(np.float32)
    if DEBUG is not None:
        kernel.dbg = [np.asarray(r["dbg"]) for r in res.results]
    return out
```
